# Optimizing a Trainium2 kernel written in Bass

```python
import jax, jax.numpy as jnp
from jax import lax
import numpy as np

D_MODEL = 1024
BATCH = 4
SEQ = 4096
DEPTH = 1
DEC_BATCH = 128
DEC_SEQ = 8
PAST_LEN = 16384
PAGE_SIZE = 128

N_HEADS = 8
N_KV_HEADS = 2
HEAD_DIM = 64
GQA_GROUP = N_HEADS // N_KV_HEADS
WINDOW = 128
ATTN_WIDTH = N_HEADS * HEAD_DIM
KV_WIDTH = N_KV_HEADS * HEAD_DIM
SCALE = HEAD_DIM ** -0.5
HG_HEADS = 4
HG_DK = 128
HG_DV = 128
HG_KEY_WIDTH = HG_HEADS * HG_DK
HG_WIDTH = HG_HEADS * HG_DV
HG_CHUNK = 64
MIX_WIDTH = ATTN_WIDTH + HG_WIDTH
IN_SPLITS = [ATTN_WIDTH, KV_WIDTH, KV_WIDTH, HG_KEY_WIDTH, HG_KEY_WIDTH, HG_WIDTH, HG_WIDTH]
IN_WIDTH = sum(IN_SPLITS)
D_FF = 2816
EPS = 1e-6

kernel_name = "hymba_swa_sink_hgrn2_macaron_step"


def rms_norm(x, g):
    xf = x.astype(jnp.float32)
    y = xf * lax.rsqrt(jnp.mean(xf * xf, axis=-1, keepdims=True) + EPS)
    return (y * g.astype(jnp.float32)).astype(x.dtype)


def swiglu(x, w_gu, w_down):
    g, u = jnp.split(x @ w_gu, 2, axis=-1)
    return (jax.nn.silu(g) * u) @ w_down


def sink_probs(s, mask, sink):
    s = jnp.where(mask, s, -jnp.inf)
    sk = sink.astype(jnp.float32)[..., None, None]
    m = jnp.maximum(jnp.max(s, axis=-1, keepdims=True), sk)
    p = jnp.exp(s - m)
    den = jnp.sum(p, axis=-1, keepdims=True) + jnp.exp(sk - m)
    return p / den


def banded_window_attention(q, k, v, sinks):
    B, L = q.shape[:2]
    W = WINDOW
    nb = L // W
    qb = q.reshape(B, nb, W, N_KV_HEADS, GQA_GROUP, HEAD_DIM).astype(jnp.float32)

    def with_prev(a):
        a = a.reshape(B, nb, W, N_KV_HEADS, HEAD_DIM).astype(jnp.float32)
        prev = jnp.pad(a, ((0, 0), (1, 0), (0, 0), (0, 0), (0, 0)))[:, :-1]
        return jnp.concatenate([prev, a], axis=2)

    kk, vv = with_prev(k), with_prev(v)
    s = jnp.einsum('bnqhgd,bnkhd->bnhgqk', qb, kk) * SCALE
    qi = jnp.arange(W)[:, None]
    kj = jnp.arange(2 * W)[None, :]
    dist = qi + W - kj
    band = (dist >= 0) & (dist <= WINDOW)
    valid = (jnp.arange(nb)[:, None, None] > 0) | (kj >= W)[None]
    mask = band[None] & valid
    p = sink_probs(s, mask[:, None, None], sinks.reshape(N_KV_HEADS, GQA_GROUP))
    out = jnp.einsum('bnhgqk,bnkhd->bnqhgd', p, vv)
    keep = min(WINDOW, L)
    return out.reshape(B, L, ATTN_WIDTH).astype(q.dtype), k[:, L - keep:], v[:, L - keep:]


def cached_window_attention(q, k, v, sinks, k_buf, v_buf):
    Bd, Ld = q.shape[:2]
    Wb = k_buf.shape[1]
    kk = jnp.concatenate([k_buf.astype(k.dtype), k], axis=1)
    vv = jnp.concatenate([v_buf.astype(v.dtype), v], axis=1)
    s = jnp.einsum('bqhgd,bkhd->bhgqk', q.astype(jnp.float32), kk.astype(jnp.float32)) * SCALE
    dist = (Wb + jnp.arange(Ld))[:, None] - jnp.arange(Wb + Ld)[None, :]
    mask = (dist >= 0) & (dist <= WINDOW)
    p = sink_probs(s, mask, sinks.reshape(N_KV_HEADS, GQA_GROUP))
    out = jnp.einsum('bhgqk,bkhd->bqhgd', p, vv.astype(jnp.float32))
    return out.reshape(Bd, Ld, ATTN_WIDTH).astype(q.dtype), kk[:, Ld:], vv[:, Ld:]


def hgrn2_chunked(q, k, logf, v, s0):
    B, L = q.shape[:2]
    C = min(HG_CHUNK, L)
    n = -(-L // C)
    pad = n * C - L

    def prep(a):
        a = jnp.pad(a, ((0, 0), (0, pad), (0, 0), (0, 0)))
        return a.reshape(B, n, C, a.shape[2], a.shape[3]).transpose(1, 0, 3, 2, 4)

    qc, kc, gc, vc = prep(q), prep(k), prep(logf), prep(v)
    causal = jnp.tril(jnp.ones((C, C), dtype=bool))[:, :, None]

    def step(S, inp):
        qb, kb, gb, vb = inp
        G = jnp.cumsum(gb, axis=2)
        diff = G[:, :, :, None, :] - G[:, :, None, :, :]
        decay = jnp.exp(jnp.where(causal, diff, -jnp.inf))
        A = jnp.einsum('bhtk,bhsk,bhtsk->bhts', qb, kb, decay)
        o = jnp.einsum('bhts,bhsv->bhtv', A, vb) + jnp.einsum('bhtk,bhkv->bhtv', qb * jnp.exp(G), S)
        G_last = G[:, :, -1:, :]
        S = jnp.exp(G_last[:, :, 0, :])[..., None] * S + jnp.einsum(
            'bhsk,bhsv->bhkv', kb * jnp.exp(G_last - G), vb)
        return S, o

    S, o = lax.scan(step, s0, (qc, kc, gc, vc))
    o = o.transpose(1, 0, 3, 2, 4).reshape(B, n * C, q.shape[2], v.shape[3])[:, :L]
    return o, S


def setup_inputs(seed: int = 0) -> dict:
    key = jax.random.key(seed)
    ks = jax.random.split(key, 24)
    f32 = jnp.float32
    win_buf = min(WINDOW, PAST_LEN)

    def nrm(k, shape, scale):
        return jax.random.normal(k, shape, f32) * scale

    def gain(k, shape):
        return 1.0 + 0.02 * jax.random.normal(k, shape, f32)

    return {
        "x_prompt": nrm(ks[0], (BATCH, SEQ, D_MODEL), 1.0),
        "x_sample": nrm(ks[1], (DEC_BATCH, DEC_SEQ, D_MODEL), 1.0),
        "cache_k_win": nrm(ks[2], (DEPTH, DEC_BATCH, win_buf, N_KV_HEADS, HEAD_DIM), 1.0),
        "cache_v_win": nrm(ks[3], (DEPTH, DEC_BATCH, win_buf, N_KV_HEADS, HEAD_DIM), 1.0),
        "state_hgrn": nrm(ks[4], (DEPTH, DEC_BATCH, HG_HEADS, HG_DK, HG_DV), 0.5),
        "w_in": nrm(ks[5], (DEPTH, D_MODEL, IN_WIDTH), D_MODEL ** -0.5),
        "b_in": nrm(ks[6], (DEPTH, IN_WIDTH), 0.02),
        "attn_sinks": nrm(ks[7], (DEPTH, N_HEADS), 0.5),
        "attn_out_norm": gain(ks[8], (DEPTH, ATTN_WIDTH)),
        "hg_lb_logits": nrm(ks[9], (DEPTH + 1, HG_KEY_WIDTH), 0.1),
        "hg_out_norm": gain(ks[10], (DEPTH, HG_DV)),
        "w_out": nrm(ks[11], (DEPTH, MIX_WIDTH, D_MODEL), MIX_WIDTH ** -0.5),
        "ffn1_w_gu": nrm(ks[12], (DEPTH, D_MODEL, 2 * D_FF), D_MODEL ** -0.5),
        "ffn1_w_down": nrm(ks[13], (DEPTH, D_FF, D_MODEL), D_FF ** -0.5),
        "ffn2_w_gu": nrm(ks[14], (DEPTH, D_MODEL, 2 * D_FF), D_MODEL ** -0.5),
        "ffn2_w_down": nrm(ks[15], (DEPTH, D_FF, D_MODEL), D_FF ** -0.5),
        "norm_ffn1_pre": gain(ks[16], (DEPTH, D_MODEL)),
        "norm_ffn1_post": gain(ks[17], (DEPTH, D_MODEL)),
        "norm_mix_pre": gain(ks[18], (DEPTH, D_MODEL)),
        "norm_mix_post": gain(ks[19], (DEPTH, D_MODEL)),
        "norm_ffn2_pre": gain(ks[20], (DEPTH, D_MODEL)),
        "norm_ffn2_post": gain(ks[21], (DEPTH, D_MODEL)),
    }


def reference(x_prompt, x_sample, cache_k_win, cache_v_win, state_hgrn,
              w_in, b_in, attn_sinks, attn_out_norm, hg_lb_logits, hg_out_norm, w_out,
              ffn1_w_gu, ffn1_w_down, ffn2_w_gu, ffn2_w_down,
              norm_ffn1_pre, norm_ffn1_post, norm_mix_pre, norm_mix_post,
              norm_ffn2_pre, norm_ffn2_post):
    f32 = jnp.float32
    lb_all = jnp.cumsum(jax.nn.softmax(hg_lb_logits.astype(f32), axis=0), axis=0)

    def run_layer(x, l, attend, s0):
        B, L = x.shape[:2]
        x = x + 0.5 * rms_norm(swiglu(rms_norm(x, norm_ffn1_pre[l]), ffn1_w_gu[l], ffn1_w_down[l]),
                               norm_ffn1_post[l])
        h = rms_norm(x, norm_mix_pre[l])
        z = h @ w_in[l] + b_in[l]
        q, k, v, hq, hf, hi, hg = jnp.split(z, [int(c) for c in np.cumsum(IN_SPLITS)[:-1]], axis=-1)
        q = q.reshape(B, L, N_KV_HEADS, GQA_GROUP, HEAD_DIM)
        k = k.reshape(B, L, N_KV_HEADS, HEAD_DIM)
        v = v.reshape(B, L, N_KV_HEADS, HEAD_DIM)
        a, k_new, v_new = attend(q, k, v, attn_sinks[l])
        a = rms_norm(a, attn_out_norm[l])
        lb = lb_all[l].reshape(HG_HEADS, HG_DK)
        hq = jax.nn.silu(hq.reshape(B, L, HG_HEADS, HG_DK).astype(f32))
        f = lb + (1.0 - lb) * jax.nn.sigmoid(hf.reshape(B, L, HG_HEADS, HG_DK).astype(f32))
        o, S = hgrn2_chunked(hq, 1.0 - f, jnp.log(f),
                             hi.reshape(B, L, HG_HEADS, HG_DV).astype(f32), s0)
        o = rms_norm(o, hg_out_norm[l]) * jax.nn.silu(hg.reshape(B, L, HG_HEADS, HG_DV).astype(f32))
        o = o.reshape(B, L, HG_WIDTH).astype(x.dtype)
        mix = jnp.concatenate([a, o], axis=-1) @ w_out[l]
        x = x + rms_norm(mix, norm_mix_post[l])
        x = x + 0.5 * rms_norm(swiglu(rms_norm(x, norm_ffn2_pre[l]), ffn2_w_gu[l], ffn2_w_down[l]),
                               norm_ffn2_post[l])
        return x, k_new, v_new, S

    yp, ys = x_prompt, x_sample
    kp, vp, sp, kd, vd, sd = [], [], [], [], [], []
    for l in range(DEPTH):
        s0p = jnp.zeros((x_prompt.shape[0], HG_HEADS, HG_DK, HG_DV), f32)
        yp, k1, v1, s1 = run_layer(yp, l, banded_window_attention, s0p)
        kp.append(k1); vp.append(v1); sp.append(s1)

        def attend_cached(q, k, v, sinks, l=l):
            return cached_window_attention(q, k, v, sinks, cache_k_win[l], cache_v_win[l])

        ys, k2, v2, s2 = run_layer(ys, l, attend_cached, state_hgrn[l].astype(f32))
        kd.append(k2); vd.append(v2); sd.append(s2)

    return (yp, ys, jnp.stack(kp), jnp.stack(vp), jnp.stack(sp), jnp.stack(kd), jnp.stack(vd), jnp.stack(sd))
```

```python
import numpy as np
from contextlib import ExitStack
import concourse.bass as bass
import concourse.mybir as mybir
from concourse.bass_utils import run_bass_kernel_spmd

F32 = mybir.dt.float32
BF16 = mybir.dt.bfloat16
AF = mybir.ActivationFunctionType
ALU = mybir.AluOpType

D = 1024
DFF = 2816
NF = DFF // 128
INW = 2816
NG = 18
EPS = 1e-6
NEG = -30000.0
SCALE = 64 ** -0.5
ENGS = ("pe", "act", "dve", "pool", "sp")
import os
PN_LEVEL = int(os.environ.get("PN_LEVEL", "4"))
MIXLVL = int(os.environ.get("MIXLVL", "5"))
SUB = int(os.environ.get("SUB", "9"))
USE_CC = bool(os.environ.get("USE_CC"))


class V:
    __slots__ = ("ap", "name", "lo", "hi")

    def __init__(self, ap, name, lo, hi):
        self.ap, self.name, self.lo, self.hi = ap, name, lo, hi

    def with_ap(self, ap):
        return V(ap, self.name, self.lo, self.hi)


class TT:
    def __init__(self, handle, name, shape, esz=1, base=0):
        self.h, self.name, self.shape = handle, name, list(shape)
        self.base = base
        st = [1] * len(shape)
        for i in range(len(shape) - 2, 0, -1):
            st[i] = st[i + 1] * shape[i + 1]
        self.st = st
        self.esz = esz

    def __getitem__(self, idx):
        if not isinstance(idx, tuple):
            idx = (idx,)
        idx = idx + (slice(None),) * (len(self.shape) - len(idx))
        lo, hi = 0, 0
        for i in range(1, len(self.shape)):
            s = idx[i]
            if isinstance(s, int):
                a, b = s, s + 1
            else:
                a = 0 if s.start is None else s.start
                b = self.shape[i] if s.stop is None else s.stop
            lo += a * self.st[i]
            hi += (b - 1) * self.st[i]
        if self.name.startswith("BK"):
            return V(self.h[idx], self.name, 0, 512)
        return V(self.h[idx], self.name, self.base + lo * self.esz, self.base + (hi + 1) * self.esz)


class Prog:
    def __init__(self):
        self.q = {e: [] for e in ENGS}
        self.cnt = {e: 0 for e in ENGS}
        self.acc = {}
        self.dma_cnt = {}
        self.dma_hist = {}

    def _deps(self, reads, writes, tok):
        deps = set()
        for v in reads:
            recs = self.acc.setdefault(v.name, [])
            for (lo, hi, kind, t) in recs:
                if kind == "w" and lo < v.hi and v.lo < hi:
                    deps.add(t)
        for v in writes:
            recs = self.acc.setdefault(v.name, [])
            for (lo, hi, kind, t) in recs:
                if lo < v.hi and v.lo < hi:
                    deps.add(t)
        for v in reads:
            recs = self.acc[v.name]
            recs[:] = [r for r in recs if not (r[2] == "r" and r[0] == v.lo and r[1] == v.hi
                                               and r[3][0] == tok[0])]
            recs.append((v.lo, v.hi, "r", tok))
        for v in writes:
            recs = self.acc[v.name]
            recs[:] = [r for r in recs if not (v.lo <= r[0] and r[1] <= v.hi)]
            recs.append((v.lo, v.hi, "w", tok))
        deps.discard(tok)
        return deps

    def op(self, eng, fn, reads=(), writes=(), sig=True):
        if sig:
            self.cnt[eng] += 1
            tok = (eng, self.cnt[eng])
        else:
            tok = (eng, self.cnt[eng] + 1)
        deps = self._deps(reads, writes, tok)
        self.q[eng].append((deps, fn, eng if sig else None, 1))
        return tok

    def dma(self, eng, out, in_, slot, extra_reads=(), extra_writes=()):
        key = "dma:" + slot
        self.dma_cnt[key] = self.dma_cnt.get(key, 0) + 16
        tok = (key, self.dma_cnt[key])
        deps = self._deps([in_] + list(extra_reads), [out] + list(extra_writes), tok)
        if slot not in ("kvout", "yout", "stout") and self.dma_cnt[key] > 16:
            deps.add((key, self.dma_cnt[key] - 16))
        hist = self.dma_hist.setdefault(eng, [])
        if len(hist) >= 6:
            deps.add(hist[-6])
        if slot not in ("kvout", "yout"):
            hist.append(tok)
        o, i = out.ap, in_.ap
        self.q[eng].append((deps, lambda e: e.dma_start(out=o, in_=i), key, 16))
        return tok

    def check(self):
        val = {}
        pos = {e: 0 for e in ENGS}
        own = {e: 0 for e in ENGS}
        progress = True
        while progress:
            progress = False
            for eng in ENGS:
                while pos[eng] < len(self.q[eng]):
                    deps, fn, sigkey, inc = self.q[eng][pos[eng]]
                    ok = True
                    for (k, v) in deps:
                        if k == eng and (eng == "pe" or v > own[eng]):
                            continue
                        if val.get(k, 0) < v:
                            ok = False
                            break
                    if not ok:
                        break
                    if sigkey is not None:
                        val[sigkey] = val.get(sigkey, 0) + inc
                        if sigkey == eng:
                            own[eng] += 1
                    pos[eng] += 1
                    progress = True
        stuck = {e: (pos[e], len(self.q[e])) for e in ENGS if pos[e] < len(self.q[e])}
        if stuck:
            msg = []
            for e in stuck:
                deps = self.q[e][pos[e]][0]
                msg.append("%s@%d waits %s" % (e, pos[e], [(k, v, val.get(k, 0)) for (k, v) in deps if val.get(k, 0) < v]))
            raise RuntimeError("DEADLOCK: " + "; ".join(msg))
        for (k, v) in self.final:
            assert val.get(k, 0) == v, (k, v, val.get(k, 0))

    def replay(self, eng, e, sems):
        seen = {}
        own = 0
        for (deps, fn, sigkey, inc) in self.q[eng]:
            for (k, val) in sorted(deps):
                if k == eng and (eng == "pe" or val > own):
                    continue
                if seen.get(k, 0) >= val:
                    continue
                e.wait_ge(sems[k], val)
                seen[k] = val
            ins = fn(e)
            if sigkey is not None:
                ins.then_inc(sems[sigkey], inc)
                if sigkey == eng:
                    own += 1
        if eng == "sp":
            for (k, val) in sorted(self.final):
                if seen.get(k, 0) < val:
                    e.wait_ge(sems[k], val)


def build_program(stage=99):
    nc = bass.Bass("TRN2", target_bir_lowering=False)
    P = Prog()
    es = ExitStack()

    def din(name, shape, dt=F32):
        return TT(nc.dram_tensor(name, list(shape), dt, kind="ExternalInput").ap(), name, [1, 1])

    def dout(name, shape, dt=F32):
        return TT(nc.dram_tensor(name, list(shape), dt, kind="ExternalOutput").ap(), name, [1, 1])

    def dram_v(t, ap=None, sub=""):
        return V(t.h if ap is None else ap, t.name + sub, 0, 1)

    def sb(name, shape, dt=F32):
        h = es.enter_context(nc.sbuf_tensor(name, list(shape), dt))
        return TT(h, name, shape)

    def ps(name, shape, dt=F32):
        h = es.enter_context(nc.psum_tensor(name, list(shape), dt))
        return TT(h, name, shape)

    xin = din("xin", [NG * 128, D])
    w_gu = [din("w_gu1", [D, 2 * DFF]), din("w_gu2", [D, 2 * DFF])]
    w_dn = [din("w_dn1", [DFF, D]), din("w_dn2", [DFF, D])]
    w_in = din("w_in", [D, INW])
    w_out = din("w_out", [D, D])
    gcols = din("gcols", [128, 3, 8])
    gpost = din("gpost", [3, 128, D])
    ident_in = din("ident", [128, 128])
    y = dout("y", [17 * 128, D])
    bkv_in = din("bkv", [128, 256])
    cache_k = din("cache_k", [16, 128, 128])
    cache_v = din("cache_v", [16, 128, 128])
    kwin_p = dout("kwin_p", [128, 128])
    vwin_p = dout("vwin_p", [128, 128])
    kwin_s = dout("kwin_s", [16, 128, 128])
    vwin_s = dout("vwin_s", [16, 128, 128])
    binF_in = din("binF", [128, 22])
    bhi_in = din("bhi", [128, 512])
    bhg_in = din("bhg", [128, 512])
    lbl_in = din("lbl", [128, 2, 4])
    rmask_in = din("rmask", [128, 512])
    hmask_in = din("hmask", [128, 128])
    sel_in = din("sel", [128, 8])
    xprev = din("xprev", [2048, D])
    pflag_in = din("pflag", [128, 1])
    state_p = dout("state_p", [128, 4, 128])
    maskP_in = din("maskP", [128, 2, 128])
    maskH_in = din("maskH", [128, 128])
    zm_in = din("zmask", [128, 248])
    nm_in = din("newmask", [128, 128])
    hmS_in = din("hmaskS", [128, 128])
    rmS_in = din("rmaskS", [128, 128])
    seqsel_in = din("seqsel", [128, 16])
    st_in = din("state_s_in", [16, 4, 128, 128])
    state_s = dout("state_s", [16, 4, 128, 128])
    sink_in = din("sinks", [128, 8])
    gattn_in = din("gattn", [128, 512])
    ghg_in = din("ghg4", [128, 512])
    bk2_in = din("bk2", [128, 2])
    cpar_in = din("cpar", [128, 2, 128])
    sc_gu = [TT(nc.dram_tensor("sc_gu%d" % i, [NF, 128, 2048], BF16).ap(), "sc_gu%d" % i, [1, 1]) for i in range(2)]
    sc_dn = [TT(nc.dram_tensor("sc_dn%d" % i, [NF, 128, 1024], BF16).ap(), "sc_dn%d" % i, [1, 1]) for i in range(2)]
    sc_in = TT(nc.dram_tensor("sc_in", [11, 128, 2048], BF16).ap(), "sc_in", [1, 1])
    sc_out = TT(nc.dram_tensor("sc_out", [8, 128, 1024], BF16).ap(), "sc_out", [1, 1])
    cached = set()
    cc_in = TT(nc.dram_tensor("cc_in", [128, 512], F32).ap(), "cc_in", [1, 1])
    cc_out = TT(nc.dram_tensor("cc_out", [8 * 128, 512], F32).ap(), "cc_out", [1, 1])

    X = sb("X", [128, NG, D])
    xn_tok = sb("xn_tok", [128, 2, D], BF16)
    xnT = sb("xnT", [128, 8, 512], BF16)
    SCR = sb("SCR", [128, 22 * 512], BF16)
    WGU = sb("WGU", [128, 3, 8, 256], BF16)
    WDR = sb("WDR", [128, 4, 1024], BF16)
    SCR2 = sb("SCR2", [128, 6144], BF16)

    def carve(base, off_b, shape, dt):
        esz = 2 if dt == F32 else 1
        n = 1
        for d_ in shape[1:]:
            n *= d_
        ap = base.h[:, off_b // 2: off_b // 2 + n * esz]
        if dt == F32:
            ap = ap.bitcast(F32)
        if len(shape) == 3:
            ap = ap.rearrange("p (a b) -> p a b", a=shape[1])
        elif len(shape) == 4:
            ap = ap.rearrange("p (a b c) -> p a b c", a=shape[1], b=shape[2])
        return TT(ap, base.name, shape, esz=esz, base=off_b // 2)

    SG = carve(SCR2, 0, [128, 2, 512], F32)
    YB = carve(SCR2, 4096, [128, 2, D], F32)
    GP = sb("GP", [128, D])
    gc = sb("gc", [128, 3, 8])
    identf = sb("identf", [128, 128])
    identb = sb("identb", [128, 128], BF16)
    junk = sb("junk", [128, D], BF16)
    st = sb("st", [128, 64])

    KV = sb("KV", [128, 2, 256])
    HT = carve(SCR2, 0, [128, 6, 512], F32)
    binF = sb("binF_sb", [128, 22])
    bhi = sb("bhi_sb", [128, 512])
    bhg = sb("bhg_sb", [128, 512])
    lbl = sb("lbl_sb", [128, 2, 4])
    lbt = sb("lbt", [128, 3, 4])
    rmask = sb("rmask_sb", [128, 512])
    hmask = sb("hmask_sb", [128, 128])
    sel = sb("sel_sb", [128, 8])
    pflag = sb("pflag_sb", [128, 1])
    S = sb("S", [128, 4, 128])
    Sbf = sb("Sbf", [128, 3, 4, 128], BF16)
    DCH = sb("DCH", [128, 4, 16])
    hi_tok = sb("hi_tok", [128, 4, 512], BF16)
    qtT0 = carve(SCR, 4096, [128, 4, 512], BF16)
    ktT = carve(SCR, 8192, [128, 4, 512], BF16)
    khT = carve(SCR, 12288, [128, 4, 512], BF16)
    qtT1 = carve(SCR, 16384, [128, 4, 512], BF16)
    khat_tok = sb("khat_tok", [128, 4, 4, 128], BF16)
    WK2 = sb("WK2", [128, 8, 2, 128], BF16)
    cpar = sb("cpar_sb", [128, 2, 128])
    esink_raw = sb("esink_raw", [128, 8])
    G8 = carve(SCR, 0, [128, 8, 512], F32)
    qT = carve(SCR, 0, [128, 4, 512], BF16)
    kT2 = sb("kT2", [128, 2, 640], BF16)
    V1 = sb("V1", [128, 5, 2, 66], BF16)
    PT = sb("PT", [128, 2, 512], BF16)
    AT = sb("AT", [128, 4, 128], BF16)
    OA = sb("OA", [128, 4, 66])
    a_tok = sb("a_tok", [128, 512])
    o_tok = sb("o_tok", [128, 512])
    gate = sb("gate", [128, 512])
    mix_tok = sb("mix_tok", [128, 1024], BF16)
    maskP = sb("maskP_sb", [128, 2, 128], BF16)
    maskH = sb("maskH_sb", [128, 128], BF16)
    Zm = sb("Zm", [128, 248], BF16)
    newmask = sb("newmask_sb", [128, 128], BF16)
    hmaskS = sb("hmaskS_sb", [128, 128])
    rmaskS = sb("rmaskS_sb", [128, 128])
    seqsel = sb("seqsel_sb", [128, 16])
    Kd = sb("Kd", [128, 3, 128], BF16)
    Vb = sb("Vb", [128, 3, 66], BF16)
    kcT = sb("kcT", [128, 2, 128], BF16)
    vmk = sb("vmk", [128, 4, 128], BF16)
    WDRf = TT(WDR.h[:].rearrange("p a b -> p (a b)"), "WDR", [128, 4096])
    Sb32 = carve(WDRf, 0, [128, 16, 128], F32)
    Sbbf = carve(SCR, 16384, [128, 16, 128], BF16)
    qexp = carve(SCR2, 4096, [128, 17 * 128], BF16)
    esink = sb("esink", [128, 8])
    gattn = sb("gattn_sb", [128, 512])
    ghg4 = sb("ghg4_sb", [128, 512])
    bk2 = sb("bk2_sb", [128, 2])
    bkv = sb("bkv_sb", [128, 256])
    BK = [ps("BK%d" % i, [128, 512]) for i in range(8)]
    bk7b = BK[7].h[:].bitcast(BF16)

    def TPB(lo, hi):
        return V(bk7b[:, lo:hi], "BK7", 0, 512)

    stc = [0]

    def stcol(n=1):
        c = stc[0]
        if c + n > 64:
            c = 0
        stc[0] = c + n
        return st[:, c:c + n]

    epst = sb("epst", [128, 2])
    epsb = {1.0: epst[:, 0:1], 0.5: epst[:, 1:2]}
    P.op("dve", lambda e: e.memset(epst[:, 0:1].ap, EPS), [], [epst[:, 0:1]])
    P.op("dve", lambda e: e.memset(epst[:, 1:2].ap, EPS / 0.25), [], [epst[:, 1:2]])
    P.dma("sp", gc[:], dram_v(gcols), "c0")
    P.dma("sp", identf[:], dram_v(ident_in), "c1")
    P.op("dve", lambda e: e.tensor_copy(out=identb[:].ap, in_=identf[:].ap), [identf[:]], [identb[:]])
    for g in (range(NG) if USE_CC else (0, 1)):
        P.dma("sp", X[:, g, :], dram_v(xin, xin.h[g * 128:(g + 1) * 128, :]), "x%d" % g)

    def rstd_of(ss, n, hs, w=1):
        r1 = stcol(w)
        P.op("act", lambda e, o=r1.ap, i=ss.ap: e.activation(out=o, in_=i, func=AF.Sqrt, scale=1.0 / (n * hs * hs), bias=epsb[hs].ap),
             [ss, epsb[hs]], [r1])
        r2 = stcol(w)
        P.op("dve", lambda e, o=r2.ap, i=r1.ap: e.reciprocal(out=o, in_=i), [r1], [r2])
        return r2

    def rms_to_T(groups, gidx, dstT):
        for j, g in enumerate(groups):
            ss = stcol()
            xg = X[:, g, :]
            P.op("act", lambda e, o=junk[:].ap, i=xg.ap, a=ss.ap: e.activation(out=o, in_=i, func=AF.Square, accum_out=a),
                 [xg], [junk[:], ss])
            r2 = rstd_of(ss, D, 1.0)
            xs = xn_tok[:, j % 2, :]
            P.op("dve", lambda e, o=xs.ap, i=xg.ap, s=r2.ap: e.tensor_scalar(out=o, in0=i, scalar1=s, scalar2=None, op0=ALU.mult),
                 [xg, r2], [xs])
            for c in range(8):
                src = xn_tok[:, j % 2, c * 128:(c + 1) * 128]
                dst = TPB(c * 128, (c + 1) * 128)
                P.op("pe", lambda e, o=dst.ap, i=src.ap: e.transpose(out=o, in_=i, identity=identb[:].ap),
                     [src, identb[:]], [dst], sig=(c == 7))
            tp = TPB(0, 1024)
            tp3 = tp.with_ap(bk7b[:, 0:1024].rearrange("p (c t) -> p c t", c=8))
            d3 = dstT[:, :, j * 128:(j + 1) * 128]
            gb = gc[:, gidx, :]
            gb3 = gb.ap.unsqueeze(2).to_broadcast([128, 8, 128])
            P.op("dve", lambda e, o=d3.ap, i=tp3.ap, g_=gb3: e.tensor_tensor(out=o, in0=i, in1=g_, op=ALU.mult),
                 [tp, gb], [d3])

    def post_norm_add(g, banks, gp_idx, half_scale, slot):
        ss2 = stcol(2)
        yb = YB[:, slot, :]
        for h in range(2):
            bk = banks[h][:]
            ssh = V(ss2.ap[:, h:h + 1], ss2.name, ss2.lo + h, ss2.lo + h + 1)
            P.op("act", lambda e, o=junk[:, 0:512].ap, i=bk.ap, a=ssh.ap: e.activation(out=o, in_=i, func=AF.Square, accum_out=a),
                 [bk], [junk[:, 0:512], ssh])
            if PN_LEVEL < 2:
                continue
            ybh = YB[:, slot, h * 512:(h + 1) * 512]
            gph = GP[:, h * 512:(h + 1) * 512]
            P.op("act", lambda e, o=ybh.ap, i=bk.ap: e.activation(out=o, in_=i, func=AF.Copy), [bk], [ybh])
            P.op("dve", lambda e, o=ybh.ap, g_=gph.ap: e.tensor_tensor(out=o, in0=o, in1=g_, op=ALU.mult),
                 [ybh, gph], [ybh])
        if PN_LEVEL < 3:
            return
        s1 = stcol()
        P.op("dve", lambda e, o=s1.ap, a=ss2.ap: e.tensor_scalar(out=o, in0=a[:, 0:1], scalar1=a[:, 1:2], scalar2=None, op0=ALU.add), [ss2], [s1])
        r2 = rstd_of(s1, D, half_scale)
        if PN_LEVEL < 4:
            return
        xg = X[:, g, :]
        P.op("dve", lambda e, o=xg.ap, i=yb.ap, s=r2.ap: e.scalar_tensor_tensor(out=o, in0=i, scalar=s, in1=o, op0=ALU.mult, op1=ALU.add),
             [yb, r2, xg], [xg])

    wgu_i = [0]
    wdr_i = [0]
    ring_i = [0]
    bg_queue = []
    bg_n = [0]

    def bg_drain(n=1):
        for _ in range(n):
            if not bg_queue:
                return
            dst_v, src_v = bg_queue.pop(0)
            P.dma("pool", dst_v, src_v, "bg%d" % (bg_n[0] % 8))
            bg_n[0] += 1

    prenormed = [None]

    def prenorm(groups, gidx):
        key = (tuple(groups), gidx)
        if prenormed[0] != key:
            rms_to_T(groups, gidx, xnT)
        prenormed[0] = None

    def prenorm_early(groups, gidx):
        rms_to_T(groups, gidx, xnT)
        prenormed[0] = (tuple(groups), gidx)

    def ffn(groups, which, parts=3, hook=None):
        n = len(groups)
        N = 128 * n
        prenorm(groups, 0 if which == 0 else 2)
        P.dma("sp", GP[:], dram_v(gpost, gpost.h[0 if which == 0 else 2]), "gp")
        wg = w_gu[which].h.rearrange("(c p) n -> p c n", p=128)
        for f in range(NF):
            s = wgu_i[0] % 3
            wgu_i[0] += 1
            slot2d = V(WGU.h[:, s, :, :].rearrange("p c n -> p (c n)"), "WGU", WGU[:, s, :, :].lo, WGU[:, s, :, :].hi)
            ck = ("gu", which, f)
            if ck in cached:
                P.dma("sp", slot2d, dram_v(sc_gu[which], sc_gu[which].h[f], "_%d" % f), "hgu%d" % s)
            else:
                for hh in range(2):
                    P.dma("pool", WGU[:, s, :, hh * 128:(hh + 1) * 128],
                          dram_v(w_gu[which], wg[:, :, hh * DFF + f * 128: hh * DFF + (f + 1) * 128]), "wgu%d_%d" % (s, hh))
                P.dma("sp", dram_v(sc_gu[which], sc_gu[which].h[f], "_%d" % f), slot2d, "wbg%d" % s)
                cached.add(ck)
            bks = []
            for hh in range(2):
                b = BK[ring_i[0] % 3]
                ring_i[0] += 1
                bks.append(b)
                for c in range(8):
                    lw = WGU[:, s, c, hh * 128:(hh + 1) * 128]
                    rx = xnT[:, c, 0:N]
                    P.op("pe", lambda e, o=b[:, 0:N].ap, l=lw.ap, r=rx.ap, c=c: e.matmul(o, lhsT=l, rhs=r, start=(c == 0), stop=(c == 7)),
                         [lw, rx], [b[:, 0:N]], sig=(c == 7))
            bg_drain(1)
            sg = SG[:, f % 2, 0:N]
            P.op("act", lambda e, o=sg.ap, i=bks[0][:, 0:N].ap: e.activation(out=o, in_=i, func=AF.Silu),
                 [bks[0][:, 0:N]], [sg])
            at = V(SCR.h[:, f * 512: f * 512 + N], "SCR", f * 512, f * 512 + N)
            P.op("dve", lambda e, o=at.ap, a=sg.ap, b_=bks[1][:, 0:N].ap: e.tensor_tensor(out=o, in0=a, in1=b_, op=ALU.mult),
                 [sg, bks[1][:, 0:N]], [at])
        if parts < 2:
            return
        wd = w_dn[which].h.rearrange("(f p) n -> p f n", p=128)
        for f in range(NF):
            s = wdr_i[0] % 4
            wdr_i[0] += 1
            ck = ("dn", which, f)
            if ck in cached:
                P.dma("sp", WDR[:, s, :], dram_v(sc_dn[which], sc_dn[which].h[f], "_%d" % f), "hdr%d" % s)
            else:
                P.dma("pool", WDR[:, s, :], dram_v(w_dn[which], wd[:, f, :]), "wdr%d" % s)
                P.dma("sp", dram_v(sc_dn[which], sc_dn[which].h[f], "_%d" % f), WDR[:, s, :], "wbd%d" % s)
                cached.add(ck)
            for j in range(n):
                for h in range(2):
                    b = BK[j * 2 + h]
                    la = V(SCR.h[:, f * 512 + j * 128: f * 512 + (j + 1) * 128], "SCR", f * 512 + j * 128, f * 512 + (j + 1) * 128)
                    rw = WDR[:, s, h * 512:(h + 1) * 512]
                    P.op("pe", lambda e, o=b[:].ap, l=la.ap, r=rw.ap, f=f: e.matmul(o, lhsT=l, rhs=r, start=(f == 0), stop=(f == NF - 1)),
                         [la, rw], [b[:]], sig=(f == NF - 1 or (j == n - 1 and h == 1)))
        if parts < 3:
            return
        order = list(range(n))
        if hook is not None and n == 4:
            j = 3
            post_norm_add(groups[j], [BK[j * 2], BK[j * 2 + 1]], 0 if which == 0 else 2, 0.5, j % 2)
            order = [0, 1, 2]
        if hook is not None:
            hook()
        for j in order:
            post_norm_add(groups[j], [BK[j * 2], BK[j * 2 + 1]], 0 if which == 0 else 2, 0.5, j % 2)

    kv_i = [0]

    def kv_proj(groups):
        rms_to_T(groups, 1, xnT)
        s_ = wgu_i[0] % 3
        wgu_i[0] += 1
        wi = w_in.h.rearrange("(c p) n -> p c n", p=128)
        for hh in range(2):
            P.dma("pool", WGU[:, s_, :, hh * 128:(hh + 1) * 128],
                  dram_v(w_in, wi[:, :, 512 + hh * 128: 512 + (hh + 1) * 128]), "wgu%d_%d" % (s_, hh))
        for j, g in enumerate(groups):
            if g not in (1, NG - 1):
                continue
            b = BK[ring_i[0] % 3]
            ring_i[0] += 1
            for c in range(8):
                la = xnT[:, c, j * 128:(j + 1) * 128]
                rw = WGU[:, s_, c, :]
                P.op("pe", lambda e, o=b[:, 0:256].ap, l=la.ap, r=rw.ap, c=c: e.matmul(o, lhsT=l, rhs=r, start=(c == 0), stop=(c == 7)),
                     [la, rw], [b[:, 0:256]], sig=(c == 7))
            sl = kv_i[0] % 2
            kv_i[0] += 1
            kvs = KV[:, sl, :]
            P.op("act", lambda e, o=kvs.ap, i=b[:, 0:256].ap: e.activation(out=o, in_=i, func=AF.Copy), [b[:, 0:256]], [kvs])
            P.op("dve", lambda e, o=kvs.ap, b_=bkv[:].ap: e.tensor_tensor(out=o, in0=o, in1=b_, op=ALU.add), [kvs, bkv[:]], [kvs])
            if g == NG - 1:
                P.dma("sp", dram_v(kwin_p), KV[:, sl, 0:128], "kvout")
                P.dma("sp", dram_v(vwin_p), KV[:, sl, 128:256], "kvout")
            else:
                for bq in range(16):
                    P.dma("sp", dram_v(kwin_s, kwin_s.h[bq, 120:128, :], "_%d" % bq),
                          V(KV.h[bq * 8:(bq + 1) * 8, sl, 0:128], "KV", sl * 256, sl * 256 + 128), "kvout")
                    P.dma("sp", dram_v(vwin_s, vwin_s.h[bq, 120:128, :], "_%d" % bq),
                          V(KV.h[bq * 8:(bq + 1) * 8, sl, 128:256], "KV", sl * 256 + 128, sl * 256 + 256), "kvout")

    bank_i = [0]

    reserved = set()

    def nb():
        while True:
            i = bank_i[0] % 7
            bank_i[0] += 1
            if i not in reserved:
                return BK[i]

    wi_ap = w_in.h.rearrange("(c p) n -> p c n", p=128)

    def win_block(blk):
        s_ = wgu_i[0] % 3
        wgu_i[0] += 1
        slot2d = V(WGU.h[:, s_, :, :].rearrange("p c n -> p (c n)"), "WGU", WGU[:, s_, :, :].lo, WGU[:, s_, :, :].hi)
        ck = ("in", blk)
        if ck in cached:
            P.dma("sp", slot2d, dram_v(sc_in, sc_in.h[blk], "_%d" % blk), "hgu%d" % s_)
        else:
            for hh in range(2):
                P.dma("pool", WGU[:, s_, :, hh * 128:(hh + 1) * 128],
                      dram_v(w_in, wi_ap[:, :, blk * 256 + hh * 128: blk * 256 + (hh + 1) * 128]), "wgu%d_%d" % (s_, hh))
            P.dma("sp", dram_v(sc_in, sc_in.h[blk], "_%d" % blk), slot2d, "wbg%d" % s_)
            cached.add(ck)
        return s_

    def tm_proj(s_, j, bank, col0):
        for c in range(8):
            la = xnT[:, c, j * 128:(j + 1) * 128]
            rw = WGU[:, s_, c, :]
            P.op("pe", lambda e, o=bank[:, col0:col0 + 256].ap, l=la.ap, r=rw.ap, c=c: e.matmul(o, lhsT=l, rhs=r, start=(c == 0), stop=(c == 7)),
                 [la, rw], [bank[:, col0:col0 + 256]], sig=(c == 7))

    def A(eng, fn, reads, writes):
        P.op(eng, fn, reads, writes)

    def fm_proj(s_, half, N, bank, col0=0):
        for c in range(8):
            lw = WGU[:, s_, c, half * 128:(half + 1) * 128]
            rx = xnT[:, c, col0:col0 + N]
            P.op("pe", lambda e, o=bank[:, 0:N].ap, l=lw.ap, r=rx.ap, c=c: e.matmul(o, lhsT=l, rhs=r, start=(c == 0), stop=(c == 7)),
                 [lw, rx], [bank[:, 0:N]], sig=(c == 7))

    def hgrn_head_feat(h, N, ng, s_hf, s_hq, phase, rm, col0=0, csz=64):
        nch = N // csz
        bf = nb()
        fm_proj(s_hf, h % 2, N, bf, col0)
        a_, b_, c_, d_, e_, f_ = [HT[:, i, 0:N] for i in range(6)]
        bcol = binF[:, 10 + h:11 + h]
        A("act", lambda e, o=a_.ap, i=bf[:, 0:N].ap, b=bcol.ap: e.activation(out=o, in_=i, func=AF.Sigmoid, bias=b),
          [bf[:, 0:N], bcol], [a_])
        om, lb_ = lbt[:, 2, h:h + 1], lbt[:, 1, h:h + 1]
        A("dve", lambda e, o=a_.ap, s1=om.ap, s2=lb_.ap: e.tensor_scalar(out=o, in0=o, scalar1=s1, scalar2=s2, op0=ALU.mult, op1=ALU.add),
          [a_, om, lb_], [a_])
        A("act", lambda e, o=b_.ap, i=a_.ap: e.activation(out=o, in_=i, func=AF.Ln), [a_], [b_])
        A("dve", lambda e, o=d_.ap, i=a_.ap: e.tensor_scalar(out=o, in0=i, scalar1=-1.0, scalar2=1.0, op0=ALU.mult, op1=ALU.add),
          [a_], [d_])
        rmv = rm[:, 0:N]
        A("dve", lambda e, o=c_.ap, m=rmv.ap, i=b_.ap: e.tensor_tensor_scan(out=o, data0=m, data1=i, initial=0.0, op0=ALU.mult, op1=ALU.add),
          [rmv, b_], [c_])
        c_last = c_.ap.rearrange("p (c t) -> p c t", t=csz)[:, :, csz - 1]
        dch = DCH[:, h, 0:nch]
        A("act", lambda e, o=dch.ap, i=c_last: e.activation(out=o, in_=i, func=AF.Exp), [c_], [dch])
        A("act", lambda e, o=b_.ap, i=c_.ap: e.activation(out=o, in_=i, func=AF.Exp, scale=-1.0), [c_], [b_])
        A("dve", lambda e, o=d_.ap, i=b_.ap: e.tensor_tensor(out=o, in0=o, in1=i, op=ALU.mult), [d_, b_], [d_])
        if phase >= 2:
            kt = ktT[:, h, 0:N]
            A("act", lambda e, o=kt.ap, i=d_.ap: e.activation(out=o, in_=i, func=AF.Copy), [d_], [kt])
        kh = khT[:, h, 0:N]
        kh3 = kh.ap.rearrange("p (c t) -> p c t", t=csz)
        d3 = d_.ap.rearrange("p (c t) -> p c t", t=csz)
        dcb = dch.ap.unsqueeze(2).to_broadcast([128, nch, csz])
        A("dve", lambda e, o=kh3, i=d3, b=dcb: e.tensor_tensor(out=o, in0=i, in1=b, op=ALU.mult), [d_, dch], [kh])
        if phase >= 2:
            bq = nb()
            fm_proj(s_hq, h % 2, N, bq, col0)
            qcol = binF[:, 6 + h:7 + h]
            A("act", lambda e, o=f_.ap, i=bq[:, 0:N].ap, b=qcol.ap: e.activation(out=o, in_=i, func=AF.Silu, bias=b),
              [bq[:, 0:N], qcol], [f_])
            A("act", lambda e, o=e_.ap, i=c_.ap: e.activation(out=o, in_=i, func=AF.Exp), [c_], [e_])
            if phase == 2:
                A("dve", lambda e, o=f_.ap, b=e_.ap: e.tensor_tensor(out=o, in0=o, in1=b, op=ALU.mult), [f_, e_], [f_])
                f3 = f_.ap.rearrange("p (g t) -> p g t", t=128)
                for par, qq in ((0, qtT0), (1, qtT1)):
                    qt = qq[:, h, 0:N]
                    q3 = qt.ap.rearrange("p (g t) -> p g t", t=128)
                    cp = cpar[:, par, :]
                    cpb = cp.ap.unsqueeze(1).to_broadcast([128, ng, 128])
                    A("dve", lambda e, o=q3, i=f3, m=cpb: e.tensor_tensor(out=o, in0=i, in1=m, op=ALU.mult), [f_, cp], [qt])
            else:
                qt = qtT0[:, h, 0:N]
                A("dve", lambda e, o=qt.ap, i=f_.ap, b=e_.ap: e.tensor_tensor(out=o, in0=i, in1=b, op=ALU.mult), [f_, e_], [qt])
        for j in range(ng):
            src = khT[:, h, j * 128:(j + 1) * 128]
            dst = TPB(j * 128, (j + 1) * 128)
            P.op("pe", lambda e, o=dst.ap, i=src.ap: e.transpose(out=o, in_=i, identity=identb[:].ap),
                 [src, identb[:]], [dst], sig=(j == ng - 1))
        tp = TPB(0, ng * 128)
        tp3 = bk7b[:, 0:ng * 128].rearrange("p (g k) -> p g k", g=ng)
        kd = khat_tok[:, 0:ng, h, :]
        A("act", lambda e, o=kd.ap, i=tp3: e.activation(out=o, in_=i, func=AF.Copy), [tp], [kd])

    def hi_proj(ng, s7, s8, js=None):
        for j in (range(ng) if js is None else js):
            bh = nb()
            tm_proj(s7, j, bh, 0)
            tm_proj(s8, j, bh, 256)
            ht = hi_tok[:, j, :]
            A("dve", lambda e, o=ht.ap, b=bhi[:].ap, i=bh[:].ap: e.tensor_tensor(out=o, in0=b, in1=i, op=ALU.add),
              [bhi[:], bh[:]], [ht])

    sver = [0]

    def state_update(j, ci, phase):
        pb = nb()
        for h in range(4):
            la = khat_tok[ci * 64:(ci + 1) * 64, j, h, :]
            rv = hi_tok[ci * 64:(ci + 1) * 64, j, h * 128:(h + 1) * 128]
            ob = pb[:, h * 128:(h + 1) * 128]
            P.op("pe", lambda e, o=ob.ap, l=la.ap, r=rv.ap: e.matmul(o, lhsT=l, rhs=r, start=True, stop=True),
                 [la, rv], [ob], sig=(h == 3))
        t32 = HT[:, 0, :]
        A("act", lambda e, o=t32.ap, i=pb[:].ap: e.activation(out=o, in_=i, func=AF.Copy), [pb[:]], [t32])
        ch = j * 2 + ci
        dsel = DCH[:, :, ch:ch + 1]
        db = dsel.ap.to_broadcast([128, 4, 128])
        A("dve", lambda e, o=S[:].ap, b=db: e.tensor_tensor(out=o, in0=o, in1=b, op=ALU.mult), [S[:], dsel], [S[:]])
        t3 = t32.ap.rearrange("p (h v) -> p h v", h=4)
        A("dve", lambda e, o=S[:].ap, i=t3: e.tensor_tensor(out=o, in0=o, in1=i, op=ALU.add), [S[:], t32], [S[:]])
        if phase == 2:
            sver[0] += 1
            sb_ = Sbf[:, sver[0] % 3, :, :]
            A("act", lambda e, o=sb_.ap, i=S[:].ap: e.activation(out=o, in_=i, func=AF.Copy), [S[:]], [sb_])

    def hgrn_scan_tile(tl):
        ng = len(tl)
        N = 128 * ng
        rms_to_T(tl, 1, xnT)
        s7 = win_block(7)
        s8 = win_block(8)
        hi_proj(ng, s7, s8)
        s5 = win_block(5)
        hgrn_head_feat(0, N, ng, s5, None, 1, rmask)
        hgrn_head_feat(1, N, ng, s5, None, 1, rmask)
        s6 = win_block(6)
        hgrn_head_feat(2, N, ng, s6, None, 1, rmask)
        hgrn_head_feat(3, N, ng, s6, None, 1, rmask)
        for j in range(ng):
            for ci in range(2):
                state_update(j, ci, 1)

    def exchange():
        if os.environ.get("NOEXCH"):
            return
        P.dma("sp", dram_v(cc_in), S[:], "ccio")
        P.cnt_cc = getattr(P, "cnt_cc", 0) + 1
        tok = ("cc", P.cnt_cc)
        deps = P._deps([dram_v(cc_in)], [dram_v(cc_out)], tok)
        for k_, v_ in P.dma_cnt.items():
            deps.add((k_, v_))
        P.q["pool"].append((deps, lambda e: e.collective_compute("AllGather", ALU.bypass, replica_groups=[list(range(8))],
                                                                   ins=[cc_in.h], outs=[cc_out.h]), "cc", 1))
        P.dma("sp", G8[:], dram_v(cc_out, cc_out.h.rearrange("(r p) f -> p r f", p=128)), "ccio")
        s2 = S[:].ap.rearrange("p h v -> p (h v)")
        g0 = G8[:, 0, :]
        A("dve", lambda e, o=s2, i=g0.ap, c=sel[:, 0:1].ap: e.tensor_scalar(out=o, in0=i, scalar1=c, scalar2=None, op0=ALU.mult),
          [g0, sel[:]], [S[:]])
        for r_ in range(1, 8):
            gr = G8[:, r_, :]
            A("dve", lambda e, o=gr.ap, c=sel[:, r_:r_ + 1].ap: e.tensor_scalar(out=o, in0=o, scalar1=c, scalar2=None, op0=ALU.mult),
              [gr, sel[:]], [gr])
            A("dve", lambda e, o=s2, i=gr.ap: e.tensor_tensor(out=o, in0=o, in1=i, op=ALU.add), [S[:], gr], [S[:]])

    def kv_part(tl, ng, N):
        if SUB < 2:
            return
        s2 = win_block(2)
        for kv in range(2):
            for dup in range(2):
                srcw = WGU[:, s2, :, kv * 64:(kv + 1) * 64]
                dstw = WK2[:, :, kv, dup * 64:(dup + 1) * 64]
                A("dve", lambda e, o=dstw.ap, i=srcw.ap: e.tensor_copy(out=o, in_=i), [srcw], [dstw])
        for kv in range(2):
            bkk = nb()
            for c in range(8):
                lw = WK2[:, c, kv, :]
                rx = xnT[:, c, 0:N]
                ob = bkk[:, 0:N]
                P.op("pe", lambda e, o=ob.ap, l=lw.ap, r=rx.ap, c=c: e.matmul(o, lhsT=l, rhs=r, start=(c == 0), stop=(c == 7)),
                     [lw, rx], [ob], sig=(c == 7))
            kd = kT2[:, kv, 128:128 + N]
            bc = bk2[:, kv:kv + 1]
            A("act", lambda e, o=kd.ap, i=bkk[:, 0:N].ap, b=bc.ap: e.activation(out=o, in_=i, func=AF.Identity, bias=b),
              [bkk[:, 0:N], bc], [kd])
        if SUB < 3:
            return
        for j, g in enumerate(tl):
            bv = nb()
            tm_proj(s2, j, bv, 0)
            vd = V1[:, 1 + j, :, 0:64]
            if g not in (1, NG - 1):
                bvv = bkv[:, 128:256].ap.rearrange("p (k d) -> p k d", k=2)
                pv3 = bv[:, 128:256].ap.rearrange("p (k d) -> p k d", k=2)
                A("dve", lambda e, o=vd.ap, b=bvv, i=pv3: e.tensor_tensor(out=o, in0=b, in1=i, op=ALU.add),
                  [bkv[:], bv[:, 128:256]], [vd])
            else:
                sl = kv_i[0] % 2
                kv_i[0] += 1
                kvs = KV[:, sl, :]
                A("act", lambda e, o=kvs.ap, i=bv[:, 0:256].ap: e.activation(out=o, in_=i, func=AF.Copy), [bv[:, 0:256]], [kvs])
                A("dve", lambda e, o=kvs.ap, b_=bkv[:].ap: e.tensor_tensor(out=o, in0=o, in1=b_, op=ALU.add), [kvs, bkv[:]], [kvs])
                kv3 = KV[:, sl, 128:256]
                A("dve", lambda e, o=vd.ap, i=kv3.ap.rearrange("p (k d) -> p k d", k=2): e.tensor_copy(out=o, in_=i), [kv3], [vd])
                KVDMA = int(os.environ.get("KVDMA", "2"))
                if g == NG - 1:
                    if KVDMA >= 1:
                        P.dma("sp", dram_v(kwin_p), KV[:, sl, 0:128], "kvout")
                        P.dma("sp", dram_v(vwin_p), KV[:, sl, 128:256], "kvout")
                elif KVDMA >= 2:
                    for bq in range(16):
                        P.dma("sp", dram_v(kwin_s, kwin_s.h[bq, 120:128, :], "_%d" % bq),
                              V(KV.h[bq * 8:(bq + 1) * 8, sl, 0:128], "KV", sl * 256, sl * 256 + 128), "kvout")
                        P.dma("sp", dram_v(vwin_s, vwin_s.h[bq, 120:128, :], "_%d" % bq),
                              V(KV.h[bq * 8:(bq + 1) * 8, sl, 128:256], "KV", sl * 256 + 128, sl * 256 + 256), "kvout")

    def carry_prev(ng, N):
        if SUB < 5:
            return
        src = kT2[:, :, N:N + 128]
        dst = kT2[:, :, 0:128]
        A("dve", lambda e, o=dst.ap, i=src.ap: e.tensor_copy(out=o, in_=i), [src], [dst])
        sv, dv = V1[:, ng, :, :], V1[:, 0, :, :]
        A("dve", lambda e, o=dv.ap, i=sv.ap: e.tensor_copy(out=o, in_=i), [sv], [dv])

    def q_part(N):
        for blk in (0, 1):
            sq = win_block(blk)
            for half in range(2):
                cc = blk * 2 + half
                bq = nb()
                fm_proj(sq, half, N, bq)
                qd = qT[:, cc, 0:N]
                bc = binF[:, cc:cc + 1]
                A("act", lambda e, o=qd.ap, i=bq[:, 0:N].ap, b=bc.ap: e.activation(out=o, in_=i, func=AF.Identity, bias=b),
                  [bq[:, 0:N], bc], [qd])

    def attention_group(j, g):
        for kv in range(2):
            for blk in range(2):
                bs = nb()
                c0 = j * 128 + blk * 128
                for hl in range(4):
                    h = kv * 4 + hl
                    p0 = (h % 2) * 64
                    ob = bs[:, hl * 128:(hl + 1) * 128]
                    mk = maskH[:] if (g == 2 and blk == 0) else maskP[:, blk, :]
                    P.op("pe", lambda e, o=ob.ap, r=mk.ap: e.matmul(o, lhsT=identb[:].ap, rhs=r, start=True, stop=False),
                         [identb[:], mk], [ob], sig=False)
                    lk = kT2[p0:p0 + 64, kv, c0:c0 + 128]
                    rq = qT[p0:p0 + 64, h // 2, j * 128:(j + 1) * 128]
                    P.op("pe", lambda e, o=ob.ap, l=lk.ap, r=rq.ap: e.matmul(o, lhsT=l, rhs=r, start=False, stop=True),
                         [lk, rq], [ob], sig=(hl == 3))
                pt = PT[:, blk, :]
                A("act", lambda e, o=pt.ap, i=bs[:].ap: e.activation(out=o, in_=i, func=AF.Exp, scale=SCALE), [bs[:]], [pt])
            bov = nb()
            for hl in range(4):
                ob = bov[:, hl * 66:hl * 66 + 65]
                for blk in range(2):
                    lp = PT[:, blk, hl * 128:(hl + 1) * 128]
                    rv = V1[:, j + blk, kv, 0:65]
                    P.op("pe", lambda e, o=ob.ap, l=lp.ap, r=rv.ap, blk=blk: e.matmul(o, lhsT=l, rhs=r, start=(blk == 0), stop=(blk == 1)),
                         [lp, rv], [ob], sig=(hl == 3 and blk == 1))
            oa2 = OA[:].ap.rearrange("p h d -> p (h d)")
            A("act", lambda e, o=oa2, i=bov[:, 0:264].ap: e.activation(out=o, in_=i, func=AF.Copy), [bov[:, 0:264]], [OA[:]])
            attn_norm(kv)
        attn_finish()

    def attn_norm(kv):
        if True:
            dn = stcol(4)
            es4 = esink[:, kv * 4:(kv + 1) * 4]
            A("dve", lambda e, o=dn.ap, i=OA[:, :, 64].ap, b=es4.ap: e.tensor_tensor(out=o, in0=i, in1=b, op=ALU.add), [OA[:], es4], [dn])
            rd = stcol(4)
            A("dve", lambda e, o=rd.ap, i=dn.ap: e.reciprocal(out=o, in_=i), [dn], [rd])
            ad = a_tok[:, kv * 256:(kv + 1) * 256]
            ad3 = ad.ap.rearrange("p (h d) -> p h d", h=4)
            rb = rd.ap.unsqueeze(2).to_broadcast([128, 4, 64])
            A("dve", lambda e, o=ad3, i=OA[:, :, 0:64].ap, b=rb: e.tensor_tensor(out=o, in0=i, in1=b, op=ALU.mult), [OA[:], rd], [ad])

    def attn_finish():
        ssa = stcol()
        A("act", lambda e, o=junk[:, 0:512].ap, i=a_tok[:].ap, a=ssa.ap: e.activation(out=o, in_=i, func=AF.Square, accum_out=a),
          [a_tok[:]], [junk[:, 0:512], ssa])
        ra = rstd_of(ssa, 512, 1.0)
        A("dve", lambda e, o=a_tok[:].ap, s_=ra.ap: e.tensor_scalar(out=o, in0=o, scalar1=s_, scalar2=None, op0=ALU.mult), [a_tok[:], ra], [a_tok[:]])
        md = mix_tok[:, 0:512]
        A("dve", lambda e, o=md.ap, i=a_tok[:].ap, g_=gattn[:].ap: e.tensor_tensor(out=o, in0=i, in1=g_, op=ALU.mult), [a_tok[:], gattn[:]], [md])

    def gate_group(j, s9, s10):
        bg = nb()
        tm_proj(s9, j, bg, 0)
        tm_proj(s10, j, bg, 256)
        A("act", lambda e, o=gate[:].ap, i=bg[:].ap: e.activation(out=o, in_=i, func=AF.Copy), [bg[:]], [gate[:]])
        A("dve", lambda e, o=gate[:].ap, b=bhg[:].ap: e.tensor_tensor(out=o, in0=o, in1=b, op=ALU.add), [gate[:], bhg[:]], [gate[:]])
        A("act", lambda e, o=gate[:].ap: e.activation(out=o, in_=o, func=AF.Silu), [gate[:]], [gate[:]])

    def hgrn_out_group(j, s9, s10):
        gate_group(j, s9, s10)
        gs = slice(j * 128, (j + 1) * 128)
        for h in range(4):
            ba = nb()
            lk = ktT[:, h, gs]
            for par, qq in ((0, qtT0), (1, qtT1)):
                rq = qq[:, h, gs]
                P.op("pe", lambda e, o=ba[:, 0:128].ap, l=lk.ap, r=rq.ap, par=par: e.matmul(o, lhsT=l, rhs=r, start=(par == 0), stop=(par == 1)),
                     [lk, rq], [ba[:, 0:128]], sig=(par == 1))
            at = AT[:, h, :]
            A("dve", lambda e, o=at.ap, m=hm_cur[0][:].ap, i=ba[:, 0:128].ap: e.tensor_tensor(out=o, in0=m, in1=i, op=ALU.mult),
              [hm_cur[0][:], ba[:, 0:128]], [at])
        vers = [sver[0] % 3]
        for ci in range(2):
            state_update(j, ci, 2)
            vers.append(sver[0] % 3)
        bo = nb()
        for h in range(4):
            ob = bo[:, h * 128:(h + 1) * 128]
            for ci, qq in ((0, qtT0), (1, qtT1)):
                lq = qq[:, h, gs]
                rs = Sbf[:, vers[ci], h, :]
                P.op("pe", lambda e, o=ob.ap, l=lq.ap, r=rs.ap, ci=ci: e.matmul(o, lhsT=l, rhs=r, start=(ci == 0), stop=False),
                     [lq, rs], [ob], sig=False)
            la = AT[:, h, :]
            rv = hi_tok[:, j, h * 128:(h + 1) * 128]
            P.op("pe", lambda e, o=ob.ap, l=la.ap, r=rv.ap: e.matmul(o, lhsT=l, rhs=r, start=False, stop=True),
                 [la, rv], [ob], sig=True)
        hgrn_finish(bo)

    def hgrn_finish(bo):
        A("act", lambda e, o=o_tok[:].ap, i=bo[:].ap: e.activation(out=o, in_=i, func=AF.Copy), [bo[:]], [o_tok[:]])
        ss4 = stcol(4)
        for h in range(4):
            sh = V(ss4.ap[:, h:h + 1], ss4.name, ss4.lo + h, ss4.lo + h + 1)
            oh = o_tok[:, h * 128:(h + 1) * 128]
            A("act", lambda e, o=junk[:, 0:128].ap, i=oh.ap, a=sh.ap: e.activation(out=o, in_=i, func=AF.Square, accum_out=a),
              [oh], [junk[:, 0:128], sh])
        r4 = rstd_of(ss4, 128, 1.0, w=4)
        o3 = o_tok[:].ap.rearrange("p (h v) -> p h v", h=4)
        r4b = r4.ap.unsqueeze(2).to_broadcast([128, 4, 128])
        A("dve", lambda e, o=o3, b=r4b: e.tensor_tensor(out=o, in0=o, in1=b, op=ALU.mult), [o_tok[:], r4], [o_tok[:]])
        A("dve", lambda e, o=o_tok[:].ap, b=ghg4[:].ap: e.tensor_tensor(out=o, in0=o, in1=b, op=ALU.mult), [o_tok[:], ghg4[:]], [o_tok[:]])
        md = mix_tok[:, 512:1024]
        A("dve", lambda e, o=md.ap, i=o_tok[:].ap, g_=gate[:].ap: e.tensor_tensor(out=o, in0=i, in1=g_, op=ALU.mult), [o_tok[:], gate[:]], [md])

    hm_cur = [hmask]

    def mix_to_T(j):
        for c in range(8):
            src = mix_tok[:, c * 128:(c + 1) * 128]
            dst = TPB(c * 128, (c + 1) * 128)
            P.op("pe", lambda e, o=dst.ap, i=src.ap: e.transpose(out=o, in_=i, identity=identb[:].ap),
                 [src, identb[:]], [dst], sig=(c == 7))
        tp = TPB(0, 1024)
        tp3 = bk7b[:, 0:1024].rearrange("p (c t) -> p c t", c=8)
        d3 = xnT[:, :, j * 128:(j + 1) * 128]
        A("act", lambda e, o=d3.ap, i=tp3: e.activation(out=o, in_=i, func=AF.Copy), [tp], [d3])

    def out_proj(tl, js):
        P.dma("sp", GP[:], dram_v(gpost, gpost.h[1]), "gp")
        for c in range(8):
            s_ = wdr_i[0] % 4
            wdr_i[0] += 1
            ck = ("out", c)
            if ck in cached:
                P.dma("sp", WDR[:, s_, :], dram_v(sc_out, sc_out.h[c], "_%d" % c), "hdr%d" % s_)
            else:
                P.dma("pool", WDR[:, s_, :], dram_v(w_out, w_out.h[c * 128:(c + 1) * 128, :]), "wdr%d" % s_)
                P.dma("sp", dram_v(sc_out, sc_out.h[c], "_%d" % c), WDR[:, s_, :], "wbd%d" % s_)
                cached.add(ck)
            for j in js:
                for h in range(2):
                    b = BK[j * 2 + h]
                    la = xnT[:, c, j * 128:(j + 1) * 128]
                    rw = WDR[:, s_, h * 512:(h + 1) * 512]
                    P.op("pe", lambda e, o=b[:].ap, l=la.ap, r=rw.ap, c=c: e.matmul(o, lhsT=l, rhs=r, start=(c == 0), stop=(c == 7)),
                         [la, rw], [b[:]], sig=(c == 7 or (j == js[-1] and h == 1)))
        for j in js:
            post_norm_add(tl[j], [BK[j * 2], BK[j * 2 + 1]], 1, 1.0, j % 2)

    def sample_hgrn(j, s9, s10):
        gate_group(j, s9, s10)
        A("dve", lambda e: e.memset(qexp[:].ap, 0.0), [], [qexp[:]])
        bo = BK[6]
        reserved.add(6)
        for h in range(4):
            P.dma("sp", Sb32[:], dram_v(st_in, st_in.h[:, h].rearrange("b k v -> k b v")), "sb32")
            sbf2 = Sbbf[:].ap.rearrange("p b v -> p (b v)")
            s322 = Sb32[:].ap.rearrange("p b v -> p (b v)")
            A("act", lambda e, o=sbf2, i=s322: e.activation(out=o, in_=i, func=AF.Copy), [Sb32[:]], [Sbbf[:]])
            qsrc = qtT0[:, h, 0:128]
            qd = qexp[:].ap[:, 0:16 * 136].rearrange("p (b x) -> p b x", x=136)[:, :, 0:8]
            A("dve", lambda e, o=qd, i=qsrc.ap.rearrange("p (b t) -> p b t", t=8): e.tensor_copy(out=o, in_=i), [qsrc], [qexp[:]])
            ba = nb()
            lk = ktT[:, h, 0:128]
            P.op("pe", lambda e, o=ba[:, 0:128].ap, l=lk.ap, r=qsrc.ap: e.matmul(o, lhsT=l, rhs=r, start=True, stop=True),
                 [lk, qsrc], [ba[:, 0:128]])
            at = AT[:, h, :]
            A("dve", lambda e, o=at.ap, m=hmaskS[:].ap, i=ba[:, 0:128].ap: e.tensor_tensor(out=o, in0=m, in1=i, op=ALU.mult),
              [hmaskS[:], ba[:, 0:128]], [at])
            ob = bo[:, h * 128:(h + 1) * 128]
            for b in range(16):
                lq = qexp[:, b * 128:(b + 1) * 128]
                rs = Sbbf[:, b, :]
                P.op("pe", lambda e, o=ob.ap, l=lq.ap, r=rs.ap, b=b: e.matmul(o, lhsT=l, rhs=r, start=(b == 0), stop=False),
                     [lq, rs], [ob], sig=False)
            rv = hi_tok[:, j, h * 128:(h + 1) * 128]
            P.op("pe", lambda e, o=ob.ap, l=at.ap, r=rv.ap: e.matmul(o, lhsT=l, rhs=r, start=False, stop=True), [at, rv], [ob])
            lkh = khat_tok[:, 0, h, :]
            for b4 in range(4):
                pb = nb()
                for bb in range(4):
                    b = b4 * 4 + bb
                    vm = vmk[:, bb, :]
                    sc = seqsel[:, b:b + 1]
                    A("dve", lambda e, o=vm.ap, i=rv.ap, c=sc.ap: e.tensor_scalar(out=o, in0=i, scalar1=c, scalar2=None, op0=ALU.mult),
                      [rv, sc], [vm])
                    pbr = pb[:, bb * 128:(bb + 1) * 128]
                    P.op("pe", lambda e, o=pbr.ap, l=lkh.ap, r=vm.ap: e.matmul(o, lhsT=l, rhs=r, start=True, stop=True), [lkh, vm], [pbr])
                t32 = HT[:, 0, :]
                A("act", lambda e, o=t32.ap, i=pb[:].ap: e.activation(out=o, in_=i, func=AF.Copy), [pb[:]], [t32])
                s4 = Sb32[:, b4 * 4:(b4 + 1) * 4, :]
                dsel = DCH[:, h, b4 * 4:(b4 + 1) * 4]
                db = dsel.ap.unsqueeze(2).to_broadcast([128, 4, 128])
                A("dve", lambda e, o=s4.ap, b_=db: e.tensor_tensor(out=o, in0=o, in1=b_, op=ALU.mult), [s4, dsel], [s4])
                t3 = t32.ap.rearrange("p (b v) -> p b v", b=4)
                A("dve", lambda e, o=s4.ap, i=t3: e.tensor_tensor(out=o, in0=o, in1=i, op=ALU.add), [s4, t32], [s4])
            P.dma("sp", dram_v(state_s, state_s.h[:, h].rearrange("b k v -> k b v"), "_%d" % h), Sb32[:], "sbo")
        reserved.discard(6)
        hgrn_finish(bo)

    def sample_attention(j):
        qc = slice(j * 128, (j + 1) * 128)
        ring = [0]
        for kv in range(2):
            accs = [BK[3 + hl] for hl in range(4)]
            for blk in range(17):
                if blk < 16:
                    r3 = ring[0] % 3
                    r2 = ring[0] % 2
                    ring[0] += 1
                    for dup in range(2):
                        P.dma("pool", Kd[:, r3, dup * 64:(dup + 1) * 64],
                              dram_v(cache_k, cache_k.h[blk, :, kv * 64:(kv + 1) * 64]), "kd%d_%d" % (r3, dup))
                    P.dma("pool", Vb[:, r3, 0:64], dram_v(cache_v, cache_v.h[blk, :, kv * 64:(kv + 1) * 64]), "vb%d" % r3)
                    src = Kd[:, r3, :]
                    dst = TPB(0, 128)
                    P.op("pe", lambda e, o=dst.ap, i=src.ap: e.transpose(out=o, in_=i, identity=identb[:].ap), [src, identb[:]], [dst])
                    kc = kcT[:, r2, :]
                    A("act", lambda e, o=kc.ap, i=dst.ap: e.activation(out=o, in_=i, func=AF.Copy), [dst], [kc])
                    kview = lambda p0, r2=r2: kcT[p0:p0 + 64, r2, :]
                    rv = Vb[:, r3, 0:65]
                    mk = Zm[:, 120 - 8 * blk: 120 - 8 * blk + 128]
                else:
                    kview = lambda p0, kv=kv: kT2[p0:p0 + 64, kv, 128 + j * 128: 128 + (j + 1) * 128]
                    rv = V1[:, 1 + j, kv, 0:65]
                    mk = newmask[:]
                bs = BK[blk % 2]
                for hl in range(4):
                    h = kv * 4 + hl
                    p0 = (h % 2) * 64
                    ob = bs[:, hl * 128:(hl + 1) * 128]
                    P.op("pe", lambda e, o=ob.ap, r=mk.ap: e.matmul(o, lhsT=identb[:].ap, rhs=r, start=True, stop=False),
                         [identb[:], mk], [ob], sig=False)
                    lk = kview(p0)
                    rq = qT[p0:p0 + 64, h // 2, qc]
                    P.op("pe", lambda e, o=ob.ap, l=lk.ap, r=rq.ap: e.matmul(o, lhsT=l, rhs=r, start=False, stop=True),
                         [lk, rq], [ob], sig=(hl == 3))
                pt = PT[:, blk % 2, :]
                A("act", lambda e, o=pt.ap, i=bs[:].ap: e.activation(out=o, in_=i, func=AF.Exp, scale=SCALE), [bs[:]], [pt])
                for hl in range(4):
                    ob = accs[hl][:, 0:65]
                    lp = PT[:, blk % 2, hl * 128:(hl + 1) * 128]
                    P.op("pe", lambda e, o=ob.ap, l=lp.ap, r=rv.ap, blk=blk: e.matmul(o, lhsT=l, rhs=r, start=(blk == 0), stop=(blk == 16)),
                         [lp, rv], [ob], sig=(hl == 3))
            for hl in range(4):
                od = OA[:, hl, 0:65]
                A("act", lambda e, o=od.ap, i=accs[hl][:, 0:65].ap: e.activation(out=o, in_=i, func=AF.Copy), [accs[hl][:, 0:65]], [od])
            attn_norm(kv)
        attn_finish()

    def mixer_tile(tl):
        ng = len(tl)
        N = 128 * ng
        prenorm(tl, 1)
        kv_part(tl, ng, N)
        if tl[0] == 0:
            carry_prev(1, 128)
            q_part(N)
            s7 = win_block(7)
            s8 = win_block(8)
            hi_proj(ng, s7, s8, js=[1])
            s3 = win_block(3)
            s5 = win_block(5)
            hgrn_head_feat(0, 128, 1, s5, s3, 3, rmaskS, col0=128, csz=8)
            hgrn_head_feat(1, 128, 1, s5, s3, 3, rmaskS, col0=128, csz=8)
            s4 = win_block(4)
            s6 = win_block(6)
            hgrn_head_feat(2, 128, 1, s6, s4, 3, rmaskS, col0=128, csz=8)
            hgrn_head_feat(3, 128, 1, s6, s4, 3, rmaskS, col0=128, csz=8)
            s9 = win_block(9)
            s10 = win_block(10)
            sample_hgrn(1, s9, s10)
            sample_attention(1)
            mix_to_T(1)
            out_proj(tl, [1])
            return
        if MIXLVL < 2:
            carry_prev(ng, N)
            return
        q_part(N)
        s7 = win_block(7)
        s8 = win_block(8)
        hi_proj(ng, s7, s8)
        s3 = win_block(3)
        s5 = win_block(5)
        hgrn_head_feat(0, N, ng, s5, s3, 2, rmask)
        hgrn_head_feat(1, N, ng, s5, s3, 2, rmask)
        s4 = win_block(4)
        s6 = win_block(6)
        hgrn_head_feat(2, N, ng, s6, s4, 2, rmask)
        hgrn_head_feat(3, N, ng, s6, s4, 2, rmask)
        s9 = win_block(9)
        s10 = win_block(10)
        for j, g in enumerate(tl):
            if MIXLVL >= 3:
                hgrn_out_group(j, s9, s10)
            if MIXLVL >= 4:
                attention_group(j, g)
            if MIXLVL >= 5:
                mix_to_T(j)
        carry_prev(ng, N)
        if MIXLVL >= 5:
            out_proj(tl, list(range(ng)))

    for dst_, src_, nm in ((maskP, maskP_in, "ca"), (maskH, maskH_in, "cb"), (Zm, zm_in, "ci"), (newmask, nm_in, "cj")):
        P.dma("pool", dst_[:], dram_v(src_), nm)
    for dst_, src_, nm in ((hmaskS, hmS_in, "ck"), (rmaskS, rmS_in, "cl"), (seqsel, seqsel_in, "cm")):
        P.dma("sp", dst_[:], dram_v(src_), nm)
    A("dve", lambda e: e.memset(Vb[:].ap.rearrange("p a b -> p (a b)"), 1.0), [], [Vb[:]])
    for dst_, src_, nm in ((esink_raw, sink_in, "cg"), (gattn, gattn_in, "cd"), (ghg4, ghg_in, "ce"), (bk2, bk2_in, "cf"),
                           (cpar, cpar_in, "ch")):
        P.dma("sp", dst_[:], dram_v(src_), nm)
    A("act", lambda e: e.activation(out=esink[:].ap, in_=esink_raw[:].ap, func=AF.Exp), [esink_raw[:]], [esink[:]])
    A("dve", lambda e: e.memset(V1[:].ap.rearrange("p a b c -> p (a b c)"), 1.0), [], [V1[:]])
    for dst_, src_, nm in ((binF, binF_in, "c3"), (bhi, bhi_in, "c4"), (bhg, bhg_in, "c5"), (lbl, lbl_in, "c6"),
                           (rmask, rmask_in, "c7"), (hmask, hmask_in, "c8"), (sel, sel_in, "c9")):
        P.dma("sp", dst_[:], dram_v(src_), nm)
    A("dve", lambda e: e.tensor_tensor(out=lbt[:, 0, :].ap, in0=lbl[:, 0, :].ap, in1=lbl[:, 1, :].ap, op=ALU.subtract),
      [lbl[:]], [lbt[:, 0, :]])
    A("act", lambda e: e.activation(out=lbt[:, 1, :].ap, in_=lbt[:, 0, :].ap, func=AF.Sigmoid), [lbt[:, 0, :]], [lbt[:, 1, :]])
    A("dve", lambda e: e.tensor_scalar(out=lbt[:, 2, :].ap, in0=lbt[:, 1, :].ap, scalar1=-1.0, scalar2=1.0, op0=ALU.mult, op1=ALU.add),
      [lbt[:, 1, :]], [lbt[:, 2, :]])
    A("dve", lambda e: e.memset(S[:].ap, 0.0), [], [S[:]])
    P.dma("sp", bkv[:], dram_v(bkv_in), "c2")
    P.dma("sp", dram_v(kwin_s, kwin_s.h[:, 0:120, :], "_c"), dram_v(cache_k, cache_k.h[:, 8:128, :]), "kvck")
    P.dma("sp", dram_v(vwin_s, vwin_s.h[:, 0:120, :], "_c"), dram_v(cache_v, cache_v.h[:, 8:128, :]), "kvcv")
    tiles = [[0, 1]] + [[2 + 4 * t + i for i in range(4)] for t in range(4)]
    if stage == -2:
        rms_to_T(tiles[0], 0, xnT)
    elif stage == -3:
        ffn(tiles[0], 0)
    elif stage == -6:
        ffn(tiles[0], 0, 1)
    elif stage == -7:
        ffn(tiles[0], 0, 2)
    elif stage in (-4, -5):
        wg = w_gu[0].h.rearrange("(c p) n -> p c n", p=128)
        for f in range(3):
            for hh in range(2):
                if stage == -4:
                    P.dma("pool", WGU[:, f, :, hh * 128:(hh + 1) * 128],
                          dram_v(w_gu[0], wg[:, :, hh * DFF + f * 128: hh * DFF + (f + 1) * 128]), "wgu%d_%d" % (f, hh))
                else:
                    for c in range(8):
                        P.dma("pool", WGU[:, f, c, hh * 128:(hh + 1) * 128],
                              dram_v(w_gu[0], w_gu[0].h[c * 128:(c + 1) * 128, hh * DFF + f * 128: hh * DFF + (f + 1) * 128]), "wgu%d_%d" % (f, hh))
    elif stage >= 1:
        if not USE_CC:
            P.dma("sp", pflag[:], dram_v(pflag_in), "cn")
            scratch = tiles[1]
            bg_i = [0]

            def bg_dma(dst_v, src_v):
                bg_queue.append((dst_v, src_v))

            def bg_convert(t):
                if t == 0:
                    wg2 = w_gu[1].h.rearrange("(c p) n -> p c n", p=128)
                    for f in range(NF):
                        d3 = sc_gu[1].h[f].rearrange("p (c n) -> p c n", c=8)
                        for hh in range(2):
                            bg_dma(dram_v(sc_gu[1], d3[:, :, hh * 128:(hh + 1) * 128], "_%d" % f),
                                   dram_v(w_gu[1], wg2[:, :, hh * DFF + f * 128: hh * DFF + (f + 1) * 128]))
                        cached.add(("gu", 1, f))
                elif t == 1:
                    wd2 = w_dn[1].h.rearrange("(f p) n -> p f n", p=128)
                    for f in range(NF):
                        bg_dma(dram_v(sc_dn[1], sc_dn[1].h[f], "_%d" % f), dram_v(w_dn[1], wd2[:, f, :]))
                        cached.add(("dn", 1, f))
                    for c in range(8):
                        bg_dma(dram_v(sc_out, sc_out.h[c], "_%d" % c), dram_v(w_out, w_out.h[c * 128:(c + 1) * 128, :]))
                        cached.add(("out", c))
                elif t == 2:
                    for blk in (2, 0, 1, 3, 4, 9, 10):
                        d3 = sc_in.h[blk].rearrange("p (c n) -> p c n", c=8)
                        for hh in range(2):
                            bg_dma(dram_v(sc_in, d3[:, :, hh * 128:(hh + 1) * 128], "_%d" % blk),
                                   dram_v(w_in, wi_ap[:, :, blk * 256 + hh * 128: blk * 256 + (hh + 1) * 128]))
                        cached.add(("in", blk))

            for t in range(4):
                for i, g in enumerate(scratch):
                    r0 = t * 512 + i * 128
                    P.dma("sp", X[:, g, :], dram_v(xprev, xprev.h[r0:r0 + 128, :]), "x%d" % g)
                ffn(scratch, 0)
                hgrn_scan_tile(scratch)
            if os.environ.get("NOBG") is None:
                for t in range(3):
                    bg_convert(t)
            s2_ = S[:].ap.rearrange("p h v -> p (h v)")
            A("dve", lambda e, o=s2_, c=pflag[:].ap: e.tensor_scalar(out=o, in0=o, scalar1=c, scalar2=None, op0=ALU.mult),
              [S[:], pflag[:]], [S[:]])
            for g in range(2, NG):
                P.dma("sp", X[:, g, :], dram_v(xin, xin.h[g * 128:(g + 1) * 128, :]), "x%d" % g)
        for ti, tl in enumerate(tiles):
            nxt = tiles[ti + 1] if ti + 1 < len(tiles) else None
            ffn(tl, 0, hook=(lambda nxt=nxt: prenorm_early(nxt, 0)) if (nxt is not None and not os.environ.get("NOHOOK")) else None)
    bg_drain(10 ** 6)
    if stage >= 4:
        if USE_CC:
            for tl in tiles[1:]:
                hgrn_scan_tile(tl)
            exchange()
        A("act", lambda e: e.activation(out=Sbf[:, 0, :, :].ap, in_=S[:].ap, func=AF.Copy), [S[:]], [Sbf[:, 0, :, :]])
        if stage == 4:
            for tl in tiles[1:]:
                hgrn_scan_tile(tl)
            P.dma("sp", dram_v(state_p), S[:], "stout")
    if stage >= 5:
        for ti, tl in enumerate(tiles[:1] if os.environ.get("AUXONLY") else tiles):
            mixer_tile(tl)
            if not os.environ.get("AUXONLY"):
                nxt = tiles[ti + 1] if ti + 1 < len(tiles) else None
                ffn(tl, 1, hook=(lambda nxt=nxt: prenorm_early(nxt, 1)) if (nxt is not None and not os.environ.get("NOHOOK")) else None)
        P.dma("sp", dram_v(state_p), S[:], "stout")
    elif stage >= 2:
        for tl in tiles:
            ffn(tl, 1)
    for g in range(1, NG):
        P.dma("sp", dram_v(y, y.h[(g - 1) * 128: g * 128, :]), X[:, g, :], "yout")

    final = set()
    for k, v in P.dma_cnt.items():
        final.add((k, v))
    for e in ENGS:
        if P.cnt[e]:
            final.add((e, P.cnt[e]))
    P.final = final

    P.check()
    print("ops:", {e: len(P.q[e]) for e in ENGS}, "sems:", len(P.dma_cnt) + 5, flush=True)
    semkeys = list(ENGS) + sorted(P.dma_cnt.keys()) + ["cc"]
    sems = {k: es.enter_context(nc.semaphore("s_" + k.replace(":", "_"))) for k in semkeys}
    with nc.Block() as block:
        @block.tensor
        def _(e):
            P.replay("pe", e, sems)

        @block.scalar
        def _(e):
            P.replay("act", e, sems)

        @block.vector
        def _(e):
            P.replay("dve", e, sems)

        @block.gpsimd
        def _(e):
            P.replay("pool", e, sems)

        @block.sync
        def _(e):
            P.replay("sp", e, sems)
    es.close()
    return nc


def _core_inputs(inp, c):
    f32 = np.float32
    b, half = c // 2, c % 2
    xp = inp["x_prompt"]
    xs = inp["x_sample"]
    xin = np.zeros((NG * 128, D), f32)
    if half == 1:
        xin[0:128] = xp[b, 2048 - 128:2048]
    xin[128:256] = xs[16 * c:16 * c + 16].reshape(128, D)
    xin[256:] = xp[b, half * 2048:(half + 1) * 2048]
    m = {"xin": xin}
    m["w_gu1"] = inp["ffn1_w_gu"][0]
    m["w_gu2"] = inp["ffn2_w_gu"][0]
    m["w_dn1"] = inp["ffn1_w_down"][0]
    m["w_dn2"] = inp["ffn2_w_down"][0]
    m["w_in"] = inp["w_in"][0]
    m["w_out"] = inp["w_out"][0]
    gcols = np.stack([inp["norm_ffn1_pre"][0], inp["norm_mix_pre"][0], inp["norm_ffn2_pre"][0]])
    m["gcols"] = np.ascontiguousarray(gcols.reshape(3, 8, 128).transpose(2, 0, 1))
    gp = np.stack([inp["norm_ffn1_post"][0], inp["norm_mix_post"][0], inp["norm_ffn2_post"][0]])
    m["gpost"] = np.ascontiguousarray(np.broadcast_to(gp[:, None, :], (3, 128, D)))
    m["ident"] = np.eye(128, dtype=f32)
    bi = inp["b_in"][0]
    m["binF"] = bi.reshape(22, 128).T
    m["bhi"] = np.broadcast_to(bi[None, 1792:2304], (128, 512))
    m["bhg"] = np.broadcast_to(bi[None, 2304:2816], (128, 512))
    m["lbl"] = inp["hg_lb_logits"].reshape(2, 4, 128).transpose(2, 0, 1)
    rm = np.ones((128, 512), f32)
    rm[:, 0::64] = 0.0
    m["rmask"] = rm
    ii = np.arange(128)
    m["hmask"] = ((ii[:, None] // 64 == ii[None, :] // 64) & (ii[:, None] <= ii[None, :])).astype(f32)
    mp = np.full((128, 2, 128), NEG, f32)
    mp[:, 0][ii[:, None] >= ii[None, :]] = 0.0
    mp[:, 1][ii[:, None] <= ii[None, :]] = 0.0
    m["maskP"] = mp
    m["maskH"] = mp[:, 0] if half == 1 else np.full((128, 128), NEG, f32)
    zm = np.full((128, 248), NEG, f32)
    for t in range(8):
        zm[t:, 120 + t] = 0.0
    m["zmask"] = zm
    sq, tq = ii // 8, ii % 8
    m["newmask"] = np.where((sq[:, None] == sq[None, :]) & (tq[:, None] <= tq[None, :]), 0.0, NEG).astype(f32)
    m["hmaskS"] = ((sq[:, None] == sq[None, :]) & (tq[:, None] <= tq[None, :])).astype(f32)
    rms_ = np.ones((128, 128), f32)
    rms_[:, 0::8] = 0.0
    m["rmaskS"] = rms_
    m["seqsel"] = (sq[:, None] == np.arange(16)[None, :]).astype(f32)
    m["state_s_in"] = inp["state_hgrn"][0, 16 * c:16 * c + 16]
    m["sinks"] = np.broadcast_to(inp["attn_sinks"][0][None, :], (128, 8))
    m["gattn"] = np.broadcast_to(inp["attn_out_norm"][0][None, :], (128, 512))
    m["ghg4"] = np.broadcast_to(np.tile(inp["hg_out_norm"][0], 4)[None, :], (128, 512))
    bk = bi[512:640].reshape(2, 64)
    m["bk2"] = np.concatenate([bk, bk], axis=1).T
    cp = np.zeros((128, 2, 128), f32)
    cp[:, 0, 0:64] = 1.0
    cp[:, 1, 64:128] = 1.0
    m["cpar"] = cp
    m["xprev"] = xp[b, 0:2048] if half == 1 else np.zeros((2048, D), f32)
    m["pflag"] = np.full((128, 1), float(half), f32)
    selv = np.zeros((128, 8), f32)
    if half == 1:
        selv[:, c - 1] = 1.0
    m["sel"] = selv
    m["bkv"] = np.broadcast_to(inp["b_in"][0][None, 512:768], (128, 256))
    m["cache_k"] = inp["cache_k_win"][0, 16 * c:16 * c + 16].reshape(16, 128, 128)
    m["cache_v"] = inp["cache_v_win"][0, 16 * c:16 * c + 16].reshape(16, 128, 128)
    return {k: np.ascontiguousarray(v, dtype=f32) for k, v in m.items()}


_NC_CACHE = {}


def _run(inputs, stage=99):
    inp = {k: np.asarray(v) for k, v in inputs.items()}
    if stage not in _NC_CACHE:
        _NC_CACHE[stage] = build_program(stage)
    nc = _NC_CACHE[stage]
    in_maps = [_core_inputs(inp, c) for c in range(8)]
    res = run_bass_kernel_spmd(nc, in_maps, core_ids=list(range(8)))
    return res.results


def kernel(**inputs):
    r = _run(inputs, int(os.environ.get("KSTAGE", "99")))
    f32 = np.float32
    yp = np.zeros((4, 4096, D), f32)
    ys = np.zeros((128, 8, D), f32)
    for c in range(8):
        b, half = c // 2, c % 2
        yc = r[c]["y"]
        ys[16 * c:16 * c + 16] = yc[0:128].reshape(16, 8, D)
        yp[b, half * 2048:(half + 1) * 2048] = yc[128:]
    kp = np.zeros((1, 4, 128, 2, 64), f32)
    vp = np.zeros((1, 4, 128, 2, 64), f32)
    sp = np.zeros((1, 4, 4, 128, 128), f32)
    kd = np.zeros((1, 128, 128, 2, 64), f32)
    vd = np.zeros((1, 128, 128, 2, 64), f32)
    sd = np.zeros((1, 128, 4, 128, 128), f32)
    for c in range(8):
        if c % 2 == 1:
            sp[0, c // 2] = r[c]["state_p"].transpose(1, 0, 2)
            kp[0, c // 2] = r[c]["kwin_p"].reshape(128, 2, 64)
            vp[0, c // 2] = r[c]["vwin_p"].reshape(128, 2, 64)
        sd[0, 16 * c:16 * c + 16] = r[c]["state_s"]
        kd[0, 16 * c:16 * c + 16] = r[c]["kwin_s"].reshape(16, 128, 2, 64)
        vd[0, 16 * c:16 * c + 16] = r[c]["vwin_s"].reshape(16, 128, 2, 64)
    return (yp, ys, kp, vp, sp, kd, vd, sd)
```

```python
import numpy as np
from contextlib import ExitStack
import concourse.bass as bass
import concourse.mybir as mybir
from concourse.bass_utils import run_bass_kernel_spmd

F32 = mybir.dt.float32
BF16 = mybir.dt.bfloat16
AF = mybir.ActivationFunctionType
ALU = mybir.AluOpType

D = 1024
DFF = 2816
NF = DFF // 128
INW = 2816
NG = 18
EPS = 1e-6
NEG = -30000.0
SCALE = 64 ** -0.5
ENGS = ("pe", "act", "dve", "pool", "sp")
import os
PN_LEVEL = int(os.environ.get("PN_LEVEL", "4"))
MIXLVL = int(os.environ.get("MIXLVL", "5"))
SUB = int(os.environ.get("SUB", "9"))
USE_CC = bool(os.environ.get("USE_CC"))


class V:
    __slots__ = ("ap", "name", "lo", "hi")

    def __init__(self, ap, name, lo, hi):
        self.ap, self.name, self.lo, self.hi = ap, name, lo, hi

    def with_ap(self, ap):
        return V(ap, self.name, self.lo, self.hi)


class TT:
    def __init__(self, handle, name, shape, esz=1, base=0):
        self.h, self.name, self.shape = handle, name, list(shape)
        self.base = base
        st = [1] * len(shape)
        for i in range(len(shape) - 2, 0, -1):
            st[i] = st[i + 1] * shape[i + 1]
        self.st = st
        self.esz = esz

    def __getitem__(self, idx):
        if not isinstance(idx, tuple):
            idx = (idx,)
        idx = idx + (slice(None),) * (len(self.shape) - len(idx))
        lo, hi = 0, 0
        for i in range(1, len(self.shape)):
            s = idx[i]
            if isinstance(s, int):
                a, b = s, s + 1
            else:
                a = 0 if s.start is None else s.start
                b = self.shape[i] if s.stop is None else s.stop
            lo += a * self.st[i]
            hi += (b - 1) * self.st[i]
        if self.name.startswith("BK"):
            return V(self.h[idx], self.name, 0, 512)
        return V(self.h[idx], self.name, self.base + lo * self.esz, self.base + (hi + 1) * self.esz)


class Prog:
    def __init__(self):
        self.q = {e: [] for e in ENGS}
        self.cnt = {e: 0 for e in ENGS}
        self.acc = {}
        self.dma_cnt = {}
        self.dma_hist = {}

    def _deps(self, reads, writes, tok):
        deps = set()
        for v in reads:
            recs = self.acc.setdefault(v.name, [])
            for (lo, hi, kind, t) in recs:
                if kind == "w" and lo < v.hi and v.lo < hi:
                    deps.add(t)
        for v in writes:
            recs = self.acc.setdefault(v.name, [])
            for (lo, hi, kind, t) in recs:
                if lo < v.hi and v.lo < hi:
                    deps.add(t)
        for v in reads:
            recs = self.acc[v.name]
            recs[:] = [r for r in recs if not (r[2] == "r" and r[0] == v.lo and r[1] == v.hi
                                               and r[3][0] == tok[0])]
            recs.append((v.lo, v.hi, "r", tok))
        for v in writes:
            recs = self.acc[v.name]
            recs[:] = [r for r in recs if not (v.lo <= r[0] and r[1] <= v.hi)]
            recs.append((v.lo, v.hi, "w", tok))
        deps.discard(tok)
        return deps

    def op(self, eng, fn, reads=(), writes=(), sig=True):
        if sig:
            self.cnt[eng] += 1
            tok = (eng, self.cnt[eng])
        else:
            tok = (eng, self.cnt[eng] + 1)
        deps = self._deps(reads, writes, tok)
        self.q[eng].append((deps, fn, eng if sig else None, 1))
        return tok

    def dma(self, eng, out, in_, slot, extra_reads=(), extra_writes=()):
        key = "dma:" + slot
        self.dma_cnt[key] = self.dma_cnt.get(key, 0) + 16
        tok = (key, self.dma_cnt[key])
        deps = self._deps([in_] + list(extra_reads), [out] + list(extra_writes), tok)
        if slot not in ("kvout", "yout", "stout") and self.dma_cnt[key] > 16:
            deps.add((key, self.dma_cnt[key] - 16))
        hist = self.dma_hist.setdefault(eng, [])
        if len(hist) >= 6:
            deps.add(hist[-6])
        if slot not in ("kvout", "yout"):
            hist.append(tok)
        o, i = out.ap, in_.ap
        self.q[eng].append((deps, lambda e: e.dma_start(out=o, in_=i), key, 16))
        return tok

    def check(self):
        val = {}
        pos = {e: 0 for e in ENGS}
        own = {e: 0 for e in ENGS}
        progress = True
        while progress:
            progress = False
            for eng in ENGS:
                while pos[eng] < len(self.q[eng]):
                    deps, fn, sigkey, inc = self.q[eng][pos[eng]]
                    ok = True
                    for (k, v) in deps:
                        if k == eng and (eng == "pe" or v > own[eng]):
                            continue
                        if val.get(k, 0) < v:
                            ok = False
                            break
                    if not ok:
                        break
                    if sigkey is not None:
                        val[sigkey] = val.get(sigkey, 0) + inc
                        if sigkey == eng:
                            own[eng] += 1
                    pos[eng] += 1
                    progress = True
        stuck = {e: (pos[e], len(self.q[e])) for e in ENGS if pos[e] < len(self.q[e])}
        if stuck:
            msg = []
            for e in stuck:
                deps = self.q[e][pos[e]][0]
                msg.append("%s@%d waits %s" % (e, pos[e], [(k, v, val.get(k, 0)) for (k, v) in deps if val.get(k, 0) < v]))
            raise RuntimeError("DEADLOCK: " + "; ".join(msg))
        for (k, v) in self.final:
            assert val.get(k, 0) == v, (k, v, val.get(k, 0))

    def replay(self, eng, e, sems):
        seen = {}
        own = 0
        for (deps, fn, sigkey, inc) in self.q[eng]:
            for (k, val) in sorted(deps):
                if k == eng and (eng == "pe" or val > own):
                    continue
                if seen.get(k, 0) >= val:
                    continue
                e.wait_ge(sems[k], val)
                seen[k] = val
            ins = fn(e)
            if sigkey is not None:
                ins.then_inc(sems[sigkey], inc)
                if sigkey == eng:
                    own += 1
        if eng == "sp":
            for (k, val) in sorted(self.final):
                if seen.get(k, 0) < val:
                    e.wait_ge(sems[k], val)


def build_program(stage=99):
    nc = bass.Bass("TRN2", target_bir_lowering=False)
    P = Prog()
    es = ExitStack()

    def din(name, shape, dt=F32):
        return TT(nc.dram_tensor(name, list(shape), dt, kind="ExternalInput").ap(), name, [1, 1])

    def dout(name, shape, dt=F32):
        return TT(nc.dram_tensor(name, list(shape), dt, kind="ExternalOutput").ap(), name, [1, 1])

    def dram_v(t, ap=None, sub=""):
        return V(t.h if ap is None else ap, t.name + sub, 0, 1)

    def sb(name, shape, dt=F32):
        h = es.enter_context(nc.sbuf_tensor(name, list(shape), dt))
        return TT(h, name, shape)

    def ps(name, shape, dt=F32):
        h = es.enter_context(nc.psum_tensor(name, list(shape), dt))
        return TT(h, name, shape)

    xin = din("xin", [NG * 128, D])
    w_gu = [din("w_gu1", [D, 2 * DFF]), din("w_gu2", [D, 2 * DFF])]
    w_dn = [din("w_dn1", [DFF, D]), din("w_dn2", [DFF, D])]
    w_in = din("w_in", [D, INW])
    w_out = din("w_out", [D, D])
    gcols = din("gcols", [128, 3, 8])
    gpost = din("gpost", [3, 128, D])
    ident_in = din("ident", [128, 128])
    y = dout("y", [17 * 128, D])
    bkv_in = din("bkv", [128, 256])
    cache_k = din("cache_k", [16, 128, 128])
    cache_v = din("cache_v", [16, 128, 128])
    kwin_p = dout("kwin_p", [128, 128])
    vwin_p = dout("vwin_p", [128, 128])
    kwin_s = dout("kwin_s", [16, 128, 128])
    vwin_s = dout("vwin_s", [16, 128, 128])
    binF_in = din("binF", [128, 22])
    bhi_in = din("bhi", [128, 512])
    bhg_in = din("bhg", [128, 512])
    lbl_in = din("lbl", [128, 2, 4])
    rmask_in = din("rmask", [128, 512])
    hmask_in = din("hmask", [128, 128])
    sel_in = din("sel", [128, 8])
    xprev = din("xprev", [2048, D])
    pflag_in = din("pflag", [128, 1])
    state_p = dout("state_p", [128, 4, 128])
    maskP_in = din("maskP", [128, 2, 128])
    maskH_in = din("maskH", [128, 128])
    zm_in = din("zmask", [128, 248])
    nm_in = din("newmask", [128, 128])
    hmS_in = din("hmaskS", [128, 128])
    rmS_in = din("rmaskS", [128, 128])
    seqsel_in = din("seqsel", [128, 16])
    st_in = din("state_s_in", [16, 4, 128, 128])
    state_s = dout("state_s", [16, 4, 128, 128])
    sink_in = din("sinks", [128, 8])
    gattn_in = din("gattn", [128, 512])
    ghg_in = din("ghg4", [128, 512])
    bk2_in = din("bk2", [128, 2])
    cpar_in = din("cpar", [128, 2, 128])
    sc_gu = [TT(nc.dram_tensor("sc_gu%d" % i, [NF, 128, 2048], BF16).ap(), "sc_gu%d" % i, [1, 1]) for i in range(2)]
    sc_dn = [TT(nc.dram_tensor("sc_dn%d" % i, [NF, 128, 1024], BF16).ap(), "sc_dn%d" % i, [1, 1]) for i in range(2)]
    sc_in = TT(nc.dram_tensor("sc_in", [11, 128, 2048], BF16).ap(), "sc_in", [1, 1])
    sc_out = TT(nc.dram_tensor("sc_out", [8, 128, 1024], BF16).ap(), "sc_out", [1, 1])
    cached = set()
    cc_in = TT(nc.dram_tensor("cc_in", [128, 512], F32).ap(), "cc_in", [1, 1])
    cc_out = TT(nc.dram_tensor("cc_out", [8 * 128, 512], F32).ap(), "cc_out", [1, 1])

    X = sb("X", [128, NG, D])
    xn_tok = sb("xn_tok", [128, 2, D], BF16)
    xnT = sb("xnT", [128, 8, 512], BF16)
    SCR = sb("SCR", [128, 22 * 512], BF16)
    WGU = sb("WGU", [128, 3, 8, 256], BF16)
    WDR = sb("WDR", [128, 4, 1024], BF16)
    SCR2 = sb("SCR2", [128, 6144], BF16)

    def carve(base, off_b, shape, dt):
        esz = 2 if dt == F32 else 1
        n = 1
        for d_ in shape[1:]:
            n *= d_
        ap = base.h[:, off_b // 2: off_b // 2 + n * esz]
        if dt == F32:
            ap = ap.bitcast(F32)
        if len(shape) == 3:
            ap = ap.rearrange("p (a b) -> p a b", a=shape[1])
        elif len(shape) == 4:
            ap = ap.rearrange("p (a b c) -> p a b c", a=shape[1], b=shape[2])
        return TT(ap, base.name, shape, esz=esz, base=off_b // 2)

    SG = carve(SCR2, 0, [128, 2, 512], F32)
    YB = carve(SCR2, 4096, [128, 2, D], F32)
    GP = sb("GP", [128, D])
    gc = sb("gc", [128, 3, 8])
    identf = sb("identf", [128, 128])
    identb = sb("identb", [128, 128], BF16)
    junk = sb("junk", [128, D], BF16)
    st = sb("st", [128, 64])

    KV = sb("KV", [128, 2, 256])
    HT = carve(SCR2, 0, [128, 6, 512], F32)
    binF = sb("binF_sb", [128, 22])
    bhi = sb("bhi_sb", [128, 512])
    bhg = sb("bhg_sb", [128, 512])
    lbl = sb("lbl_sb", [128, 2, 4])
    lbt = sb("lbt", [128, 3, 4])
    rmask = sb("rmask_sb", [128, 512])
    hmask = sb("hmask_sb", [128, 128])
    sel = sb("sel_sb", [128, 8])
    pflag = sb("pflag_sb", [128, 1])
    S = sb("S", [128, 4, 128])
    Sbf = sb("Sbf", [128, 3, 4, 128], BF16)
    DCH = sb("DCH", [128, 4, 16])
    hi_tok = sb("hi_tok", [128, 4, 512], BF16)
    qtT0 = carve(SCR, 4096, [128, 4, 512], BF16)
    ktT = carve(SCR, 8192, [128, 4, 512], BF16)
    khT = carve(SCR, 12288, [128, 4, 512], BF16)
    qtT1 = carve(SCR, 16384, [128, 4, 512], BF16)
    khat_tok = sb("khat_tok", [128, 4, 4, 128], BF16)
    WK2 = sb("WK2", [128, 8, 2, 128], BF16)
    cpar = sb("cpar_sb", [128, 2, 128])
    esink_raw = sb("esink_raw", [128, 8])
    G8 = carve(SCR, 0, [128, 8, 512], F32)
    qT = carve(SCR, 0, [128, 4, 512], BF16)
    kT2 = sb("kT2", [128, 2, 640], BF16)
    V1 = sb("V1", [128, 5, 2, 66], BF16)
    PT = sb("PT", [128, 2, 512], BF16)
    AT = sb("AT", [128, 4, 128], BF16)
    OA = sb("OA", [128, 4, 66])
    a_tok = sb("a_tok", [128, 512])
    o_tok = sb("o_tok", [128, 512])
    gate = sb("gate", [128, 512])
    mix_tok = sb("mix_tok", [128, 1024], BF16)
    maskP = sb("maskP_sb", [128, 2, 128], BF16)
    maskH = sb("maskH_sb", [128, 128], BF16)
    Zm = sb("Zm", [128, 248], BF16)
    newmask = sb("newmask_sb", [128, 128], BF16)
    hmaskS = sb("hmaskS_sb", [128, 128])
    rmaskS = sb("rmaskS_sb", [128, 128])
    seqsel = sb("seqsel_sb", [128, 16])
    Kd = sb("Kd", [128, 3, 128], BF16)
    Vb = sb("Vb", [128, 3, 66], BF16)
    kcT = sb("kcT", [128, 2, 128], BF16)
    vmk = sb("vmk", [128, 4, 128], BF16)
    WDRf = TT(WDR.h[:].rearrange("p a b -> p (a b)"), "WDR", [128, 4096])
    Sb32 = carve(WDRf, 0, [128, 16, 128], F32)
    Sbbf = carve(SCR, 16384, [128, 16, 128], BF16)
    qexp = carve(SCR2, 4096, [128, 17 * 128], BF16)
    esink = sb("esink", [128, 8])
    gattn = sb("gattn_sb", [128, 512])
    ghg4 = sb("ghg4_sb", [128, 512])
    bk2 = sb("bk2_sb", [128, 2])
    bkv = sb("bkv_sb", [128, 256])
    BK = [ps("BK%d" % i, [128, 512]) for i in range(8)]
    bk7b = BK[7].h[:].bitcast(BF16)

    def TPB(lo, hi):
        return V(bk7b[:, lo:hi], "BK7", 0, 512)

    stc = [0]

    def stcol(n=1):
        c = stc[0]
        if c + n > 64:
            c = 0
        stc[0] = c + n
        return st[:, c:c + n]

    epst = sb("epst", [128, 2])
    epsb = {1.0: epst[:, 0:1], 0.5: epst[:, 1:2]}
    P.op("dve", lambda e: e.memset(epst[:, 0:1].ap, EPS), [], [epst[:, 0:1]])
    P.op("dve", lambda e: e.memset(epst[:, 1:2].ap, EPS / 0.25), [], [epst[:, 1:2]])
    P.dma("sp", gc[:], dram_v(gcols), "c0")
    P.dma("sp", identf[:], dram_v(ident_in), "c1")
    P.op("dve", lambda e: e.tensor_copy(out=identb[:].ap, in_=identf[:].ap), [identf[:]], [identb[:]])
    for g in (range(NG) if USE_CC else (0, 1)):
        P.dma("sp", X[:, g, :], dram_v(xin, xin.h[g * 128:(g + 1) * 128, :]), "x%d" % g)

    def rstd_of(ss, n, hs, w=1):
        r1 = stcol(w)
        P.op("act", lambda e, o=r1.ap, i=ss.ap: e.activation(out=o, in_=i, func=AF.Sqrt, scale=1.0 / (n * hs * hs), bias=epsb[hs].ap),
             [ss, epsb[hs]], [r1])
        r2 = stcol(w)
        P.op("dve", lambda e, o=r2.ap, i=r1.ap: e.reciprocal(out=o, in_=i), [r1], [r2])
        return r2

    def rms_to_T(groups, gidx, dstT):
        for j, g in enumerate(groups):
            ss = stcol()
            xg = X[:, g, :]
            P.op("act", lambda e, o=junk[:].ap, i=xg.ap, a=ss.ap: e.activation(out=o, in_=i, func=AF.Square, accum_out=a),
                 [xg], [junk[:], ss])
            r2 = rstd_of(ss, D, 1.0)
            xs = xn_tok[:, j % 2, :]
            P.op("dve", lambda e, o=xs.ap, i=xg.ap, s=r2.ap: e.tensor_scalar(out=o, in0=i, scalar1=s, scalar2=None, op0=ALU.mult),
                 [xg, r2], [xs])
            for c in range(8):
                src = xn_tok[:, j % 2, c * 128:(c + 1) * 128]
                dst = TPB(c * 128, (c + 1) * 128)
                P.op("pe", lambda e, o=dst.ap, i=src.ap: e.transpose(out=o, in_=i, identity=identb[:].ap),
                     [src, identb[:]], [dst], sig=(c == 7))
            tp = TPB(0, 1024)
            tp3 = tp.with_ap(bk7b[:, 0:1024].rearrange("p (c t) -> p c t", c=8))
            d3 = dstT[:, :, j * 128:(j + 1) * 128]
            gb = gc[:, gidx, :]
            gb3 = gb.ap.unsqueeze(2).to_broadcast([128, 8, 128])
            P.op("dve", lambda e, o=d3.ap, i=tp3.ap, g_=gb3: e.tensor_tensor(out=o, in0=i, in1=g_, op=ALU.mult),
                 [tp, gb], [d3])

    def post_norm_add(g, banks, gp_idx, half_scale, slot):
        ss2 = stcol(2)
        yb = YB[:, slot, :]
        for h in range(2):
            bk = banks[h][:]
            ssh = V(ss2.ap[:, h:h + 1], ss2.name, ss2.lo + h, ss2.lo + h + 1)
            P.op("act", lambda e, o=junk[:, 0:512].ap, i=bk.ap, a=ssh.ap: e.activation(out=o, in_=i, func=AF.Square, accum_out=a),
                 [bk], [junk[:, 0:512], ssh])
            if PN_LEVEL < 2:
                continue
            ybh = YB[:, slot, h * 512:(h + 1) * 512]
            gph = GP[:, h * 512:(h + 1) * 512]
            P.op("act", lambda e, o=ybh.ap, i=bk.ap: e.activation(out=o, in_=i, func=AF.Copy), [bk], [ybh])
            P.op("dve", lambda e, o=ybh.ap, g_=gph.ap: e.tensor_tensor(out=o, in0=o, in1=g_, op=ALU.mult),
                 [ybh, gph], [ybh])
        if PN_LEVEL < 3:
            return
        s1 = stcol()
        P.op("dve", lambda e, o=s1.ap, a=ss2.ap: e.tensor_scalar(out=o, in0=a[:, 0:1], scalar1=a[:, 1:2], scalar2=None, op0=ALU.add), [ss2], [s1])
        r2 = rstd_of(s1, D, half_scale)
        if PN_LEVEL < 4:
            return
        xg = X[:, g, :]
        P.op("dve", lambda e, o=xg.ap, i=yb.ap, s=r2.ap: e.scalar_tensor_tensor(out=o, in0=i, scalar=s, in1=o, op0=ALU.mult, op1=ALU.add),
             [yb, r2, xg], [xg])

    wgu_i = [0]
    wdr_i = [0]
    ring_i = [0]
    bg_queue = []
    bg_n = [0]

    def bg_drain(n=1):
        for _ in range(n):
            if not bg_queue:
                return
            dst_v, src_v = bg_queue.pop(0)
            P.dma("pool", dst_v, src_v, "bg%d" % (bg_n[0] % 8))
            bg_n[0] += 1

    prenormed = [None]

    def prenorm(groups, gidx):
        key = (tuple(groups), gidx)
        if prenormed[0] != key:
            rms_to_T(groups, gidx, xnT)
        prenormed[0] = None

    def prenorm_early(groups, gidx):
        rms_to_T(groups, gidx, xnT)
        prenormed[0] = (tuple(groups), gidx)

    def ffn(groups, which, parts=3, hook=None):
        n = len(groups)
        N = 128 * n
        prenorm(groups, 0 if which == 0 else 2)
        P.dma("sp", GP[:], dram_v(gpost, gpost.h[0 if which == 0 else 2]), "gp")
        wg = w_gu[which].h.rearrange("(c p) n -> p c n", p=128)
        for f in range(NF):
            s = wgu_i[0] % 3
            wgu_i[0] += 1
            slot2d = V(WGU.h[:, s, :, :].rearrange("p c n -> p (c n)"), "WGU", WGU[:, s, :, :].lo, WGU[:, s, :, :].hi)
            ck = ("gu", which, f)
            if ck in cached:
                P.dma("sp", slot2d, dram_v(sc_gu[which], sc_gu[which].h[f], "_%d" % f), "hgu%d" % s)
            else:
                for hh in range(2):
                    P.dma("pool", WGU[:, s, :, hh * 128:(hh + 1) * 128],
                          dram_v(w_gu[which], wg[:, :, hh * DFF + f * 128: hh * DFF + (f + 1) * 128]), "wgu%d_%d" % (s, hh))
                P.dma("sp", dram_v(sc_gu[which], sc_gu[which].h[f], "_%d" % f), slot2d, "wbg%d" % s)
                cached.add(ck)
            bks = []
            for hh in range(2):
                b = BK[ring_i[0] % 3]
                ring_i[0] += 1
                bks.append(b)
                for c in range(8):
                    lw = WGU[:, s, c, hh * 128:(hh + 1) * 128]
                    rx = xnT[:, c, 0:N]
                    P.op("pe", lambda e, o=b[:, 0:N].ap, l=lw.ap, r=rx.ap, c=c: e.matmul(o, lhsT=l, rhs=r, start=(c == 0), stop=(c == 7)),
                         [lw, rx], [b[:, 0:N]], sig=(c == 7))
            bg_drain(1)
            sg = SG[:, f % 2, 0:N]
            P.op("act", lambda e, o=sg.ap, i=bks[0][:, 0:N].ap: e.activation(out=o, in_=i, func=AF.Silu),
                 [bks[0][:, 0:N]], [sg])
            at = V(SCR.h[:, f * 512: f * 512 + N], "SCR", f * 512, f * 512 + N)
            P.op("dve", lambda e, o=at.ap, a=sg.ap, b_=bks[1][:, 0:N].ap: e.tensor_tensor(out=o, in0=a, in1=b_, op=ALU.mult),
                 [sg, bks[1][:, 0:N]], [at])
        if parts < 2:
            return
        wd = w_dn[which].h.rearrange("(f p) n -> p f n", p=128)
        for f in range(NF):
            s = wdr_i[0] % 4
            wdr_i[0] += 1
            ck = ("dn", which, f)
            if ck in cached:
                P.dma("sp", WDR[:, s, :], dram_v(sc_dn[which], sc_dn[which].h[f], "_%d" % f), "hdr%d" % s)
            else:
                P.dma("pool", WDR[:, s, :], dram_v(w_dn[which], wd[:, f, :]), "wdr%d" % s)
                P.dma("sp", dram_v(sc_dn[which], sc_dn[which].h[f], "_%d" % f), WDR[:, s, :], "wbd%d" % s)
                cached.add(ck)
            for j in range(n):
                for h in range(2):
                    b = BK[j * 2 + h]
                    la = V(SCR.h[:, f * 512 + j * 128: f * 512 + (j + 1) * 128], "SCR", f * 512 + j * 128, f * 512 + (j + 1) * 128)
                    rw = WDR[:, s, h * 512:(h + 1) * 512]
                    P.op("pe", lambda e, o=b[:].ap, l=la.ap, r=rw.ap, f=f: e.matmul(o, lhsT=l, rhs=r, start=(f == 0), stop=(f == NF - 1)),
                         [la, rw], [b[:]], sig=(f == NF - 1 or (j == n - 1 and h == 1)))
        if parts < 3:
            return
        order = list(range(n))
        if hook is not None and n == 4:
            j = 3
            post_norm_add(groups[j], [BK[j * 2], BK[j * 2 + 1]], 0 if which == 0 else 2, 0.5, j % 2)
            order = [0, 1, 2]
        if hook is not None:
            hook()
        for j in order:
            post_norm_add(groups[j], [BK[j * 2], BK[j * 2 + 1]], 0 if which == 0 else 2, 0.5, j % 2)

    kv_i = [0]

    def kv_proj(groups):
        rms_to_T(groups, 1, xnT)
        s_ = wgu_i[0] % 3
        wgu_i[0] += 1
        wi = w_in.h.rearrange("(c p) n -> p c n", p=128)
        for hh in range(2):
            P.dma("pool", WGU[:, s_, :, hh * 128:(hh + 1) * 128],
                  dram_v(w_in, wi[:, :, 512 + hh * 128: 512 + (hh + 1) * 128]), "wgu%d_%d" % (s_, hh))
        for j, g in enumerate(groups):
            if g not in (1, NG - 1):
                continue
            b = BK[ring_i[0] % 3]
            ring_i[0] += 1
            for c in range(8):
                la = xnT[:, c, j * 128:(j + 1) * 128]
                rw = WGU[:, s_, c, :]
                P.op("pe", lambda e, o=b[:, 0:256].ap, l=la.ap, r=rw.ap, c=c: e.matmul(o, lhsT=l, rhs=r, start=(c == 0), stop=(c == 7)),
                     [la, rw], [b[:, 0:256]], sig=(c == 7))
            sl = kv_i[0] % 2
            kv_i[0] += 1
            kvs = KV[:, sl, :]
            P.op("act", lambda e, o=kvs.ap, i=b[:, 0:256].ap: e.activation(out=o, in_=i, func=AF.Copy), [b[:, 0:256]], [kvs])
            P.op("dve", lambda e, o=kvs.ap, b_=bkv[:].ap: e.tensor_tensor(out=o, in0=o, in1=b_, op=ALU.add), [kvs, bkv[:]], [kvs])
            if g == NG - 1:
                P.dma("sp", dram_v(kwin_p), KV[:, sl, 0:128], "kvout")
                P.dma("sp", dram_v(vwin_p), KV[:, sl, 128:256], "kvout")
            else:
                for bq in range(16):
                    P.dma("sp", dram_v(kwin_s, kwin_s.h[bq, 120:128, :], "_%d" % bq),
                          V(KV.h[bq * 8:(bq + 1) * 8, sl, 0:128], "KV", sl * 256, sl * 256 + 128), "kvout")
                    P.dma("sp", dram_v(vwin_s, vwin_s.h[bq, 120:128, :], "_%d" % bq),
                          V(KV.h[bq * 8:(bq + 1) * 8, sl, 128:256], "KV", sl * 256 + 128, sl * 256 + 256), "kvout")

    bank_i = [0]

    reserved = set()

    def nb():
        while True:
            i = bank_i[0] % 7
            bank_i[0] += 1
            if i not in reserved:
                return BK[i]

    wi_ap = w_in.h.rearrange("(c p) n -> p c n", p=128)

    def win_block(blk):
        s_ = wgu_i[0] % 3
        wgu_i[0] += 1
        slot2d = V(WGU.h[:, s_, :, :].rearrange("p c n -> p (c n)"), "WGU", WGU[:, s_, :, :].lo, WGU[:, s_, :, :].hi)
        ck = ("in", blk)
        if ck in cached:
            P.dma("sp", slot2d, dram_v(sc_in, sc_in.h[blk], "_%d" % blk), "hgu%d" % s_)
        else:
            for hh in range(2):
                P.dma("pool", WGU[:, s_, :, hh * 128:(hh + 1) * 128],
                      dram_v(w_in, wi_ap[:, :, blk * 256 + hh * 128: blk * 256 + (hh + 1) * 128]), "wgu%d_%d" % (s_, hh))
            P.dma("sp", dram_v(sc_in, sc_in.h[blk], "_%d" % blk), slot2d, "wbg%d" % s_)
            cached.add(ck)
        return s_

    def tm_proj(s_, j, bank, col0):
        for c in range(8):
            la = xnT[:, c, j * 128:(j + 1) * 128]
            rw = WGU[:, s_, c, :]
            P.op("pe", lambda e, o=bank[:, col0:col0 + 256].ap, l=la.ap, r=rw.ap, c=c: e.matmul(o, lhsT=l, rhs=r, start=(c == 0), stop=(c == 7)),
                 [la, rw], [bank[:, col0:col0 + 256]], sig=(c == 7))

    def A(eng, fn, reads, writes):
        P.op(eng, fn, reads, writes)

    def fm_proj(s_, half, N, bank, col0=0):
        for c in range(8):
            lw = WGU[:, s_, c, half * 128:(half + 1) * 128]
            rx = xnT[:, c, col0:col0 + N]
            P.op("pe", lambda e, o=bank[:, 0:N].ap, l=lw.ap, r=rx.ap, c=c: e.matmul(o, lhsT=l, rhs=r, start=(c == 0), stop=(c == 7)),
                 [lw, rx], [bank[:, 0:N]], sig=(c == 7))

    def hgrn_head_feat(h, N, ng, s_hf, s_hq, phase, rm, col0=0, csz=64):
        nch = N // csz
        bf = nb()
        fm_proj(s_hf, h % 2, N, bf, col0)
        a_, b_, c_, d_, e_, f_ = [HT[:, i, 0:N] for i in range(6)]
        bcol = binF[:, 10 + h:11 + h]
        A("act", lambda e, o=a_.ap, i=bf[:, 0:N].ap, b=bcol.ap: e.activation(out=o, in_=i, func=AF.Sigmoid, bias=b),
          [bf[:, 0:N], bcol], [a_])
        om, lb_ = lbt[:, 2, h:h + 1], lbt[:, 1, h:h + 1]
        A("dve", lambda e, o=a_.ap, s1=om.ap, s2=lb_.ap: e.tensor_scalar(out=o, in0=o, scalar1=s1, scalar2=s2, op0=ALU.mult, op1=ALU.add),
          [a_, om, lb_], [a_])
        A("act", lambda e, o=b_.ap, i=a_.ap: e.activation(out=o, in_=i, func=AF.Ln), [a_], [b_])
        A("dve", lambda e, o=d_.ap, i=a_.ap: e.tensor_scalar(out=o, in0=i, scalar1=-1.0, scalar2=1.0, op0=ALU.mult, op1=ALU.add),
          [a_], [d_])
        rmv = rm[:, 0:N]
        A("dve", lambda e, o=c_.ap, m=rmv.ap, i=b_.ap: e.tensor_tensor_scan(out=o, data0=m, data1=i, initial=0.0, op0=ALU.mult, op1=ALU.add),
          [rmv, b_], [c_])
        c_last = c_.ap.rearrange("p (c t) -> p c t", t=csz)[:, :, csz - 1]
        dch = DCH[:, h, 0:nch]
        A("act", lambda e, o=dch.ap, i=c_last: e.activation(out=o, in_=i, func=AF.Exp), [c_], [dch])
        A("act", lambda e, o=b_.ap, i=c_.ap: e.activation(out=o, in_=i, func=AF.Exp, scale=-1.0), [c_], [b_])
        A("dve", lambda e, o=d_.ap, i=b_.ap: e.tensor_tensor(out=o, in0=o, in1=i, op=ALU.mult), [d_, b_], [d_])
        if phase >= 2:
            kt = ktT[:, h, 0:N]
            A("act", lambda e, o=kt.ap, i=d_.ap: e.activation(out=o, in_=i, func=AF.Copy), [d_], [kt])
        kh = khT[:, h, 0:N]
        kh3 = kh.ap.rearrange("p (c t) -> p c t", t=csz)
        d3 = d_.ap.rearrange("p (c t) -> p c t", t=csz)
        dcb = dch.ap.unsqueeze(2).to_broadcast([128, nch, csz])
        A("dve", lambda e, o=kh3, i=d3, b=dcb: e.tensor_tensor(out=o, in0=i, in1=b, op=ALU.mult), [d_, dch], [kh])
        if phase >= 2:
            bq = nb()
            fm_proj(s_hq, h % 2, N, bq, col0)
            qcol = binF[:, 6 + h:7 + h]
            A("act", lambda e, o=f_.ap, i=bq[:, 0:N].ap, b=qcol.ap: e.activation(out=o, in_=i, func=AF.Silu, bias=b),
              [bq[:, 0:N], qcol], [f_])
            A("act", lambda e, o=e_.ap, i=c_.ap: e.activation(out=o, in_=i, func=AF.Exp), [c_], [e_])
            if phase == 2:
                A("dve", lambda e, o=f_.ap, b=e_.ap: e.tensor_tensor(out=o, in0=o, in1=b, op=ALU.mult), [f_, e_], [f_])
                f3 = f_.ap.rearrange("p (g t) -> p g t", t=128)
                for par, qq in ((0, qtT0), (1, qtT1)):
                    qt = qq[:, h, 0:N]
                    q3 = qt.ap.rearrange("p (g t) -> p g t", t=128)
                    cp = cpar[:, par, :]
                    cpb = cp.ap.unsqueeze(1).to_broadcast([128, ng, 128])
                    A("dve", lambda e, o=q3, i=f3, m=cpb: e.tensor_tensor(out=o, in0=i, in1=m, op=ALU.mult), [f_, cp], [qt])
            else:
                qt = qtT0[:, h, 0:N]
                A("dve", lambda e, o=qt.ap, i=f_.ap, b=e_.ap: e.tensor_tensor(out=o, in0=i, in1=b, op=ALU.mult), [f_, e_], [qt])
        for j in range(ng):
            src = khT[:, h, j * 128:(j + 1) * 128]
            dst = TPB(j * 128, (j + 1) * 128)
            P.op("pe", lambda e, o=dst.ap, i=src.ap: e.transpose(out=o, in_=i, identity=identb[:].ap),
                 [src, identb[:]], [dst], sig=(j == ng - 1))
        tp = TPB(0, ng * 128)
        tp3 = bk7b[:, 0:ng * 128].rearrange("p (g k) -> p g k", g=ng)
        kd = khat_tok[:, 0:ng, h, :]
        A("act", lambda e, o=kd.ap, i=tp3: e.activation(out=o, in_=i, func=AF.Copy), [tp], [kd])

    def hi_proj(ng, s7, s8, js=None):
        for j in (range(ng) if js is None else js):
            bh = nb()
            tm_proj(s7, j, bh, 0)
            tm_proj(s8, j, bh, 256)
            ht = hi_tok[:, j, :]
            A("dve", lambda e, o=ht.ap, b=bhi[:].ap, i=bh[:].ap: e.tensor_tensor(out=o, in0=b, in1=i, op=ALU.add),
              [bhi[:], bh[:]], [ht])

    sver = [0]

    def state_update(j, ci, phase):
        pb = nb()
        for h in range(4):
            la = khat_tok[ci * 64:(ci + 1) * 64, j, h, :]
            rv = hi_tok[ci * 64:(ci + 1) * 64, j, h * 128:(h + 1) * 128]
            ob = pb[:, h * 128:(h + 1) * 128]
            P.op("pe", lambda e, o=ob.ap, l=la.ap, r=rv.ap: e.matmul(o, lhsT=l, rhs=r, start=True, stop=True),
                 [la, rv], [ob], sig=(h == 3))
        t32 = HT[:, 0, :]
        A("act", lambda e, o=t32.ap, i=pb[:].ap: e.activation(out=o, in_=i, func=AF.Copy), [pb[:]], [t32])
        ch = j * 2 + ci
        dsel = DCH[:, :, ch:ch + 1]
        db = dsel.ap.to_broadcast([128, 4, 128])
        A("dve", lambda e, o=S[:].ap, b=db: e.tensor_tensor(out=o, in0=o, in1=b, op=ALU.mult), [S[:], dsel], [S[:]])
        t3 = t32.ap.rearrange("p (h v) -> p h v", h=4)
        A("dve", lambda e, o=S[:].ap, i=t3: e.tensor_tensor(out=o, in0=o, in1=i, op=ALU.add), [S[:], t32], [S[:]])
        if phase == 2:
            sver[0] += 1
            sb_ = Sbf[:, sver[0] % 3, :, :]
            A("act", lambda e, o=sb_.ap, i=S[:].ap: e.activation(out=o, in_=i, func=AF.Copy), [S[:]], [sb_])

    def hgrn_scan_tile(tl):
        ng = len(tl)
        N = 128 * ng
        rms_to_T(tl, 1, xnT)
        s7 = win_block(7)
        s8 = win_block(8)
        hi_proj(ng, s7, s8)
        s5 = win_block(5)
        hgrn_head_feat(0, N, ng, s5, None, 1, rmask)
        hgrn_head_feat(1, N, ng, s5, None, 1, rmask)
        s6 = win_block(6)
        hgrn_head_feat(2, N, ng, s6, None, 1, rmask)
        hgrn_head_feat(3, N, ng, s6, None, 1, rmask)
        for j in range(ng):
            for ci in range(2):
                state_update(j, ci, 1)

    def exchange():
        if os.environ.get("NOEXCH"):
            return
        P.dma("sp", dram_v(cc_in), S[:], "ccio")
        P.cnt_cc = getattr(P, "cnt_cc", 0) + 1
        tok = ("cc", P.cnt_cc)
        deps = P._deps([dram_v(cc_in)], [dram_v(cc_out)], tok)
        for k_, v_ in P.dma_cnt.items():
            deps.add((k_, v_))
        P.q["pool"].append((deps, lambda e: e.collective_compute("AllGather", ALU.bypass, replica_groups=[list(range(8))],
                                                                   ins=[cc_in.h], outs=[cc_out.h]), "cc", 1))
        P.dma("sp", G8[:], dram_v(cc_out, cc_out.h.rearrange("(r p) f -> p r f", p=128)), "ccio")
        s2 = S[:].ap.rearrange("p h v -> p (h v)")
        g0 = G8[:, 0, :]
        A("dve", lambda e, o=s2, i=g0.ap, c=sel[:, 0:1].ap: e.tensor_scalar(out=o, in0=i, scalar1=c, scalar2=None, op0=ALU.mult),
          [g0, sel[:]], [S[:]])
        for r_ in range(1, 8):
            gr = G8[:, r_, :]
            A("dve", lambda e, o=gr.ap, c=sel[:, r_:r_ + 1].ap: e.tensor_scalar(out=o, in0=o, scalar1=c, scalar2=None, op0=ALU.mult),
              [gr, sel[:]], [gr])
            A("dve", lambda e, o=s2, i=gr.ap: e.tensor_tensor(out=o, in0=o, in1=i, op=ALU.add), [S[:], gr], [S[:]])

    def kv_part(tl, ng, N):
        if SUB < 2:
            return
        s2 = win_block(2)
        for kv in range(2):
            for dup in range(2):
                srcw = WGU[:, s2, :, kv * 64:(kv + 1) * 64]
                dstw = WK2[:, :, kv, dup * 64:(dup + 1) * 64]
                A("dve", lambda e, o=dstw.ap, i=srcw.ap: e.tensor_copy(out=o, in_=i), [srcw], [dstw])
        for kv in range(2):
            bkk = nb()
            for c in range(8):
                lw = WK2[:, c, kv, :]
                rx = xnT[:, c, 0:N]
                ob = bkk[:, 0:N]
                P.op("pe", lambda e, o=ob.ap, l=lw.ap, r=rx.ap, c=c: e.matmul(o, lhsT=l, rhs=r, start=(c == 0), stop=(c == 7)),
                     [lw, rx], [ob], sig=(c == 7))
            kd = kT2[:, kv, 128:128 + N]
            bc = bk2[:, kv:kv + 1]
            A("act", lambda e, o=kd.ap, i=bkk[:, 0:N].ap, b=bc.ap: e.activation(out=o, in_=i, func=AF.Identity, bias=b),
              [bkk[:, 0:N], bc], [kd])
        if SUB < 3:
            return
        for j, g in enumerate(tl):
            bv = nb()
            tm_proj(s2, j, bv, 0)
            vd = V1[:, 1 + j, :, 0:64]
            if g not in (1, NG - 1):
                bvv = bkv[:, 128:256].ap.rearrange("p (k d) -> p k d", k=2)
                pv3 = bv[:, 128:256].ap.rearrange("p (k d) -> p k d", k=2)
                A("dve", lambda e, o=vd.ap, b=bvv, i=pv3: e.tensor_tensor(out=o, in0=b, in1=i, op=ALU.add),
                  [bkv[:], bv[:, 128:256]], [vd])
            else:
                sl = kv_i[0] % 2
                kv_i[0] += 1
                kvs = KV[:, sl, :]
                A("act", lambda e, o=kvs.ap, i=bv[:, 0:256].ap: e.activation(out=o, in_=i, func=AF.Copy), [bv[:, 0:256]], [kvs])
                A("dve", lambda e, o=kvs.ap, b_=bkv[:].ap: e.tensor_tensor(out=o, in0=o, in1=b_, op=ALU.add), [kvs, bkv[:]], [kvs])
                kv3 = KV[:, sl, 128:256]
                A("dve", lambda e, o=vd.ap, i=kv3.ap.rearrange("p (k d) -> p k d", k=2): e.tensor_copy(out=o, in_=i), [kv3], [vd])
                KVDMA = int(os.environ.get("KVDMA", "2"))
                if g == NG - 1:
                    if KVDMA >= 1:
                        P.dma("sp", dram_v(kwin_p), KV[:, sl, 0:128], "kvout")
                        P.dma("sp", dram_v(vwin_p), KV[:, sl, 128:256], "kvout")
                elif KVDMA >= 2:
                    for bq in range(16):
                        P.dma("sp", dram_v(kwin_s, kwin_s.h[bq, 120:128, :], "_%d" % bq),
                              V(KV.h[bq * 8:(bq + 1) * 8, sl, 0:128], "KV", sl * 256, sl * 256 + 128), "kvout")
                        P.dma("sp", dram_v(vwin_s, vwin_s.h[bq, 120:128, :], "_%d" % bq),
                              V(KV.h[bq * 8:(bq + 1) * 8, sl, 128:256], "KV", sl * 256 + 128, sl * 256 + 256), "kvout")

    def carry_prev(ng, N):
        if SUB < 5:
            return
        src = kT2[:, :, N:N + 128]
        dst = kT2[:, :, 0:128]
        A("dve", lambda e, o=dst.ap, i=src.ap: e.tensor_copy(out=o, in_=i), [src], [dst])
        sv, dv = V1[:, ng, :, :], V1[:, 0, :, :]
        A("dve", lambda e, o=dv.ap, i=sv.ap: e.tensor_copy(out=o, in_=i), [sv], [dv])

    def q_part(N):
        for blk in (0, 1):
            sq = win_block(blk)
            for half in range(2):
                cc = blk * 2 + half
                bq = nb()
                fm_proj(sq, half, N, bq)
                qd = qT[:, cc, 0:N]
                bc = binF[:, cc:cc + 1]
                A("act", lambda e, o=qd.ap, i=bq[:, 0:N].ap, b=bc.ap: e.activation(out=o, in_=i, func=AF.Identity, bias=b),
                  [bq[:, 0:N], bc], [qd])

    def attention_group(j, g):
        for kv in range(2):
            for blk in range(2):
                bs = nb()
                c0 = j * 128 + blk * 128
                for hl in range(4):
                    h = kv * 4 + hl
                    p0 = (h % 2) * 64
                    ob = bs[:, hl * 128:(hl + 1) * 128]
                    mk = maskH[:] if (g == 2 and blk == 0) else maskP[:, blk, :]
                    P.op("pe", lambda e, o=ob.ap, r=mk.ap: e.matmul(o, lhsT=identb[:].ap, rhs=r, start=True, stop=False),
                         [identb[:], mk], [ob], sig=False)
                    lk = kT2[p0:p0 + 64, kv, c0:c0 + 128]
                    rq = qT[p0:p0 + 64, h // 2, j * 128:(j + 1) * 128]
                    P.op("pe", lambda e, o=ob.ap, l=lk.ap, r=rq.ap: e.matmul(o, lhsT=l, rhs=r, start=False, stop=True),
                         [lk, rq], [ob], sig=(hl == 3))
                pt = PT[:, blk, :]
                A("act", lambda e, o=pt.ap, i=bs[:].ap: e.activation(out=o, in_=i, func=AF.Exp, scale=SCALE), [bs[:]], [pt])
            bov = nb()
            for hl in range(4):
                ob = bov[:, hl * 66:hl * 66 + 65]
                for blk in range(2):
                    lp = PT[:, blk, hl * 128:(hl + 1) * 128]
                    rv = V1[:, j + blk, kv, 0:65]
                    P.op("pe", lambda e, o=ob.ap, l=lp.ap, r=rv.ap, blk=blk: e.matmul(o, lhsT=l, rhs=r, start=(blk == 0), stop=(blk == 1)),
                         [lp, rv], [ob], sig=(hl == 3 and blk == 1))
            oa2 = OA[:].ap.rearrange("p h d -> p (h d)")
            A("act", lambda e, o=oa2, i=bov[:, 0:264].ap: e.activation(out=o, in_=i, func=AF.Copy), [bov[:, 0:264]], [OA[:]])
            attn_norm(kv)
        attn_finish()

    def attn_norm(kv):
        if True:
            dn = stcol(4)
            es4 = esink[:, kv * 4:(kv + 1) * 4]
            A("dve", lambda e, o=dn.ap, i=OA[:, :, 64].ap, b=es4.ap: e.tensor_tensor(out=o, in0=i, in1=b, op=ALU.add), [OA[:], es4], [dn])
            rd = stcol(4)
            A("dve", lambda e, o=rd.ap, i=dn.ap: e.reciprocal(out=o, in_=i), [dn], [rd])
            ad = a_tok[:, kv * 256:(kv + 1) * 256]
            ad3 = ad.ap.rearrange("p (h d) -> p h d", h=4)
            rb = rd.ap.unsqueeze(2).to_broadcast([128, 4, 64])
            A("dve", lambda e, o=ad3, i=OA[:, :, 0:64].ap, b=rb: e.tensor_tensor(out=o, in0=i, in1=b, op=ALU.mult), [OA[:], rd], [ad])

    def attn_finish():
        ssa = stcol()
        A("act", lambda e, o=junk[:, 0:512].ap, i=a_tok[:].ap, a=ssa.ap: e.activation(out=o, in_=i, func=AF.Square, accum_out=a),
          [a_tok[:]], [junk[:, 0:512], ssa])
        ra = rstd_of(ssa, 512, 1.0)
        A("dve", lambda e, o=a_tok[:].ap, s_=ra.ap: e.tensor_scalar(out=o, in0=o, scalar1=s_, scalar2=None, op0=ALU.mult), [a_tok[:], ra], [a_tok[:]])
        md = mix_tok[:, 0:512]
        A("dve", lambda e, o=md.ap, i=a_tok[:].ap, g_=gattn[:].ap: e.tensor_tensor(out=o, in0=i, in1=g_, op=ALU.mult), [a_tok[:], gattn[:]], [md])

    def gate_group(j, s9, s10):
        bg = nb()
        tm_proj(s9, j, bg, 0)
        tm_proj(s10, j, bg, 256)
        A("act", lambda e, o=gate[:].ap, i=bg[:].ap: e.activation(out=o, in_=i, func=AF.Copy), [bg[:]], [gate[:]])
        A("dve", lambda e, o=gate[:].ap, b=bhg[:].ap: e.tensor_tensor(out=o, in0=o, in1=b, op=ALU.add), [gate[:], bhg[:]], [gate[:]])
        A("act", lambda e, o=gate[:].ap: e.activation(out=o, in_=o, func=AF.Silu), [gate[:]], [gate[:]])

    def hgrn_out_group(j, s9, s10):
        gate_group(j, s9, s10)
        gs = slice(j * 128, (j + 1) * 128)
        for h in range(4):
            ba = nb()
            lk = ktT[:, h, gs]
            for par, qq in ((0, qtT0), (1, qtT1)):
                rq = qq[:, h, gs]
                P.op("pe", lambda e, o=ba[:, 0:128].ap, l=lk.ap, r=rq.ap, par=par: e.matmul(o, lhsT=l, rhs=r, start=(par == 0), stop=(par == 1)),
                     [lk, rq], [ba[:, 0:128]], sig=(par == 1))
            at = AT[:, h, :]
            A("dve", lambda e, o=at.ap, m=hm_cur[0][:].ap, i=ba[:, 0:128].ap: e.tensor_tensor(out=o, in0=m, in1=i, op=ALU.mult),
              [hm_cur[0][:], ba[:, 0:128]], [at])
        vers = [sver[0] % 3]
        for ci in range(2):
            state_update(j, ci, 2)
            vers.append(sver[0] % 3)
        bo = nb()
        for h in range(4):
            ob = bo[:, h * 128:(h + 1) * 128]
            for ci, qq in ((0, qtT0), (1, qtT1)):
                lq = qq[:, h, gs]
                rs = Sbf[:, vers[ci], h, :]
                P.op("pe", lambda e, o=ob.ap, l=lq.ap, r=rs.ap, ci=ci: e.matmul(o, lhsT=l, rhs=r, start=(ci == 0), stop=False),
                     [lq, rs], [ob], sig=False)
            la = AT[:, h, :]
            rv = hi_tok[:, j, h * 128:(h + 1) * 128]
            P.op("pe", lambda e, o=ob.ap, l=la.ap, r=rv.ap: e.matmul(o, lhsT=l, rhs=r, start=False, stop=True),
                 [la, rv], [ob], sig=True)
        hgrn_finish(bo)

    def hgrn_finish(bo):
        A("act", lambda e, o=o_tok[:].ap, i=bo[:].ap: e.activation(out=o, in_=i, func=AF.Copy), [bo[:]], [o_tok[:]])
        ss4 = stcol(4)
        for h in range(4):
            sh = V(ss4.ap[:, h:h + 1], ss4.name, ss4.lo + h, ss4.lo + h + 1)
            oh = o_tok[:, h * 128:(h + 1) * 128]
            A("act", lambda e, o=junk[:, 0:128].ap, i=oh.ap, a=sh.ap: e.activation(out=o, in_=i, func=AF.Square, accum_out=a),
              [oh], [junk[:, 0:128], sh])
        r4 = rstd_of(ss4, 128, 1.0, w=4)
        o3 = o_tok[:].ap.rearrange("p (h v) -> p h v", h=4)
        r4b = r4.ap.unsqueeze(2).to_broadcast([128, 4, 128])
        A("dve", lambda e, o=o3, b=r4b: e.tensor_tensor(out=o, in0=o, in1=b, op=ALU.mult), [o_tok[:], r4], [o_tok[:]])
        A("dve", lambda e, o=o_tok[:].ap, b=ghg4[:].ap: e.tensor_tensor(out=o, in0=o, in1=b, op=ALU.mult), [o_tok[:], ghg4[:]], [o_tok[:]])
        md = mix_tok[:, 512:1024]
        A("dve", lambda e, o=md.ap, i=o_tok[:].ap, g_=gate[:].ap: e.tensor_tensor(out=o, in0=i, in1=g_, op=ALU.mult), [o_tok[:], gate[:]], [md])

    hm_cur = [hmask]

    def mix_to_T(j):
        for c in range(8):
            src = mix_tok[:, c * 128:(c + 1) * 128]
            dst = TPB(c * 128, (c + 1) * 128)
            P.op("pe", lambda e, o=dst.ap, i=src.ap: e.transpose(out=o, in_=i, identity=identb[:].ap),
                 [src, identb[:]], [dst], sig=(c == 7))
        tp = TPB(0, 1024)
        tp3 = bk7b[:, 0:1024].rearrange("p (c t) -> p c t", c=8)
        d3 = xnT[:, :, j * 128:(j + 1) * 128]
        A("act", lambda e, o=d3.ap, i=tp3: e.activation(out=o, in_=i, func=AF.Copy), [tp], [d3])

    def out_proj(tl, js):
        P.dma("sp", GP[:], dram_v(gpost, gpost.h[1]), "gp")
        for c in range(8):
            s_ = wdr_i[0] % 4
            wdr_i[0] += 1
            ck = ("out", c)
            if ck in cached:
                P.dma("sp", WDR[:, s_, :], dram_v(sc_out, sc_out.h[c], "_%d" % c), "hdr%d" % s_)
            else:
                P.dma("pool", WDR[:, s_, :], dram_v(w_out, w_out.h[c * 128:(c + 1) * 128, :]), "wdr%d" % s_)
                P.dma("sp", dram_v(sc_out, sc_out.h[c], "_%d" % c), WDR[:, s_, :], "wbd%d" % s_)
                cached.add(ck)
            for j in js:
                for h in range(2):
                    b = BK[j * 2 + h]
                    la = xnT[:, c, j * 128:(j + 1) * 128]
                    rw = WDR[:, s_, h * 512:(h + 1) * 512]
                    P.op("pe", lambda e, o=b[:].ap, l=la.ap, r=rw.ap, c=c: e.matmul(o, lhsT=l, rhs=r, start=(c == 0), stop=(c == 7)),
                         [la, rw], [b[:]], sig=(c == 7 or (j == js[-1] and h == 1)))
        for j in js:
            post_norm_add(tl[j], [BK[j * 2], BK[j * 2 + 1]], 1, 1.0, j % 2)

    def sample_hgrn(j, s9, s10):
        gate_group(j, s9, s10)
        A("dve", lambda e: e.memset(qexp[:].ap, 0.0), [], [qexp[:]])
        bo = BK[6]
        reserved.add(6)
        for h in range(4):
            P.dma("sp", Sb32[:], dram_v(st_in, st_in.h[:, h].rearrange("b k v -> k b v")), "sb32")
            sbf2 = Sbbf[:].ap.rearrange("p b v -> p (b v)")
            s322 = Sb32[:].ap.rearrange("p b v -> p (b v)")
            A("act", lambda e, o=sbf2, i=s322: e.activation(out=o, in_=i, func=AF.Copy), [Sb32[:]], [Sbbf[:]])
            qsrc = qtT0[:, h, 0:128]
            qd = qexp[:].ap[:, 0:16 * 136].rearrange("p (b x) -> p b x", x=136)[:, :, 0:8]
            A("dve", lambda e, o=qd, i=qsrc.ap.rearrange("p (b t) -> p b t", t=8): e.tensor_copy(out=o, in_=i), [qsrc], [qexp[:]])
            ba = nb()
            lk = ktT[:, h, 0:128]
            P.op("pe", lambda e, o=ba[:, 0:128].ap, l=lk.ap, r=qsrc.ap: e.matmul(o, lhsT=l, rhs=r, start=True, stop=True),
                 [lk, qsrc], [ba[:, 0:128]])
            at = AT[:, h, :]
            A("dve", lambda e, o=at.ap, m=hmaskS[:].ap, i=ba[:, 0:128].ap: e.tensor_tensor(out=o, in0=m, in1=i, op=ALU.mult),
              [hmaskS[:], ba[:, 0:128]], [at])
            ob = bo[:, h * 128:(h + 1) * 128]
            for b in range(16):
                lq = qexp[:, b * 128:(b + 1) * 128]
                rs = Sbbf[:, b, :]
                P.op("pe", lambda e, o=ob.ap, l=lq.ap, r=rs.ap, b=b: e.matmul(o, lhsT=l, rhs=r, start=(b == 0), stop=False),
                     [lq, rs], [ob], sig=False)
            rv = hi_tok[:, j, h * 128:(h + 1) * 128]
            P.op("pe", lambda e, o=ob.ap, l=at.ap, r=rv.ap: e.matmul(o, lhsT=l, rhs=r, start=False, stop=True), [at, rv], [ob])
            lkh = khat_tok[:, 0, h, :]
            for b4 in range(4):
                pb = nb()
                for bb in range(4):
                    b = b4 * 4 + bb
                    vm = vmk[:, bb, :]
                    sc = seqsel[:, b:b + 1]
                    A("dve", lambda e, o=vm.ap, i=rv.ap, c=sc.ap: e.tensor_scalar(out=o, in0=i, scalar1=c, scalar2=None, op0=ALU.mult),
                      [rv, sc], [vm])
                    pbr = pb[:, bb * 128:(bb + 1) * 128]
                    P.op("pe", lambda e, o=pbr.ap, l=lkh.ap, r=vm.ap: e.matmul(o, lhsT=l, rhs=r, start=True, stop=True), [lkh, vm], [pbr])
                t32 = HT[:, 0, :]
                A("act", lambda e, o=t32.ap, i=pb[:].ap: e.activation(out=o, in_=i, func=AF.Copy), [pb[:]], [t32])
                s4 = Sb32[:, b4 * 4:(b4 + 1) * 4, :]
                dsel = DCH[:, h, b4 * 4:(b4 + 1) * 4]
                db = dsel.ap.unsqueeze(2).to_broadcast([128, 4, 128])
                A("dve", lambda e, o=s4.ap, b_=db: e.tensor_tensor(out=o, in0=o, in1=b_, op=ALU.mult), [s4, dsel], [s4])
                t3 = t32.ap.rearrange("p (b v) -> p b v", b=4)
                A("dve", lambda e, o=s4.ap, i=t3: e.tensor_tensor(out=o, in0=o, in1=i, op=ALU.add), [s4, t32], [s4])
            P.dma("sp", dram_v(state_s, state_s.h[:, h].rearrange("b k v -> k b v"), "_%d" % h), Sb32[:], "sbo")
        reserved.discard(6)
        hgrn_finish(bo)

    def sample_attention(j):
        qc = slice(j * 128, (j + 1) * 128)
        ring = [0]
        for kv in range(2):
            accs = [BK[3 + hl] for hl in range(4)]
            for blk in range(17):
                if blk < 16:
                    r3 = ring[0] % 3
                    r2 = ring[0] % 2
                    ring[0] += 1
                    for dup in range(2):
                        P.dma("pool", Kd[:, r3, dup * 64:(dup + 1) * 64],
                              dram_v(cache_k, cache_k.h[blk, :, kv * 64:(kv + 1) * 64]), "kd%d_%d" % (r3, dup))
                    P.dma("pool", Vb[:, r3, 0:64], dram_v(cache_v, cache_v.h[blk, :, kv * 64:(kv + 1) * 64]), "vb%d" % r3)
                    src = Kd[:, r3, :]
                    dst = TPB(0, 128)
                    P.op("pe", lambda e, o=dst.ap, i=src.ap: e.transpose(out=o, in_=i, identity=identb[:].ap), [src, identb[:]], [dst])
                    kc = kcT[:, r2, :]
                    A("act", lambda e, o=kc.ap, i=dst.ap: e.activation(out=o, in_=i, func=AF.Copy), [dst], [kc])
                    kview = lambda p0, r2=r2: kcT[p0:p0 + 64, r2, :]
                    rv = Vb[:, r3, 0:65]
                    mk = Zm[:, 120 - 8 * blk: 120 - 8 * blk + 128]
                else:
                    kview = lambda p0, kv=kv: kT2[p0:p0 + 64, kv, 128 + j * 128: 128 + (j + 1) * 128]
                    rv = V1[:, 1 + j, kv, 0:65]
                    mk = newmask[:]
                bs = BK[blk % 2]
                for hl in range(4):
                    h = kv * 4 + hl
                    p0 = (h % 2) * 64
                    ob = bs[:, hl * 128:(hl + 1) * 128]
                    P.op("pe", lambda e, o=ob.ap, r=mk.ap: e.matmul(o, lhsT=identb[:].ap, rhs=r, start=True, stop=False),
                         [identb[:], mk], [ob], sig=False)
                    lk = kview(p0)
                    rq = qT[p0:p0 + 64, h // 2, qc]
                    P.op("pe", lambda e, o=ob.ap, l=lk.ap, r=rq.ap: e.matmul(o, lhsT=l, rhs=r, start=False, stop=True),
                         [lk, rq], [ob], sig=(hl == 3))
                pt = PT[:, blk % 2, :]
                A("act", lambda e, o=pt.ap, i=bs[:].ap: e.activation(out=o, in_=i, func=AF.Exp, scale=SCALE), [bs[:]], [pt])
                for hl in range(4):
                    ob = accs[hl][:, 0:65]
                    lp = PT[:, blk % 2, hl * 128:(hl + 1) * 128]
                    P.op("pe", lambda e, o=ob.ap, l=lp.ap, r=rv.ap, blk=blk: e.matmul(o, lhsT=l, rhs=r, start=(blk == 0), stop=(blk == 16)),
                         [lp, rv], [ob], sig=(hl == 3))
            for hl in range(4):
                od = OA[:, hl, 0:65]
                A("act", lambda e, o=od.ap, i=accs[hl][:, 0:65].ap: e.activation(out=o, in_=i, func=AF.Copy), [accs[hl][:, 0:65]], [od])
            attn_norm(kv)
        attn_finish()

    def mixer_tile(tl):
        ng = len(tl)
        N = 128 * ng
        prenorm(tl, 1)
        kv_part(tl, ng, N)
        if tl[0] == 0:
            carry_prev(1, 128)
            q_part(N)
            s7 = win_block(7)
            s8 = win_block(8)
            hi_proj(ng, s7, s8, js=[1])
            s3 = win_block(3)
            s5 = win_block(5)
            hgrn_head_feat(0, 128, 1, s5, s3, 3, rmaskS, col0=128, csz=8)
            hgrn_head_feat(1, 128, 1, s5, s3, 3, rmaskS, col0=128, csz=8)
            s4 = win_block(4)
            s6 = win_block(6)
            hgrn_head_feat(2, 128, 1, s6, s4, 3, rmaskS, col0=128, csz=8)
            hgrn_head_feat(3, 128, 1, s6, s4, 3, rmaskS, col0=128, csz=8)
            s9 = win_block(9)
            s10 = win_block(10)
            sample_hgrn(1, s9, s10)
            sample_attention(1)
            mix_to_T(1)
            out_proj(tl, [1])
            return
        if MIXLVL < 2:
            carry_prev(ng, N)
            return
        q_part(N)
        s7 = win_block(7)
        s8 = win_block(8)
        hi_proj(ng, s7, s8)
        s3 = win_block(3)
        s5 = win_block(5)
        hgrn_head_feat(0, N, ng, s5, s3, 2, rmask)
        hgrn_head_feat(1, N, ng, s5, s3, 2, rmask)
        s4 = win_block(4)
        s6 = win_block(6)
        hgrn_head_feat(2, N, ng, s6, s4, 2, rmask)
        hgrn_head_feat(3, N, ng, s6, s4, 2, rmask)
        s9 = win_block(9)
        s10 = win_block(10)
        for j, g in enumerate(tl):
            if MIXLVL >= 3:
                hgrn_out_group(j, s9, s10)
            if MIXLVL >= 4:
                attention_group(j, g)
            if MIXLVL >= 5:
                mix_to_T(j)
        carry_prev(ng, N)
        if MIXLVL >= 5:
            out_proj(tl, list(range(ng)))

    for dst_, src_, nm in ((maskP, maskP_in, "ca"), (maskH, maskH_in, "cb"), (Zm, zm_in, "ci"), (newmask, nm_in, "cj")):
        P.dma("pool", dst_[:], dram_v(src_), nm)
    for dst_, src_, nm in ((hmaskS, hmS_in, "ck"), (rmaskS, rmS_in, "cl"), (seqsel, seqsel_in, "cm")):
        P.dma("sp", dst_[:], dram_v(src_), nm)
    A("dve", lambda e: e.memset(Vb[:].ap.rearrange("p a b -> p (a b)"), 1.0), [], [Vb[:]])
    for dst_, src_, nm in ((esink_raw, sink_in, "cg"), (gattn, gattn_in, "cd"), (ghg4, ghg_in, "ce"), (bk2, bk2_in, "cf"),
                           (cpar, cpar_in, "ch")):
        P.dma("sp", dst_[:], dram_v(src_), nm)
    A("act", lambda e: e.activation(out=esink[:].ap, in_=esink_raw[:].ap, func=AF.Exp), [esink_raw[:]], [esink[:]])
    A("dve", lambda e: e.memset(V1[:].ap.rearrange("p a b c -> p (a b c)"), 1.0), [], [V1[:]])
    for dst_, src_, nm in ((binF, binF_in, "c3"), (bhi, bhi_in, "c4"), (bhg, bhg_in, "c5"), (lbl, lbl_in, "c6"),
                           (rmask, rmask_in, "c7"), (hmask, hmask_in, "c8"), (sel, sel_in, "c9")):
        P.dma("sp", dst_[:], dram_v(src_), nm)
    A("dve", lambda e: e.tensor_tensor(out=lbt[:, 0, :].ap, in0=lbl[:, 0, :].ap, in1=lbl[:, 1, :].ap, op=ALU.subtract),
      [lbl[:]], [lbt[:, 0, :]])
    A("act", lambda e: e.activation(out=lbt[:, 1, :].ap, in_=lbt[:, 0, :].ap, func=AF.Sigmoid), [lbt[:, 0, :]], [lbt[:, 1, :]])
    A("dve", lambda e: e.tensor_scalar(out=lbt[:, 2, :].ap, in0=lbt[:, 1, :].ap, scalar1=-1.0, scalar2=1.0, op0=ALU.mult, op1=ALU.add),
      [lbt[:, 1, :]], [lbt[:, 2, :]])
    A("dve", lambda e: e.memset(S[:].ap, 0.0), [], [S[:]])
    P.dma("sp", bkv[:], dram_v(bkv_in), "c2")
    P.dma("sp", dram_v(kwin_s, kwin_s.h[:, 0:120, :], "_c"), dram_v(cache_k, cache_k.h[:, 8:128, :]), "kvck")
    P.dma("sp", dram_v(vwin_s, vwin_s.h[:, 0:120, :], "_c"), dram_v(cache_v, cache_v.h[:, 8:128, :]), "kvcv")
    tiles = [[0, 1]] + [[2 + 4 * t + i for i in range(4)] for t in range(4)]
    if stage == -2:
        rms_to_T(tiles[0], 0, xnT)
    elif stage == -3:
        ffn(tiles[0], 0)
    elif stage == -6:
        ffn(tiles[0], 0, 1)
    elif stage == -7:
        ffn(tiles[0], 0, 2)
    elif stage in (-4, -5):
        wg = w_gu[0].h.rearrange("(c p) n -> p c n", p=128)
        for f in range(3):
            for hh in range(2):
                if stage == -4:
                    P.dma("pool", WGU[:, f, :, hh * 128:(hh + 1) * 128],
                          dram_v(w_gu[0], wg[:, :, hh * DFF + f * 128: hh * DFF + (f + 1) * 128]), "wgu%d_%d" % (f, hh))
                else:
                    for c in range(8):
                        P.dma("pool", WGU[:, f, c, hh * 128:(hh + 1) * 128],
                              dram_v(w_gu[0], w_gu[0].h[c * 128:(c + 1) * 128, hh * DFF + f * 128: hh * DFF + (f + 1) * 128]), "wgu%d_%d" % (f, hh))
    elif stage >= 1:
        if not USE_CC:
            P.dma("sp", pflag[:], dram_v(pflag_in), "cn")
            scratch = tiles[1]
            bg_i = [0]

            def bg_dma(dst_v, src_v):
                bg_queue.append((dst_v, src_v))

            def bg_convert(t):
                if t == 0:
                    wg2 = w_gu[1].h.rearrange("(c p) n -> p c n", p=128)
                    for f in range(NF):
                        d3 = sc_gu[1].h[f].rearrange("p (c n) -> p c n", c=8)
                        for hh in range(2):
                            bg_dma(dram_v(sc_gu[1], d3[:, :, hh * 128:(hh + 1) * 128], "_%d" % f),
                                   dram_v(w_gu[1], wg2[:, :, hh * DFF + f * 128: hh * DFF + (f + 1) * 128]))
                        cached.add(("gu", 1, f))
                elif t == 1:
                    wd2 = w_dn[1].h.rearrange("(f p) n -> p f n", p=128)
                    for f in range(NF):
                        bg_dma(dram_v(sc_dn[1], sc_dn[1].h[f], "_%d" % f), dram_v(w_dn[1], wd2[:, f, :]))
                        cached.add(("dn", 1, f))
                    for c in range(8):
                        bg_dma(dram_v(sc_out, sc_out.h[c], "_%d" % c), dram_v(w_out, w_out.h[c * 128:(c + 1) * 128, :]))
                        cached.add(("out", c))
                elif t == 2:
                    for blk in (2, 0, 1, 3, 4, 9, 10):
                        d3 = sc_in.h[blk].rearrange("p (c n) -> p c n", c=8)
                        for hh in range(2):
                            bg_dma(dram_v(sc_in, d3[:, :, hh * 128:(hh + 1) * 128], "_%d" % blk),
                                   dram_v(w_in, wi_ap[:, :, blk * 256 + hh * 128: blk * 256 + (hh + 1) * 128]))
                        cached.add(("in", blk))

            for t in range(4):
                for i, g in enumerate(scratch):
                    r0 = t * 512 + i * 128
                    P.dma("sp", X[:, g, :], dram_v(xprev, xprev.h[r0:r0 + 128, :]), "x%d" % g)
                ffn(scratch, 0)
                hgrn_scan_tile(scratch)
            if os.environ.get("NOBG") is None:
                for t in range(3):
                    bg_convert(t)
            s2_ = S[:].ap.rearrange("p h v -> p (h v)")
            A("dve", lambda e, o=s2_, c=pflag[:].ap: e.tensor_scalar(out=o, in0=o, scalar1=c, scalar2=None, op0=ALU.mult),
              [S[:], pflag[:]], [S[:]])
            for g in range(2, NG):
                P.dma("sp", X[:, g, :], dram_v(xin, xin.h[g * 128:(g + 1) * 128, :]), "x%d" % g)
        for ti, tl in enumerate(tiles):
            nxt = tiles[ti + 1] if ti + 1 < len(tiles) else None
            ffn(tl, 0, hook=(lambda nxt=nxt: prenorm_early(nxt, 0)) if (nxt is not None and not os.environ.get("NOHOOK")) else None)
    bg_drain(10 ** 6)
    if stage >= 4:
        if USE_CC:
            for tl in tiles[1:]:
                hgrn_scan_tile(tl)
            exchange()
        A("act", lambda e: e.activation(out=Sbf[:, 0, :, :].ap, in_=S[:].ap, func=AF.Copy), [S[:]], [Sbf[:, 0, :, :]])
        if stage == 4:
            for tl in tiles[1:]:
                hgrn_scan_tile(tl)
            P.dma("sp", dram_v(state_p), S[:], "stout")
    if stage >= 5:
        for ti, tl in enumerate(tiles[:1] if os.environ.get("AUXONLY") else tiles):
            mixer_tile(tl)
            if not os.environ.get("AUXONLY"):
                nxt = tiles[ti + 1] if ti + 1 < len(tiles) else None
                ffn(tl if tl[0] != 0 else [1], 1,
                    hook=(lambda nxt=nxt: prenorm_early(nxt, 1)) if (nxt is not None and not os.environ.get("NOHOOK")) else None)
        P.dma("sp", dram_v(state_p), S[:], "stout")
    elif stage >= 2:
        for tl in tiles:
            ffn(tl, 1)
    for g in range(1, NG):
        P.dma("sp", dram_v(y, y.h[(g - 1) * 128: g * 128, :]), X[:, g, :], "yout")

    final = set()
    for k, v in P.dma_cnt.items():
        final.add((k, v))
    for e in ENGS:
        if P.cnt[e]:
            final.add((e, P.cnt[e]))
    P.final = final

    P.check()
    print("ops:", {e: len(P.q[e]) for e in ENGS}, "sems:", len(P.dma_cnt) + 5, flush=True)
    semkeys = list(ENGS) + sorted(P.dma_cnt.keys()) + ["cc"]
    sems = {k: es.enter_context(nc.semaphore("s_" + k.replace(":", "_"))) for k in semkeys}
    with nc.Block() as block:
        @block.tensor
        def _(e):
            P.replay("pe", e, sems)

        @block.scalar
        def _(e):
            P.replay("act", e, sems)

        @block.vector
        def _(e):
            P.replay("dve", e, sems)

        @block.gpsimd
        def _(e):
            P.replay("pool", e, sems)

        @block.sync
        def _(e):
            P.replay("sp", e, sems)
    es.close()
    return nc


def _core_inputs(inp, c):
    f32 = np.float32
    b, half = c // 2, c % 2
    xp = inp["x_prompt"]
    xs = inp["x_sample"]
    xin = np.zeros((NG * 128, D), f32)
    if half == 1:
        xin[0:128] = xp[b, 2048 - 128:2048]
    xin[128:256] = xs[16 * c:16 * c + 16].reshape(128, D)
    xin[256:] = xp[b, half * 2048:(half + 1) * 2048]
    m = {"xin": xin}
    m["w_gu1"] = inp["ffn1_w_gu"][0]
    m["w_gu2"] = inp["ffn2_w_gu"][0]
    m["w_dn1"] = inp["ffn1_w_down"][0]
    m["w_dn2"] = inp["ffn2_w_down"][0]
    m["w_in"] = inp["w_in"][0]
    m["w_out"] = inp["w_out"][0]
    gcols = np.stack([inp["norm_ffn1_pre"][0], inp["norm_mix_pre"][0], inp["norm_ffn2_pre"][0]])
    m["gcols"] = np.ascontiguousarray(gcols.reshape(3, 8, 128).transpose(2, 0, 1))
    gp = np.stack([inp["norm_ffn1_post"][0], inp["norm_mix_post"][0], inp["norm_ffn2_post"][0]])
    m["gpost"] = np.ascontiguousarray(np.broadcast_to(gp[:, None, :], (3, 128, D)))
    m["ident"] = np.eye(128, dtype=f32)
    bi = inp["b_in"][0]
    m["binF"] = bi.reshape(22, 128).T
    m["bhi"] = np.broadcast_to(bi[None, 1792:2304], (128, 512))
    m["bhg"] = np.broadcast_to(bi[None, 2304:2816], (128, 512))
    m["lbl"] = inp["hg_lb_logits"].reshape(2, 4, 128).transpose(2, 0, 1)
    rm = np.ones((128, 512), f32)
    rm[:, 0::64] = 0.0
    m["rmask"] = rm
    ii = np.arange(128)
    m["hmask"] = ((ii[:, None] // 64 == ii[None, :] // 64) & (ii[:, None] <= ii[None, :])).astype(f32)
    mp = np.full((128, 2, 128), NEG, f32)
    mp[:, 0][ii[:, None] >= ii[None, :]] = 0.0
    mp[:, 1][ii[:, None] <= ii[None, :]] = 0.0
    m["maskP"] = mp
    m["maskH"] = mp[:, 0] if half == 1 else np.full((128, 128), NEG, f32)
    zm = np.full((128, 248), NEG, f32)
    for t in range(8):
        zm[t:, 120 + t] = 0.0
    m["zmask"] = zm
    sq, tq = ii // 8, ii % 8
    m["newmask"] = np.where((sq[:, None] == sq[None, :]) & (tq[:, None] <= tq[None, :]), 0.0, NEG).astype(f32)
    m["hmaskS"] = ((sq[:, None] == sq[None, :]) & (tq[:, None] <= tq[None, :])).astype(f32)
    rms_ = np.ones((128, 128), f32)
    rms_[:, 0::8] = 0.0
    m["rmaskS"] = rms_
    m["seqsel"] = (sq[:, None] == np.arange(16)[None, :]).astype(f32)
    m["state_s_in"] = inp["state_hgrn"][0, 16 * c:16 * c + 16]
    m["sinks"] = np.broadcast_to(inp["attn_sinks"][0][None, :], (128, 8))
    m["gattn"] = np.broadcast_to(inp["attn_out_norm"][0][None, :], (128, 512))
    m["ghg4"] = np.broadcast_to(np.tile(inp["hg_out_norm"][0], 4)[None, :], (128, 512))
    bk = bi[512:640].reshape(2, 64)
    m["bk2"] = np.concatenate([bk, bk], axis=1).T
    cp = np.zeros((128, 2, 128), f32)
    cp[:, 0, 0:64] = 1.0
    cp[:, 1, 64:128] = 1.0
    m["cpar"] = cp
    m["xprev"] = xp[b, 0:2048] if half == 1 else np.zeros((2048, D), f32)
    m["pflag"] = np.full((128, 1), float(half), f32)
    selv = np.zeros((128, 8), f32)
    if half == 1:
        selv[:, c - 1] = 1.0
    m["sel"] = selv
    m["bkv"] = np.broadcast_to(inp["b_in"][0][None, 512:768], (128, 256))
    m["cache_k"] = inp["cache_k_win"][0, 16 * c:16 * c + 16].reshape(16, 128, 128)
    m["cache_v"] = inp["cache_v_win"][0, 16 * c:16 * c + 16].reshape(16, 128, 128)
    return {k: np.ascontiguousarray(v, dtype=f32) for k, v in m.items()}


_NC_CACHE = {}


def _run(inputs, stage=99):
    inp = {k: np.asarray(v) for k, v in inputs.items()}
    if stage not in _NC_CACHE:
        _NC_CACHE[stage] = build_program(stage)
    nc = _NC_CACHE[stage]
    in_maps = [_core_inputs(inp, c) for c in range(8)]
    res = run_bass_kernel_spmd(nc, in_maps, core_ids=list(range(8)))
    return res.results


def kernel(**inputs):
    r = _run(inputs, int(os.environ.get("KSTAGE", "99")))
    f32 = np.float32
    yp = np.zeros((4, 4096, D), f32)
    ys = np.zeros((128, 8, D), f32)
    for c in range(8):
        b, half = c // 2, c % 2
        yc = r[c]["y"]
        ys[16 * c:16 * c + 16] = yc[0:128].reshape(16, 8, D)
        yp[b, half * 2048:(half + 1) * 2048] = yc[128:]
    kp = np.zeros((1, 4, 128, 2, 64), f32)
    vp = np.zeros((1, 4, 128, 2, 64), f32)
    sp = np.zeros((1, 4, 4, 128, 128), f32)
    kd = np.zeros((1, 128, 128, 2, 64), f32)
    vd = np.zeros((1, 128, 128, 2, 64), f32)
    sd = np.zeros((1, 128, 4, 128, 128), f32)
    for c in range(8):
        if c % 2 == 1:
            sp[0, c // 2] = r[c]["state_p"].transpose(1, 0, 2)
            kp[0, c // 2] = r[c]["kwin_p"].reshape(128, 2, 64)
            vp[0, c // 2] = r[c]["vwin_p"].reshape(128, 2, 64)
        sd[0, 16 * c:16 * c + 16] = r[c]["state_s"]
        kd[0, 16 * c:16 * c + 16] = r[c]["kwin_s"].reshape(16, 128, 2, 64)
        vd[0, 16 * c:16 * c + 16] = r[c]["vwin_s"].reshape(16, 128, 2, 64)
    return (yp, ys, kp, vp, sp, kd, vd, sd)
```

```python
import numpy as np
from contextlib import ExitStack
import concourse.bass as bass
import concourse.mybir as mybir
from concourse.bass_utils import run_bass_kernel_spmd

F32 = mybir.dt.float32
BF16 = mybir.dt.bfloat16
AF = mybir.ActivationFunctionType
ALU = mybir.AluOpType

D = 1024
DFF = 2816
NF = DFF // 128
INW = 2816
NG = 18
EPS = 1e-6
NEG = -30000.0
SCALE = 64 ** -0.5
ENGS = ("pe", "act", "dve", "pool", "sp")
import os
PN_LEVEL = int(os.environ.get("PN_LEVEL", "4"))
MIXLVL = int(os.environ.get("MIXLVL", "5"))
SUB = int(os.environ.get("SUB", "9"))
USE_CC = bool(os.environ.get("USE_CC"))


class V:
    __slots__ = ("ap", "name", "lo", "hi")

    def __init__(self, ap, name, lo, hi):
        self.ap, self.name, self.lo, self.hi = ap, name, lo, hi

    def with_ap(self, ap):
        return V(ap, self.name, self.lo, self.hi)


class TT:
    def __init__(self, handle, name, shape, esz=1, base=0):
        self.h, self.name, self.shape = handle, name, list(shape)
        self.base = base
        st = [1] * len(shape)
        for i in range(len(shape) - 2, 0, -1):
            st[i] = st[i + 1] * shape[i + 1]
        self.st = st
        self.esz = esz

    def __getitem__(self, idx):
        if not isinstance(idx, tuple):
            idx = (idx,)
        idx = idx + (slice(None),) * (len(self.shape) - len(idx))
        lo, hi = 0, 0
        for i in range(1, len(self.shape)):
            s = idx[i]
            if isinstance(s, int):
                a, b = s, s + 1
            else:
                a = 0 if s.start is None else s.start
                b = self.shape[i] if s.stop is None else s.stop
            lo += a * self.st[i]
            hi += (b - 1) * self.st[i]
        if self.name.startswith("BK"):
            return V(self.h[idx], self.name, 0, 512)
        return V(self.h[idx], self.name, self.base + lo * self.esz, self.base + (hi + 1) * self.esz)


class Prog:
    def __init__(self):
        self.q = {e: [] for e in ENGS}
        self.cnt = {e: 0 for e in ENGS}
        self.acc = {}
        self.dma_cnt = {}
        self.dma_hist = {}

    def _deps(self, reads, writes, tok):
        deps = set()
        for v in reads:
            recs = self.acc.setdefault(v.name, [])
            for (lo, hi, kind, t) in recs:
                if kind == "w" and lo < v.hi and v.lo < hi:
                    deps.add(t)
        for v in writes:
            recs = self.acc.setdefault(v.name, [])
            for (lo, hi, kind, t) in recs:
                if lo < v.hi and v.lo < hi:
                    deps.add(t)
        for v in reads:
            recs = self.acc[v.name]
            recs[:] = [r for r in recs if not (r[2] == "r" and r[0] == v.lo and r[1] == v.hi
                                               and r[3][0] == tok[0])]
            recs.append((v.lo, v.hi, "r", tok))
        for v in writes:
            recs = self.acc[v.name]
            recs[:] = [r for r in recs if not (v.lo <= r[0] and r[1] <= v.hi)]
            recs.append((v.lo, v.hi, "w", tok))
        deps.discard(tok)
        return deps

    def op(self, eng, fn, reads=(), writes=(), sig=True):
        if sig:
            self.cnt[eng] += 1
            tok = (eng, self.cnt[eng])
        else:
            tok = (eng, self.cnt[eng] + 1)
        deps = self._deps(reads, writes, tok)
        self.q[eng].append((deps, fn, eng if sig else None, 1))
        return tok

    def dma(self, eng, out, in_, slot, extra_reads=(), extra_writes=()):
        key = "dma:" + slot
        self.dma_cnt[key] = self.dma_cnt.get(key, 0) + 16
        tok = (key, self.dma_cnt[key])
        deps = self._deps([in_] + list(extra_reads), [out] + list(extra_writes), tok)
        if slot not in ("kvout", "yout", "stout") and self.dma_cnt[key] > 16:
            deps.add((key, self.dma_cnt[key] - 16))
        hist = self.dma_hist.setdefault(eng, [])
        if len(hist) >= 6:
            deps.add(hist[-6])
        if slot not in ("kvout", "yout"):
            hist.append(tok)
        o, i = out.ap, in_.ap
        self.q[eng].append((deps, lambda e: e.dma_start(out=o, in_=i), key, 16))
        return tok

    def check(self):
        val = {}
        pos = {e: 0 for e in ENGS}
        own = {e: 0 for e in ENGS}
        progress = True
        while progress:
            progress = False
            for eng in ENGS:
                while pos[eng] < len(self.q[eng]):
                    deps, fn, sigkey, inc = self.q[eng][pos[eng]]
                    ok = True
                    for (k, v) in deps:
                        if k == eng and (eng == "pe" or v > own[eng]):
                            continue
                        if val.get(k, 0) < v:
                            ok = False
                            break
                    if not ok:
                        break
                    if sigkey is not None:
                        val[sigkey] = val.get(sigkey, 0) + inc
                        if sigkey == eng:
                            own[eng] += 1
                    pos[eng] += 1
                    progress = True
        stuck = {e: (pos[e], len(self.q[e])) for e in ENGS if pos[e] < len(self.q[e])}
        if stuck:
            msg = []
            for e in stuck:
                deps = self.q[e][pos[e]][0]
                msg.append("%s@%d waits %s" % (e, pos[e], [(k, v, val.get(k, 0)) for (k, v) in deps if val.get(k, 0) < v]))
            raise RuntimeError("DEADLOCK: " + "; ".join(msg))
        for (k, v) in self.final:
            assert val.get(k, 0) == v, (k, v, val.get(k, 0))

    def replay(self, eng, e, sems):
        seen = {}
        own = 0
        for (deps, fn, sigkey, inc) in self.q[eng]:
            for (k, val) in sorted(deps):
                if k == eng and (eng == "pe" or val > own):
                    continue
                if seen.get(k, 0) >= val:
                    continue
                e.wait_ge(sems[k], val)
                seen[k] = val
            ins = fn(e)
            if sigkey is not None:
                ins.then_inc(sems[sigkey], inc)
                if sigkey == eng:
                    own += 1
        if eng == "sp":
            for (k, val) in sorted(self.final):
                if seen.get(k, 0) < val:
                    e.wait_ge(sems[k], val)


def build_program(stage=99):
    nc = bass.Bass("TRN2", target_bir_lowering=False)
    P = Prog()
    es = ExitStack()

    def din(name, shape, dt=F32):
        return TT(nc.dram_tensor(name, list(shape), dt, kind="ExternalInput").ap(), name, [1, 1])

    def dout(name, shape, dt=F32):
        return TT(nc.dram_tensor(name, list(shape), dt, kind="ExternalOutput").ap(), name, [1, 1])

    def dram_v(t, ap=None, sub=""):
        return V(t.h if ap is None else ap, t.name + sub, 0, 1)

    def sb(name, shape, dt=F32):
        h = es.enter_context(nc.sbuf_tensor(name, list(shape), dt))
        return TT(h, name, shape)

    def ps(name, shape, dt=F32):
        h = es.enter_context(nc.psum_tensor(name, list(shape), dt))
        return TT(h, name, shape)

    xin = din("xin", [NG * 128, D])
    w_gu = [din("w_gu1", [D, 2 * DFF]), din("w_gu2", [D, 2 * DFF])]
    w_dn = [din("w_dn1", [DFF, D]), din("w_dn2", [DFF, D])]
    w_in = din("w_in", [D, INW])
    w_out = din("w_out", [D, D])
    gcols = din("gcols", [128, 3, 8])
    gpost = din("gpost", [3, 128, D])
    ident_in = din("ident", [128, 128])
    y = dout("y", [17 * 128, D])
    bkv_in = din("bkv", [128, 256])
    cache_k = din("cache_k", [16, 128, 128])
    cache_v = din("cache_v", [16, 128, 128])
    kwin_p = dout("kwin_p", [128, 128])
    vwin_p = dout("vwin_p", [128, 128])
    kwin_s = dout("kwin_s", [16, 128, 128])
    vwin_s = dout("vwin_s", [16, 128, 128])
    binF_in = din("binF", [128, 22])
    bhi_in = din("bhi", [128, 512])
    bhg_in = din("bhg", [128, 512])
    lbl_in = din("lbl", [128, 2, 4])
    rmask_in = din("rmask", [128, 512])
    hmask_in = din("hmask", [128, 128])
    sel_in = din("sel", [128, 8])
    xprev = din("xprev", [2048, D])
    pflag_in = din("pflag", [128, 1])
    state_p = dout("state_p", [128, 4, 128])
    maskP_in = din("maskP", [128, 2, 128])
    maskH_in = din("maskH", [128, 128])
    zm_in = din("zmask", [128, 248])
    nm_in = din("newmask", [128, 128])
    hmS_in = din("hmaskS", [128, 128])
    rmS_in = din("rmaskS", [128, 128])
    seqsel_in = din("seqsel", [128, 16])
    st_in = din("state_s_in", [16, 4, 128, 128])
    state_s = dout("state_s", [16, 4, 128, 128])
    sink_in = din("sinks", [128, 8])
    gattn_in = din("gattn", [128, 512])
    ghg_in = din("ghg4", [128, 512])
    bk2_in = din("bk2", [128, 2])
    cpar_in = din("cpar", [128, 2, 128])
    sc_gu = [TT(nc.dram_tensor("sc_gu%d" % i, [NF, 128, 2048], BF16).ap(), "sc_gu%d" % i, [1, 1]) for i in range(2)]
    sc_dn = [TT(nc.dram_tensor("sc_dn%d" % i, [NF, 128, 1024], BF16).ap(), "sc_dn%d" % i, [1, 1]) for i in range(2)]
    sc_in = TT(nc.dram_tensor("sc_in", [11, 128, 2048], BF16).ap(), "sc_in", [1, 1])
    sc_out = TT(nc.dram_tensor("sc_out", [8, 128, 1024], BF16).ap(), "sc_out", [1, 1])
    cached = set()
    cc_in = TT(nc.dram_tensor("cc_in", [128, 512], F32).ap(), "cc_in", [1, 1])
    cc_out = TT(nc.dram_tensor("cc_out", [8 * 128, 512], F32).ap(), "cc_out", [1, 1])

    X = sb("X", [128, NG, D])
    xn_tok = sb("xn_tok", [128, 2, D], BF16)
    xnT = sb("xnT", [128, 8, 512], BF16)
    SCR = sb("SCR", [128, 22 * 512], BF16)
    WGU = sb("WGU", [128, 3, 8, 256], BF16)
    WDR = sb("WDR", [128, 4, 1024], BF16)
    SCR2 = sb("SCR2", [128, 6144], BF16)

    def carve(base, off_b, shape, dt):
        esz = 2 if dt == F32 else 1
        n = 1
        for d_ in shape[1:]:
            n *= d_
        ap = base.h[:, off_b // 2: off_b // 2 + n * esz]
        if dt == F32:
            ap = ap.bitcast(F32)
        if len(shape) == 3:
            ap = ap.rearrange("p (a b) -> p a b", a=shape[1])
        elif len(shape) == 4:
            ap = ap.rearrange("p (a b c) -> p a b c", a=shape[1], b=shape[2])
        return TT(ap, base.name, shape, esz=esz, base=off_b // 2)

    SG = carve(SCR2, 0, [128, 2, 512], F32)
    YB = carve(SCR2, 4096, [128, 2, D], F32)
    GP = sb("GP", [128, D])
    gc = sb("gc", [128, 3, 8])
    identf = sb("identf", [128, 128])
    identb = sb("identb", [128, 128], BF16)
    junk = sb("junk", [128, D], BF16)
    st = sb("st", [128, 64])

    KV = sb("KV", [128, 2, 256])
    HT = carve(SCR2, 0, [128, 6, 512], F32)
    binF = sb("binF_sb", [128, 22])
    bhi = sb("bhi_sb", [128, 512])
    bhg = sb("bhg_sb", [128, 512])
    lbl = sb("lbl_sb", [128, 2, 4])
    lbt = sb("lbt", [128, 3, 4])
    rmask = sb("rmask_sb", [128, 512])
    hmask = sb("hmask_sb", [128, 128])
    sel = sb("sel_sb", [128, 8])
    pflag = sb("pflag_sb", [128, 1])
    S = sb("S", [128, 4, 128])
    Sbf = sb("Sbf", [128, 3, 4, 128], BF16)
    DCH = sb("DCH", [128, 4, 16])
    hi_tok = sb("hi_tok", [128, 4, 512], BF16)
    qtT0 = carve(SCR, 4096, [128, 4, 512], BF16)
    ktT = carve(SCR, 8192, [128, 4, 512], BF16)
    khT = carve(SCR, 12288, [128, 4, 512], BF16)
    qtT1 = carve(SCR, 16384, [128, 4, 512], BF16)
    khat_tok = sb("khat_tok", [128, 4, 4, 128], BF16)
    WK2 = sb("WK2", [128, 8, 2, 128], BF16)
    cpar = sb("cpar_sb", [128, 2, 128])
    esink_raw = sb("esink_raw", [128, 8])
    G8 = carve(SCR, 0, [128, 8, 512], F32)
    qT = carve(SCR, 0, [128, 4, 512], BF16)
    kT2 = sb("kT2", [128, 2, 640], BF16)
    V1 = sb("V1", [128, 5, 2, 66], BF16)
    PT = sb("PT", [128, 2, 512], BF16)
    AT = sb("AT", [128, 4, 128], BF16)
    OA = sb("OA", [128, 4, 66])
    a_tok = sb("a_tok", [128, 512])
    o_tok = sb("o_tok", [128, 512])
    gate = sb("gate", [128, 512])
    mix_tok = sb("mix_tok", [128, 1024], BF16)
    maskP = sb("maskP_sb", [128, 2, 128], BF16)
    maskH = sb("maskH_sb", [128, 128], BF16)
    Zm = sb("Zm", [128, 248], BF16)
    newmask = sb("newmask_sb", [128, 128], BF16)
    hmaskS = sb("hmaskS_sb", [128, 128])
    rmaskS = sb("rmaskS_sb", [128, 128])
    seqsel = sb("seqsel_sb", [128, 16])
    Kd = sb("Kd", [128, 3, 128], BF16)
    Vb = sb("Vb", [128, 3, 66], BF16)
    kcT = sb("kcT", [128, 2, 128], BF16)
    vmk = sb("vmk", [128, 4, 128], BF16)
    WDRf = TT(WDR.h[:].rearrange("p a b -> p (a b)"), "WDR", [128, 4096])
    Sb32 = carve(WDRf, 0, [128, 16, 128], F32)
    Sbbf = carve(SCR, 16384, [128, 16, 128], BF16)
    qexp = carve(SCR2, 4096, [128, 17 * 128], BF16)
    esink = sb("esink", [128, 8])
    gattn = sb("gattn_sb", [128, 512])
    ghg4 = sb("ghg4_sb", [128, 512])
    bk2 = sb("bk2_sb", [128, 2])
    bkv = sb("bkv_sb", [128, 256])
    BK = [ps("BK%d" % i, [128, 512]) for i in range(8)]
    bk7b = BK[7].h[:].bitcast(BF16)

    def TPB(lo, hi):
        return V(bk7b[:, lo:hi], "BK7", 0, 512)

    stc = [0]

    def stcol(n=1):
        c = stc[0]
        if c + n > 64:
            c = 0
        stc[0] = c + n
        return st[:, c:c + n]

    epst = sb("epst", [128, 2])
    epsb = {1.0: epst[:, 0:1], 0.5: epst[:, 1:2]}
    P.op("dve", lambda e: e.memset(epst[:, 0:1].ap, EPS), [], [epst[:, 0:1]])
    P.op("dve", lambda e: e.memset(epst[:, 1:2].ap, EPS / 0.25), [], [epst[:, 1:2]])
    P.dma("sp", gc[:], dram_v(gcols), "c0")
    P.dma("sp", identf[:], dram_v(ident_in), "c1")
    P.op("dve", lambda e: e.tensor_copy(out=identb[:].ap, in_=identf[:].ap), [identf[:]], [identb[:]])
    for g in (range(NG) if USE_CC else (0, 1)):
        P.dma("sp", X[:, g, :], dram_v(xin, xin.h[g * 128:(g + 1) * 128, :]), "x%d" % g)

    def rstd_of(ss, n, hs, w=1):
        r1 = stcol(w)
        P.op("act", lambda e, o=r1.ap, i=ss.ap: e.activation(out=o, in_=i, func=AF.Sqrt, scale=1.0 / (n * hs * hs), bias=epsb[hs].ap),
             [ss, epsb[hs]], [r1])
        r2 = stcol(w)
        P.op("dve", lambda e, o=r2.ap, i=r1.ap: e.reciprocal(out=o, in_=i), [r1], [r2])
        return r2

    def rms_to_T(groups, gidx, dstT):
        for j, g in enumerate(groups):
            ss = stcol()
            xg = X[:, g, :]
            P.op("act", lambda e, o=junk[:].ap, i=xg.ap, a=ss.ap: e.activation(out=o, in_=i, func=AF.Square, accum_out=a),
                 [xg], [junk[:], ss])
            r2 = rstd_of(ss, D, 1.0)
            xs = xn_tok[:, j % 2, :]
            P.op("dve", lambda e, o=xs.ap, i=xg.ap, s=r2.ap: e.tensor_scalar(out=o, in0=i, scalar1=s, scalar2=None, op0=ALU.mult),
                 [xg, r2], [xs])
            for c in range(8):
                src = xn_tok[:, j % 2, c * 128:(c + 1) * 128]
                dst = TPB(c * 128, (c + 1) * 128)
                P.op("pe", lambda e, o=dst.ap, i=src.ap: e.transpose(out=o, in_=i, identity=identb[:].ap),
                     [src, identb[:]], [dst], sig=(c == 7))
            tp = TPB(0, 1024)
            tp3 = tp.with_ap(bk7b[:, 0:1024].rearrange("p (c t) -> p c t", c=8))
            d3 = dstT[:, :, j * 128:(j + 1) * 128]
            gb = gc[:, gidx, :]
            gb3 = gb.ap.unsqueeze(2).to_broadcast([128, 8, 128])
            P.op("dve", lambda e, o=d3.ap, i=tp3.ap, g_=gb3: e.tensor_tensor(out=o, in0=i, in1=g_, op=ALU.mult),
                 [tp, gb], [d3])

    def post_norm_add(g, banks, gp_idx, half_scale, slot):
        ss2 = stcol(2)
        yb = YB[:, slot, :]
        for h in range(2):
            bk = banks[h][:]
            ssh = V(ss2.ap[:, h:h + 1], ss2.name, ss2.lo + h, ss2.lo + h + 1)
            P.op("act", lambda e, o=junk[:, 0:512].ap, i=bk.ap, a=ssh.ap: e.activation(out=o, in_=i, func=AF.Square, accum_out=a),
                 [bk], [junk[:, 0:512], ssh])
            if PN_LEVEL < 2:
                continue
            ybh = YB[:, slot, h * 512:(h + 1) * 512]
            gph = GP[:, h * 512:(h + 1) * 512]
            P.op("act", lambda e, o=ybh.ap, i=bk.ap: e.activation(out=o, in_=i, func=AF.Copy), [bk], [ybh])
            P.op("dve", lambda e, o=ybh.ap, g_=gph.ap: e.tensor_tensor(out=o, in0=o, in1=g_, op=ALU.mult),
                 [ybh, gph], [ybh])
        if PN_LEVEL < 3:
            return
        s1 = stcol()
        P.op("dve", lambda e, o=s1.ap, a=ss2.ap: e.tensor_scalar(out=o, in0=a[:, 0:1], scalar1=a[:, 1:2], scalar2=None, op0=ALU.add), [ss2], [s1])
        r2 = rstd_of(s1, D, half_scale)
        if PN_LEVEL < 4:
            return
        xg = X[:, g, :]
        P.op("dve", lambda e, o=xg.ap, i=yb.ap, s=r2.ap: e.scalar_tensor_tensor(out=o, in0=i, scalar=s, in1=o, op0=ALU.mult, op1=ALU.add),
             [yb, r2, xg], [xg])

    wgu_i = [0]
    wdr_i = [0]
    ring_i = [0]
    bg_queue = []
    bg_n = [0]

    def bg_drain(n=1):
        for _ in range(n):
            if not bg_queue:
                return
            dst_v, src_v = bg_queue.pop(0)
            P.dma("pool", dst_v, src_v, "bg%d" % (bg_n[0] % 8))
            bg_n[0] += 1

    prenormed = [None]

    def prenorm(groups, gidx):
        key = (tuple(groups), gidx)
        if prenormed[0] != key:
            rms_to_T(groups, gidx, xnT)
        prenormed[0] = None

    def prenorm_early(groups, gidx):
        rms_to_T(groups, gidx, xnT)
        prenormed[0] = (tuple(groups), gidx)

    def ffn(groups, which, parts=3, hook=None):
        n = len(groups)
        N = 128 * n
        prenorm(groups, 0 if which == 0 else 2)
        P.dma("sp", GP[:], dram_v(gpost, gpost.h[0 if which == 0 else 2]), "gp")
        wg = w_gu[which].h.rearrange("(c p) n -> p c n", p=128)
        for f in range(NF):
            s = wgu_i[0] % 3
            wgu_i[0] += 1
            slot2d = V(WGU.h[:, s, :, :].rearrange("p c n -> p (c n)"), "WGU", WGU[:, s, :, :].lo, WGU[:, s, :, :].hi)
            ck = ("gu", which, f)
            if ck in cached:
                P.dma("sp", slot2d, dram_v(sc_gu[which], sc_gu[which].h[f], "_%d" % f), "hgu%d" % s)
            else:
                for hh in range(2):
                    P.dma("pool", WGU[:, s, :, hh * 128:(hh + 1) * 128],
                          dram_v(w_gu[which], wg[:, :, hh * DFF + f * 128: hh * DFF + (f + 1) * 128]), "wgu%d_%d" % (s, hh))
                P.dma("sp", dram_v(sc_gu[which], sc_gu[which].h[f], "_%d" % f), slot2d, "wbg%d" % s)
                cached.add(ck)
            bks = []
            for hh in range(2):
                b = BK[ring_i[0] % 3]
                ring_i[0] += 1
                bks.append(b)
                for c in range(8):
                    lw = WGU[:, s, c, hh * 128:(hh + 1) * 128]
                    rx = xnT[:, c, 0:N]
                    P.op("pe", lambda e, o=b[:, 0:N].ap, l=lw.ap, r=rx.ap, c=c: e.matmul(o, lhsT=l, rhs=r, start=(c == 0), stop=(c == 7)),
                         [lw, rx], [b[:, 0:N]], sig=(c == 7))
            bg_drain(1)
            sg = SG[:, f % 2, 0:N]
            P.op("act", lambda e, o=sg.ap, i=bks[0][:, 0:N].ap: e.activation(out=o, in_=i, func=AF.Silu),
                 [bks[0][:, 0:N]], [sg])
            at = V(SCR.h[:, f * 512: f * 512 + N], "SCR", f * 512, f * 512 + N)
            P.op("dve", lambda e, o=at.ap, a=sg.ap, b_=bks[1][:, 0:N].ap: e.tensor_tensor(out=o, in0=a, in1=b_, op=ALU.mult),
                 [sg, bks[1][:, 0:N]], [at])
        if parts < 2:
            return
        wd = w_dn[which].h.rearrange("(f p) n -> p f n", p=128)
        for f in range(NF):
            s = wdr_i[0] % 4
            wdr_i[0] += 1
            ck = ("dn", which, f)
            if ck in cached:
                P.dma("sp", WDR[:, s, :], dram_v(sc_dn[which], sc_dn[which].h[f], "_%d" % f), "hdr%d" % s)
            else:
                P.dma("pool", WDR[:, s, :], dram_v(w_dn[which], wd[:, f, :]), "wdr%d" % s)
                P.dma("sp", dram_v(sc_dn[which], sc_dn[which].h[f], "_%d" % f), WDR[:, s, :], "wbd%d" % s)
                cached.add(ck)
            for j in range(n):
                for h in range(2):
                    b = BK[j * 2 + h]
                    la = V(SCR.h[:, f * 512 + j * 128: f * 512 + (j + 1) * 128], "SCR", f * 512 + j * 128, f * 512 + (j + 1) * 128)
                    rw = WDR[:, s, h * 512:(h + 1) * 512]
                    P.op("pe", lambda e, o=b[:].ap, l=la.ap, r=rw.ap, f=f: e.matmul(o, lhsT=l, rhs=r, start=(f == 0), stop=(f == NF - 1)),
                         [la, rw], [b[:]], sig=(f == NF - 1 or (j == n - 1 and h == 1)))
        if parts < 3:
            return
        order = list(range(n))
        if hook is not None and n == 4:
            j = 3
            post_norm_add(groups[j], [BK[j * 2], BK[j * 2 + 1]], 0 if which == 0 else 2, 0.5, j % 2)
            order = [0, 1, 2]
        if hook is not None:
            hook()
        for j in order:
            post_norm_add(groups[j], [BK[j * 2], BK[j * 2 + 1]], 0 if which == 0 else 2, 0.5, j % 2)

    kv_i = [0]

    def kv_proj(groups):
        rms_to_T(groups, 1, xnT)
        s_ = wgu_i[0] % 3
        wgu_i[0] += 1
        wi = w_in.h.rearrange("(c p) n -> p c n", p=128)
        for hh in range(2):
            P.dma("pool", WGU[:, s_, :, hh * 128:(hh + 1) * 128],
                  dram_v(w_in, wi[:, :, 512 + hh * 128: 512 + (hh + 1) * 128]), "wgu%d_%d" % (s_, hh))
        for j, g in enumerate(groups):
            if g not in (1, NG - 1):
                continue
            b = BK[ring_i[0] % 3]
            ring_i[0] += 1
            for c in range(8):
                la = xnT[:, c, j * 128:(j + 1) * 128]
                rw = WGU[:, s_, c, :]
                P.op("pe", lambda e, o=b[:, 0:256].ap, l=la.ap, r=rw.ap, c=c: e.matmul(o, lhsT=l, rhs=r, start=(c == 0), stop=(c == 7)),
                     [la, rw], [b[:, 0:256]], sig=(c == 7))
            sl = kv_i[0] % 2
            kv_i[0] += 1
            kvs = KV[:, sl, :]
            P.op("act", lambda e, o=kvs.ap, i=b[:, 0:256].ap: e.activation(out=o, in_=i, func=AF.Copy), [b[:, 0:256]], [kvs])
            P.op("dve", lambda e, o=kvs.ap, b_=bkv[:].ap: e.tensor_tensor(out=o, in0=o, in1=b_, op=ALU.add), [kvs, bkv[:]], [kvs])
            if g == NG - 1:
                P.dma("sp", dram_v(kwin_p), KV[:, sl, 0:128], "kvout")
                P.dma("sp", dram_v(vwin_p), KV[:, sl, 128:256], "kvout")
            else:
                for bq in range(16):
                    P.dma("sp", dram_v(kwin_s, kwin_s.h[bq, 120:128, :], "_%d" % bq),
                          V(KV.h[bq * 8:(bq + 1) * 8, sl, 0:128], "KV", sl * 256, sl * 256 + 128), "kvout")
                    P.dma("sp", dram_v(vwin_s, vwin_s.h[bq, 120:128, :], "_%d" % bq),
                          V(KV.h[bq * 8:(bq + 1) * 8, sl, 128:256], "KV", sl * 256 + 128, sl * 256 + 256), "kvout")

    bank_i = [0]

    reserved = set()

    def nb():
        while True:
            i = bank_i[0] % 7
            bank_i[0] += 1
            if i not in reserved:
                return BK[i]

    wi_ap = w_in.h.rearrange("(c p) n -> p c n", p=128)

    def win_block(blk):
        s_ = wgu_i[0] % 3
        wgu_i[0] += 1
        slot2d = V(WGU.h[:, s_, :, :].rearrange("p c n -> p (c n)"), "WGU", WGU[:, s_, :, :].lo, WGU[:, s_, :, :].hi)
        ck = ("in", blk)
        if ck in cached:
            P.dma("sp", slot2d, dram_v(sc_in, sc_in.h[blk], "_%d" % blk), "hgu%d" % s_)
        else:
            for hh in range(2):
                P.dma("pool", WGU[:, s_, :, hh * 128:(hh + 1) * 128],
                      dram_v(w_in, wi_ap[:, :, blk * 256 + hh * 128: blk * 256 + (hh + 1) * 128]), "wgu%d_%d" % (s_, hh))
            P.dma("sp", dram_v(sc_in, sc_in.h[blk], "_%d" % blk), slot2d, "wbg%d" % s_)
            cached.add(ck)
        return s_

    def tm_proj(s_, j, bank, col0):
        for c in range(8):
            la = xnT[:, c, j * 128:(j + 1) * 128]
            rw = WGU[:, s_, c, :]
            P.op("pe", lambda e, o=bank[:, col0:col0 + 256].ap, l=la.ap, r=rw.ap, c=c: e.matmul(o, lhsT=l, rhs=r, start=(c == 0), stop=(c == 7)),
                 [la, rw], [bank[:, col0:col0 + 256]], sig=(c == 7))

    def A(eng, fn, reads, writes):
        P.op(eng, fn, reads, writes)

    def fm_proj(s_, half, N, bank, col0=0):
        for c in range(8):
            lw = WGU[:, s_, c, half * 128:(half + 1) * 128]
            rx = xnT[:, c, col0:col0 + N]
            P.op("pe", lambda e, o=bank[:, 0:N].ap, l=lw.ap, r=rx.ap, c=c: e.matmul(o, lhsT=l, rhs=r, start=(c == 0), stop=(c == 7)),
                 [lw, rx], [bank[:, 0:N]], sig=(c == 7))

    def hgrn_head_feat(h, N, ng, s_hf, s_hq, phase, rm, col0=0, csz=64):
        nch = N // csz
        bf = nb()
        fm_proj(s_hf, h % 2, N, bf, col0)
        a_, b_, c_, d_, e_, f_ = [HT[:, i, 0:N] for i in range(6)]
        bcol = binF[:, 10 + h:11 + h]
        A("act", lambda e, o=a_.ap, i=bf[:, 0:N].ap, b=bcol.ap: e.activation(out=o, in_=i, func=AF.Sigmoid, bias=b),
          [bf[:, 0:N], bcol], [a_])
        om, lb_ = lbt[:, 2, h:h + 1], lbt[:, 1, h:h + 1]
        A("dve", lambda e, o=a_.ap, s1=om.ap, s2=lb_.ap: e.tensor_scalar(out=o, in0=o, scalar1=s1, scalar2=s2, op0=ALU.mult, op1=ALU.add),
          [a_, om, lb_], [a_])
        A("act", lambda e, o=b_.ap, i=a_.ap: e.activation(out=o, in_=i, func=AF.Ln), [a_], [b_])
        A("dve", lambda e, o=d_.ap, i=a_.ap: e.tensor_scalar(out=o, in0=i, scalar1=-1.0, scalar2=1.0, op0=ALU.mult, op1=ALU.add),
          [a_], [d_])
        rmv = rm[:, 0:N]
        A("dve", lambda e, o=c_.ap, m=rmv.ap, i=b_.ap: e.tensor_tensor_scan(out=o, data0=m, data1=i, initial=0.0, op0=ALU.mult, op1=ALU.add),
          [rmv, b_], [c_])
        c_last = c_.ap.rearrange("p (c t) -> p c t", t=csz)[:, :, csz - 1]
        dch = DCH[:, h, 0:nch]
        A("act", lambda e, o=dch.ap, i=c_last: e.activation(out=o, in_=i, func=AF.Exp), [c_], [dch])
        A("act", lambda e, o=b_.ap, i=c_.ap: e.activation(out=o, in_=i, func=AF.Exp, scale=-1.0), [c_], [b_])
        A("dve", lambda e, o=d_.ap, i=b_.ap: e.tensor_tensor(out=o, in0=o, in1=i, op=ALU.mult), [d_, b_], [d_])
        if phase >= 2:
            kt = ktT[:, h, 0:N]
            A("act", lambda e, o=kt.ap, i=d_.ap: e.activation(out=o, in_=i, func=AF.Copy), [d_], [kt])
        kh = khT[:, h, 0:N]
        kh3 = kh.ap.rearrange("p (c t) -> p c t", t=csz)
        d3 = d_.ap.rearrange("p (c t) -> p c t", t=csz)
        dcb = dch.ap.unsqueeze(2).to_broadcast([128, nch, csz])
        A("dve", lambda e, o=kh3, i=d3, b=dcb: e.tensor_tensor(out=o, in0=i, in1=b, op=ALU.mult), [d_, dch], [kh])
        if phase >= 2:
            bq = nb()
            fm_proj(s_hq, h % 2, N, bq, col0)
            qcol = binF[:, 6 + h:7 + h]
            A("act", lambda e, o=f_.ap, i=bq[:, 0:N].ap, b=qcol.ap: e.activation(out=o, in_=i, func=AF.Silu, bias=b),
              [bq[:, 0:N], qcol], [f_])
            A("act", lambda e, o=e_.ap, i=c_.ap: e.activation(out=o, in_=i, func=AF.Exp), [c_], [e_])
            if phase == 2:
                A("dve", lambda e, o=f_.ap, b=e_.ap: e.tensor_tensor(out=o, in0=o, in1=b, op=ALU.mult), [f_, e_], [f_])
                f3 = f_.ap.rearrange("p (g t) -> p g t", t=128)
                for par, qq in ((0, qtT0), (1, qtT1)):
                    qt = qq[:, h, 0:N]
                    q3 = qt.ap.rearrange("p (g t) -> p g t", t=128)
                    cp = cpar[:, par, :]
                    cpb = cp.ap.unsqueeze(1).to_broadcast([128, ng, 128])
                    A("dve", lambda e, o=q3, i=f3, m=cpb: e.tensor_tensor(out=o, in0=i, in1=m, op=ALU.mult), [f_, cp], [qt])
            else:
                qt = qtT0[:, h, 0:N]
                A("dve", lambda e, o=qt.ap, i=f_.ap, b=e_.ap: e.tensor_tensor(out=o, in0=i, in1=b, op=ALU.mult), [f_, e_], [qt])
        for j in range(ng):
            src = khT[:, h, j * 128:(j + 1) * 128]
            dst = TPB(j * 128, (j + 1) * 128)
            P.op("pe", lambda e, o=dst.ap, i=src.ap: e.transpose(out=o, in_=i, identity=identb[:].ap),
                 [src, identb[:]], [dst], sig=(j == ng - 1))
        tp = TPB(0, ng * 128)
        tp3 = bk7b[:, 0:ng * 128].rearrange("p (g k) -> p g k", g=ng)
        kd = khat_tok[:, 0:ng, h, :]
        A("act", lambda e, o=kd.ap, i=tp3: e.activation(out=o, in_=i, func=AF.Copy), [tp], [kd])

    def hi_proj(ng, s7, s8, js=None):
        for j in (range(ng) if js is None else js):
            bh = nb()
            tm_proj(s7, j, bh, 0)
            tm_proj(s8, j, bh, 256)
            ht = hi_tok[:, j, :]
            A("dve", lambda e, o=ht.ap, b=bhi[:].ap, i=bh[:].ap: e.tensor_tensor(out=o, in0=b, in1=i, op=ALU.add),
              [bhi[:], bh[:]], [ht])

    sver = [0]

    def state_update(j, ci, phase):
        pb = nb()
        for h in range(4):
            la = khat_tok[ci * 64:(ci + 1) * 64, j, h, :]
            rv = hi_tok[ci * 64:(ci + 1) * 64, j, h * 128:(h + 1) * 128]
            ob = pb[:, h * 128:(h + 1) * 128]
            P.op("pe", lambda e, o=ob.ap, l=la.ap, r=rv.ap: e.matmul(o, lhsT=l, rhs=r, start=True, stop=True),
                 [la, rv], [ob], sig=(h == 3))
        t32 = HT[:, 0, :]
        A("act", lambda e, o=t32.ap, i=pb[:].ap: e.activation(out=o, in_=i, func=AF.Copy), [pb[:]], [t32])
        ch = j * 2 + ci
        dsel = DCH[:, :, ch:ch + 1]
        db = dsel.ap.to_broadcast([128, 4, 128])
        A("dve", lambda e, o=S[:].ap, b=db: e.tensor_tensor(out=o, in0=o, in1=b, op=ALU.mult), [S[:], dsel], [S[:]])
        t3 = t32.ap.rearrange("p (h v) -> p h v", h=4)
        A("dve", lambda e, o=S[:].ap, i=t3: e.tensor_tensor(out=o, in0=o, in1=i, op=ALU.add), [S[:], t32], [S[:]])
        if phase == 2:
            sver[0] += 1
            sb_ = Sbf[:, sver[0] % 3, :, :]
            A("act", lambda e, o=sb_.ap, i=S[:].ap: e.activation(out=o, in_=i, func=AF.Copy), [S[:]], [sb_])

    def hgrn_scan_tile(tl, hook=None):
        ng = len(tl)
        N = 128 * ng
        rms_to_T(tl, 1, xnT)
        s7 = win_block(7)
        s8 = win_block(8)
        hi_proj(ng, s7, s8)
        s5 = win_block(5)
        hgrn_head_feat(0, N, ng, s5, None, 1, rmask)
        hgrn_head_feat(1, N, ng, s5, None, 1, rmask)
        s6 = win_block(6)
        hgrn_head_feat(2, N, ng, s6, None, 1, rmask)
        hgrn_head_feat(3, N, ng, s6, None, 1, rmask)
        if hook is not None:
            hook()
        for j in range(ng):
            for ci in range(2):
                state_update(j, ci, 1)

    def exchange():
        if os.environ.get("NOEXCH"):
            return
        P.dma("sp", dram_v(cc_in), S[:], "ccio")
        P.cnt_cc = getattr(P, "cnt_cc", 0) + 1
        tok = ("cc", P.cnt_cc)
        deps = P._deps([dram_v(cc_in)], [dram_v(cc_out)], tok)
        for k_, v_ in P.dma_cnt.items():
            deps.add((k_, v_))
        P.q["pool"].append((deps, lambda e: e.collective_compute("AllGather", ALU.bypass, replica_groups=[list(range(8))],
                                                                   ins=[cc_in.h], outs=[cc_out.h]), "cc", 1))
        P.dma("sp", G8[:], dram_v(cc_out, cc_out.h.rearrange("(r p) f -> p r f", p=128)), "ccio")
        s2 = S[:].ap.rearrange("p h v -> p (h v)")
        g0 = G8[:, 0, :]
        A("dve", lambda e, o=s2, i=g0.ap, c=sel[:, 0:1].ap: e.tensor_scalar(out=o, in0=i, scalar1=c, scalar2=None, op0=ALU.mult),
          [g0, sel[:]], [S[:]])
        for r_ in range(1, 8):
            gr = G8[:, r_, :]
            A("dve", lambda e, o=gr.ap, c=sel[:, r_:r_ + 1].ap: e.tensor_scalar(out=o, in0=o, scalar1=c, scalar2=None, op0=ALU.mult),
              [gr, sel[:]], [gr])
            A("dve", lambda e, o=s2, i=gr.ap: e.tensor_tensor(out=o, in0=o, in1=i, op=ALU.add), [S[:], gr], [S[:]])

    def kv_part(tl, ng, N):
        if SUB < 2:
            return
        s2 = win_block(2)
        for kv in range(2):
            for dup in range(2):
                srcw = WGU[:, s2, :, kv * 64:(kv + 1) * 64]
                dstw = WK2[:, :, kv, dup * 64:(dup + 1) * 64]
                A("dve", lambda e, o=dstw.ap, i=srcw.ap: e.tensor_copy(out=o, in_=i), [srcw], [dstw])
        for kv in range(2):
            bkk = nb()
            for c in range(8):
                lw = WK2[:, c, kv, :]
                rx = xnT[:, c, 0:N]
                ob = bkk[:, 0:N]
                P.op("pe", lambda e, o=ob.ap, l=lw.ap, r=rx.ap, c=c: e.matmul(o, lhsT=l, rhs=r, start=(c == 0), stop=(c == 7)),
                     [lw, rx], [ob], sig=(c == 7))
            kd = kT2[:, kv, 128:128 + N]
            bc = bk2[:, kv:kv + 1]
            A("act", lambda e, o=kd.ap, i=bkk[:, 0:N].ap, b=bc.ap: e.activation(out=o, in_=i, func=AF.Identity, bias=b),
              [bkk[:, 0:N], bc], [kd])
        if SUB < 3:
            return
        for j, g in enumerate(tl):
            bv = nb()
            tm_proj(s2, j, bv, 0)
            vd = V1[:, 1 + j, :, 0:64]
            if g not in (1, NG - 1):
                bvv = bkv[:, 128:256].ap.rearrange("p (k d) -> p k d", k=2)
                pv3 = bv[:, 128:256].ap.rearrange("p (k d) -> p k d", k=2)
                A("dve", lambda e, o=vd.ap, b=bvv, i=pv3: e.tensor_tensor(out=o, in0=b, in1=i, op=ALU.add),
                  [bkv[:], bv[:, 128:256]], [vd])
            else:
                sl = kv_i[0] % 2
                kv_i[0] += 1
                kvs = KV[:, sl, :]
                A("act", lambda e, o=kvs.ap, i=bv[:, 0:256].ap: e.activation(out=o, in_=i, func=AF.Copy), [bv[:, 0:256]], [kvs])
                A("dve", lambda e, o=kvs.ap, b_=bkv[:].ap: e.tensor_tensor(out=o, in0=o, in1=b_, op=ALU.add), [kvs, bkv[:]], [kvs])
                kv3 = KV[:, sl, 128:256]
                A("dve", lambda e, o=vd.ap, i=kv3.ap.rearrange("p (k d) -> p k d", k=2): e.tensor_copy(out=o, in_=i), [kv3], [vd])
                KVDMA = int(os.environ.get("KVDMA", "2"))
                if g == NG - 1:
                    if KVDMA >= 1:
                        P.dma("sp", dram_v(kwin_p), KV[:, sl, 0:128], "kvout")
                        P.dma("sp", dram_v(vwin_p), KV[:, sl, 128:256], "kvout")
                elif KVDMA >= 2:
                    for bq in range(16):
                        P.dma("sp", dram_v(kwin_s, kwin_s.h[bq, 120:128, :], "_%d" % bq),
                              V(KV.h[bq * 8:(bq + 1) * 8, sl, 0:128], "KV", sl * 256, sl * 256 + 128), "kvout")
                        P.dma("sp", dram_v(vwin_s, vwin_s.h[bq, 120:128, :], "_%d" % bq),
                              V(KV.h[bq * 8:(bq + 1) * 8, sl, 128:256], "KV", sl * 256 + 128, sl * 256 + 256), "kvout")

    def carry_prev(ng, N):
        if SUB < 5:
            return
        src = kT2[:, :, N:N + 128]
        dst = kT2[:, :, 0:128]
        A("dve", lambda e, o=dst.ap, i=src.ap: e.tensor_copy(out=o, in_=i), [src], [dst])
        sv, dv = V1[:, ng, :, :], V1[:, 0, :, :]
        A("dve", lambda e, o=dv.ap, i=sv.ap: e.tensor_copy(out=o, in_=i), [sv], [dv])

    def q_part(N):
        for blk in (0, 1):
            sq = win_block(blk)
            for half in range(2):
                cc = blk * 2 + half
                bq = nb()
                fm_proj(sq, half, N, bq)
                qd = qT[:, cc, 0:N]
                bc = binF[:, cc:cc + 1]
                A("act", lambda e, o=qd.ap, i=bq[:, 0:N].ap, b=bc.ap: e.activation(out=o, in_=i, func=AF.Identity, bias=b),
                  [bq[:, 0:N], bc], [qd])

    def attention_group(j, g):
        for kv in range(2):
            for blk in range(2):
                bs = nb()
                c0 = j * 128 + blk * 128
                for hl in range(4):
                    h = kv * 4 + hl
                    p0 = (h % 2) * 64
                    ob = bs[:, hl * 128:(hl + 1) * 128]
                    mk = maskH[:] if (g == 2 and blk == 0) else maskP[:, blk, :]
                    P.op("pe", lambda e, o=ob.ap, r=mk.ap: e.matmul(o, lhsT=identb[:].ap, rhs=r, start=True, stop=False),
                         [identb[:], mk], [ob], sig=False)
                    lk = kT2[p0:p0 + 64, kv, c0:c0 + 128]
                    rq = qT[p0:p0 + 64, h // 2, j * 128:(j + 1) * 128]
                    P.op("pe", lambda e, o=ob.ap, l=lk.ap, r=rq.ap: e.matmul(o, lhsT=l, rhs=r, start=False, stop=True),
                         [lk, rq], [ob], sig=(hl == 3))
                pt = PT[:, blk, :]
                A("act", lambda e, o=pt.ap, i=bs[:].ap: e.activation(out=o, in_=i, func=AF.Exp, scale=SCALE), [bs[:]], [pt])
            bov = nb()
            for hl in range(4):
                ob = bov[:, hl * 66:hl * 66 + 65]
                for blk in range(2):
                    lp = PT[:, blk, hl * 128:(hl + 1) * 128]
                    rv = V1[:, j + blk, kv, 0:65]
                    P.op("pe", lambda e, o=ob.ap, l=lp.ap, r=rv.ap, blk=blk: e.matmul(o, lhsT=l, rhs=r, start=(blk == 0), stop=(blk == 1)),
                         [lp, rv], [ob], sig=(hl == 3 and blk == 1))
            oa2 = OA[:].ap.rearrange("p h d -> p (h d)")
            A("act", lambda e, o=oa2, i=bov[:, 0:264].ap: e.activation(out=o, in_=i, func=AF.Copy), [bov[:, 0:264]], [OA[:]])
            attn_norm(kv)
        attn_finish()

    def attn_norm(kv):
        if True:
            dn = stcol(4)
            es4 = esink[:, kv * 4:(kv + 1) * 4]
            A("dve", lambda e, o=dn.ap, i=OA[:, :, 64].ap, b=es4.ap: e.tensor_tensor(out=o, in0=i, in1=b, op=ALU.add), [OA[:], es4], [dn])
            rd = stcol(4)
            A("dve", lambda e, o=rd.ap, i=dn.ap: e.reciprocal(out=o, in_=i), [dn], [rd])
            ad = a_tok[:, kv * 256:(kv + 1) * 256]
            ad3 = ad.ap.rearrange("p (h d) -> p h d", h=4)
            rb = rd.ap.unsqueeze(2).to_broadcast([128, 4, 64])
            A("dve", lambda e, o=ad3, i=OA[:, :, 0:64].ap, b=rb: e.tensor_tensor(out=o, in0=i, in1=b, op=ALU.mult), [OA[:], rd], [ad])

    def attn_finish():
        ssa = stcol()
        A("act", lambda e, o=junk[:, 0:512].ap, i=a_tok[:].ap, a=ssa.ap: e.activation(out=o, in_=i, func=AF.Square, accum_out=a),
          [a_tok[:]], [junk[:, 0:512], ssa])
        ra = rstd_of(ssa, 512, 1.0)
        A("dve", lambda e, o=a_tok[:].ap, s_=ra.ap: e.tensor_scalar(out=o, in0=o, scalar1=s_, scalar2=None, op0=ALU.mult), [a_tok[:], ra], [a_tok[:]])
        md = mix_tok[:, 0:512]
        A("dve", lambda e, o=md.ap, i=a_tok[:].ap, g_=gattn[:].ap: e.tensor_tensor(out=o, in0=i, in1=g_, op=ALU.mult), [a_tok[:], gattn[:]], [md])

    def gate_group(j, s9, s10):
        bg = nb()
        tm_proj(s9, j, bg, 0)
        tm_proj(s10, j, bg, 256)
        A("act", lambda e, o=gate[:].ap, i=bg[:].ap: e.activation(out=o, in_=i, func=AF.Copy), [bg[:]], [gate[:]])
        A("dve", lambda e, o=gate[:].ap, b=bhg[:].ap: e.tensor_tensor(out=o, in0=o, in1=b, op=ALU.add), [gate[:], bhg[:]], [gate[:]])
        A("act", lambda e, o=gate[:].ap: e.activation(out=o, in_=o, func=AF.Silu), [gate[:]], [gate[:]])

    def hgrn_out_group(j, s9, s10):
        gate_group(j, s9, s10)
        gs = slice(j * 128, (j + 1) * 128)
        for h in range(4):
            ba = nb()
            lk = ktT[:, h, gs]
            for par, qq in ((0, qtT0), (1, qtT1)):
                rq = qq[:, h, gs]
                P.op("pe", lambda e, o=ba[:, 0:128].ap, l=lk.ap, r=rq.ap, par=par: e.matmul(o, lhsT=l, rhs=r, start=(par == 0), stop=(par == 1)),
                     [lk, rq], [ba[:, 0:128]], sig=(par == 1))
            at = AT[:, h, :]
            A("dve", lambda e, o=at.ap, m=hm_cur[0][:].ap, i=ba[:, 0:128].ap: e.tensor_tensor(out=o, in0=m, in1=i, op=ALU.mult),
              [hm_cur[0][:], ba[:, 0:128]], [at])
        vers = [sver[0] % 3]
        for ci in range(2):
            state_update(j, ci, 2)
            vers.append(sver[0] % 3)
        bo = nb()
        for h in range(4):
            ob = bo[:, h * 128:(h + 1) * 128]
            for ci, qq in ((0, qtT0), (1, qtT1)):
                lq = qq[:, h, gs]
                rs = Sbf[:, vers[ci], h, :]
                P.op("pe", lambda e, o=ob.ap, l=lq.ap, r=rs.ap, ci=ci: e.matmul(o, lhsT=l, rhs=r, start=(ci == 0), stop=False),
                     [lq, rs], [ob], sig=False)
            la = AT[:, h, :]
            rv = hi_tok[:, j, h * 128:(h + 1) * 128]
            P.op("pe", lambda e, o=ob.ap, l=la.ap, r=rv.ap: e.matmul(o, lhsT=l, rhs=r, start=False, stop=True),
                 [la, rv], [ob], sig=True)
        hgrn_finish(bo)

    def hgrn_finish(bo):
        A("act", lambda e, o=o_tok[:].ap, i=bo[:].ap: e.activation(out=o, in_=i, func=AF.Copy), [bo[:]], [o_tok[:]])
        ss4 = stcol(4)
        for h in range(4):
            sh = V(ss4.ap[:, h:h + 1], ss4.name, ss4.lo + h, ss4.lo + h + 1)
            oh = o_tok[:, h * 128:(h + 1) * 128]
            A("act", lambda e, o=junk[:, 0:128].ap, i=oh.ap, a=sh.ap: e.activation(out=o, in_=i, func=AF.Square, accum_out=a),
              [oh], [junk[:, 0:128], sh])
        r4 = rstd_of(ss4, 128, 1.0, w=4)
        o3 = o_tok[:].ap.rearrange("p (h v) -> p h v", h=4)
        r4b = r4.ap.unsqueeze(2).to_broadcast([128, 4, 128])
        A("dve", lambda e, o=o3, b=r4b: e.tensor_tensor(out=o, in0=o, in1=b, op=ALU.mult), [o_tok[:], r4], [o_tok[:]])
        A("dve", lambda e, o=o_tok[:].ap, b=ghg4[:].ap: e.tensor_tensor(out=o, in0=o, in1=b, op=ALU.mult), [o_tok[:], ghg4[:]], [o_tok[:]])
        md = mix_tok[:, 512:1024]
        A("dve", lambda e, o=md.ap, i=o_tok[:].ap, g_=gate[:].ap: e.tensor_tensor(out=o, in0=i, in1=g_, op=ALU.mult), [o_tok[:], gate[:]], [md])

    hm_cur = [hmask]

    def mix_to_T(j):
        for c in range(8):
            src = mix_tok[:, c * 128:(c + 1) * 128]
            dst = TPB(c * 128, (c + 1) * 128)
            P.op("pe", lambda e, o=dst.ap, i=src.ap: e.transpose(out=o, in_=i, identity=identb[:].ap),
                 [src, identb[:]], [dst], sig=(c == 7))
        tp = TPB(0, 1024)
        tp3 = bk7b[:, 0:1024].rearrange("p (c t) -> p c t", c=8)
        d3 = xnT[:, :, j * 128:(j + 1) * 128]
        A("act", lambda e, o=d3.ap, i=tp3: e.activation(out=o, in_=i, func=AF.Copy), [tp], [d3])

    def out_proj(tl, js):
        P.dma("sp", GP[:], dram_v(gpost, gpost.h[1]), "gp")
        for c in range(8):
            s_ = wdr_i[0] % 4
            wdr_i[0] += 1
            ck = ("out", c)
            if ck in cached:
                P.dma("sp", WDR[:, s_, :], dram_v(sc_out, sc_out.h[c], "_%d" % c), "hdr%d" % s_)
            else:
                P.dma("pool", WDR[:, s_, :], dram_v(w_out, w_out.h[c * 128:(c + 1) * 128, :]), "wdr%d" % s_)
                P.dma("sp", dram_v(sc_out, sc_out.h[c], "_%d" % c), WDR[:, s_, :], "wbd%d" % s_)
                cached.add(ck)
            for j in js:
                for h in range(2):
                    b = BK[j * 2 + h]
                    la = xnT[:, c, j * 128:(j + 1) * 128]
                    rw = WDR[:, s_, h * 512:(h + 1) * 512]
                    P.op("pe", lambda e, o=b[:].ap, l=la.ap, r=rw.ap, c=c: e.matmul(o, lhsT=l, rhs=r, start=(c == 0), stop=(c == 7)),
                         [la, rw], [b[:]], sig=(c == 7 or (j == js[-1] and h == 1)))
        for j in js:
            post_norm_add(tl[j], [BK[j * 2], BK[j * 2 + 1]], 1, 1.0, j % 2)

    def sample_hgrn(j, s9, s10):
        gate_group(j, s9, s10)
        A("dve", lambda e: e.memset(qexp[:].ap, 0.0), [], [qexp[:]])
        bo = BK[6]
        reserved.add(6)
        for h in range(4):
            P.dma("sp", Sb32[:], dram_v(st_in, st_in.h[:, h].rearrange("b k v -> k b v")), "sb32")
            sbf2 = Sbbf[:].ap.rearrange("p b v -> p (b v)")
            s322 = Sb32[:].ap.rearrange("p b v -> p (b v)")
            A("act", lambda e, o=sbf2, i=s322: e.activation(out=o, in_=i, func=AF.Copy), [Sb32[:]], [Sbbf[:]])
            qsrc = qtT0[:, h, 0:128]
            qd = qexp[:].ap[:, 0:16 * 136].rearrange("p (b x) -> p b x", x=136)[:, :, 0:8]
            A("dve", lambda e, o=qd, i=qsrc.ap.rearrange("p (b t) -> p b t", t=8): e.tensor_copy(out=o, in_=i), [qsrc], [qexp[:]])
            ba = nb()
            lk = ktT[:, h, 0:128]
            P.op("pe", lambda e, o=ba[:, 0:128].ap, l=lk.ap, r=qsrc.ap: e.matmul(o, lhsT=l, rhs=r, start=True, stop=True),
                 [lk, qsrc], [ba[:, 0:128]])
            at = AT[:, h, :]
            A("dve", lambda e, o=at.ap, m=hmaskS[:].ap, i=ba[:, 0:128].ap: e.tensor_tensor(out=o, in0=m, in1=i, op=ALU.mult),
              [hmaskS[:], ba[:, 0:128]], [at])
            ob = bo[:, h * 128:(h + 1) * 128]
            for b in range(16):
                lq = qexp[:, b * 128:(b + 1) * 128]
                rs = Sbbf[:, b, :]
                P.op("pe", lambda e, o=ob.ap, l=lq.ap, r=rs.ap, b=b: e.matmul(o, lhsT=l, rhs=r, start=(b == 0), stop=False),
                     [lq, rs], [ob], sig=False)
            rv = hi_tok[:, j, h * 128:(h + 1) * 128]
            P.op("pe", lambda e, o=ob.ap, l=at.ap, r=rv.ap: e.matmul(o, lhsT=l, rhs=r, start=False, stop=True), [at, rv], [ob])
            lkh = khat_tok[:, 0, h, :]
            for b4 in range(4):
                pb = nb()
                for bb in range(4):
                    b = b4 * 4 + bb
                    vm = vmk[:, bb, :]
                    sc = seqsel[:, b:b + 1]
                    A("dve", lambda e, o=vm.ap, i=rv.ap, c=sc.ap: e.tensor_scalar(out=o, in0=i, scalar1=c, scalar2=None, op0=ALU.mult),
                      [rv, sc], [vm])
                    pbr = pb[:, bb * 128:(bb + 1) * 128]
                    P.op("pe", lambda e, o=pbr.ap, l=lkh.ap, r=vm.ap: e.matmul(o, lhsT=l, rhs=r, start=True, stop=True), [lkh, vm], [pbr])
                t32 = HT[:, 0, :]
                A("act", lambda e, o=t32.ap, i=pb[:].ap: e.activation(out=o, in_=i, func=AF.Copy), [pb[:]], [t32])
                s4 = Sb32[:, b4 * 4:(b4 + 1) * 4, :]
                dsel = DCH[:, h, b4 * 4:(b4 + 1) * 4]
                db = dsel.ap.unsqueeze(2).to_broadcast([128, 4, 128])
                A("dve", lambda e, o=s4.ap, b_=db: e.tensor_tensor(out=o, in0=o, in1=b_, op=ALU.mult), [s4, dsel], [s4])
                t3 = t32.ap.rearrange("p (b v) -> p b v", b=4)
                A("dve", lambda e, o=s4.ap, i=t3: e.tensor_tensor(out=o, in0=o, in1=i, op=ALU.add), [s4, t32], [s4])
            P.dma("sp", dram_v(state_s, state_s.h[:, h].rearrange("b k v -> k b v"), "_%d" % h), Sb32[:], "sbo")
        reserved.discard(6)
        hgrn_finish(bo)

    def sample_attention(j):
        qc = slice(j * 128, (j + 1) * 128)
        ring = [0]
        for kv in range(2):
            accs = [BK[3 + hl] for hl in range(4)]
            for blk in range(17):
                if blk < 16:
                    r3 = ring[0] % 3
                    r2 = ring[0] % 2
                    ring[0] += 1
                    for dup in range(2):
                        P.dma("pool", Kd[:, r3, dup * 64:(dup + 1) * 64],
                              dram_v(cache_k, cache_k.h[blk, :, kv * 64:(kv + 1) * 64]), "kd%d_%d" % (r3, dup))
                    P.dma("pool", Vb[:, r3, 0:64], dram_v(cache_v, cache_v.h[blk, :, kv * 64:(kv + 1) * 64]), "vb%d" % r3)
                    src = Kd[:, r3, :]
                    dst = TPB(0, 128)
                    P.op("pe", lambda e, o=dst.ap, i=src.ap: e.transpose(out=o, in_=i, identity=identb[:].ap), [src, identb[:]], [dst])
                    kc = kcT[:, r2, :]
                    A("act", lambda e, o=kc.ap, i=dst.ap: e.activation(out=o, in_=i, func=AF.Copy), [dst], [kc])
                    kview = lambda p0, r2=r2: kcT[p0:p0 + 64, r2, :]
                    rv = Vb[:, r3, 0:65]
                    mk = Zm[:, 120 - 8 * blk: 120 - 8 * blk + 128]
                else:
                    kview = lambda p0, kv=kv: kT2[p0:p0 + 64, kv, 128 + j * 128: 128 + (j + 1) * 128]
                    rv = V1[:, 1 + j, kv, 0:65]
                    mk = newmask[:]
                bs = BK[blk % 2]
                for hl in range(4):
                    h = kv * 4 + hl
                    p0 = (h % 2) * 64
                    ob = bs[:, hl * 128:(hl + 1) * 128]
                    P.op("pe", lambda e, o=ob.ap, r=mk.ap: e.matmul(o, lhsT=identb[:].ap, rhs=r, start=True, stop=False),
                         [identb[:], mk], [ob], sig=False)
                    lk = kview(p0)
                    rq = qT[p0:p0 + 64, h // 2, qc]
                    P.op("pe", lambda e, o=ob.ap, l=lk.ap, r=rq.ap: e.matmul(o, lhsT=l, rhs=r, start=False, stop=True),
                         [lk, rq], [ob], sig=(hl == 3))
                pt = PT[:, blk % 2, :]
                A("act", lambda e, o=pt.ap, i=bs[:].ap: e.activation(out=o, in_=i, func=AF.Exp, scale=SCALE), [bs[:]], [pt])
                for hl in range(4):
                    ob = accs[hl][:, 0:65]
                    lp = PT[:, blk % 2, hl * 128:(hl + 1) * 128]
                    P.op("pe", lambda e, o=ob.ap, l=lp.ap, r=rv.ap, blk=blk: e.matmul(o, lhsT=l, rhs=r, start=(blk == 0), stop=(blk == 16)),
                         [lp, rv], [ob], sig=(hl == 3))
            for hl in range(4):
                od = OA[:, hl, 0:65]
                A("act", lambda e, o=od.ap, i=accs[hl][:, 0:65].ap: e.activation(out=o, in_=i, func=AF.Copy), [accs[hl][:, 0:65]], [od])
            attn_norm(kv)
        attn_finish()

    def mixer_tile(tl):
        ng = len(tl)
        N = 128 * ng
        prenorm(tl, 1)
        kv_part(tl, ng, N)
        if tl[0] == 0:
            carry_prev(1, 128)
            q_part(N)
            s7 = win_block(7)
            s8 = win_block(8)
            hi_proj(ng, s7, s8, js=[1])
            s3 = win_block(3)
            s5 = win_block(5)
            hgrn_head_feat(0, 128, 1, s5, s3, 3, rmaskS, col0=128, csz=8)
            hgrn_head_feat(1, 128, 1, s5, s3, 3, rmaskS, col0=128, csz=8)
            s4 = win_block(4)
            s6 = win_block(6)
            hgrn_head_feat(2, 128, 1, s6, s4, 3, rmaskS, col0=128, csz=8)
            hgrn_head_feat(3, 128, 1, s6, s4, 3, rmaskS, col0=128, csz=8)
            s9 = win_block(9)
            s10 = win_block(10)
            sample_hgrn(1, s9, s10)
            sample_attention(1)
            mix_to_T(1)
            out_proj(tl, [1])
            return
        if MIXLVL < 2:
            carry_prev(ng, N)
            return
        q_part(N)
        s7 = win_block(7)
        s8 = win_block(8)
        hi_proj(ng, s7, s8)
        s3 = win_block(3)
        s5 = win_block(5)
        hgrn_head_feat(0, N, ng, s5, s3, 2, rmask)
        hgrn_head_feat(1, N, ng, s5, s3, 2, rmask)
        s4 = win_block(4)
        s6 = win_block(6)
        hgrn_head_feat(2, N, ng, s6, s4, 2, rmask)
        hgrn_head_feat(3, N, ng, s6, s4, 2, rmask)
        s9 = win_block(9)
        s10 = win_block(10)
        for j, g in enumerate(tl):
            if MIXLVL >= 3:
                hgrn_out_group(j, s9, s10)
            if MIXLVL >= 4:
                attention_group(j, g)
            if MIXLVL >= 5:
                mix_to_T(j)
        carry_prev(ng, N)
        if MIXLVL >= 5:
            out_proj(tl, list(range(ng)))

    for dst_, src_, nm in ((maskP, maskP_in, "ca"), (maskH, maskH_in, "cb"), (Zm, zm_in, "ci"), (newmask, nm_in, "cj")):
        P.dma("pool", dst_[:], dram_v(src_), nm)
    for dst_, src_, nm in ((hmaskS, hmS_in, "ck"), (rmaskS, rmS_in, "cl"), (seqsel, seqsel_in, "cm")):
        P.dma("sp", dst_[:], dram_v(src_), nm)
    A("dve", lambda e: e.memset(Vb[:].ap.rearrange("p a b -> p (a b)"), 1.0), [], [Vb[:]])
    for dst_, src_, nm in ((esink_raw, sink_in, "cg"), (gattn, gattn_in, "cd"), (ghg4, ghg_in, "ce"), (bk2, bk2_in, "cf"),
                           (cpar, cpar_in, "ch")):
        P.dma("sp", dst_[:], dram_v(src_), nm)
    A("act", lambda e: e.activation(out=esink[:].ap, in_=esink_raw[:].ap, func=AF.Exp), [esink_raw[:]], [esink[:]])
    A("dve", lambda e: e.memset(V1[:].ap.rearrange("p a b c -> p (a b c)"), 1.0), [], [V1[:]])
    for dst_, src_, nm in ((binF, binF_in, "c3"), (bhi, bhi_in, "c4"), (bhg, bhg_in, "c5"), (lbl, lbl_in, "c6"),
                           (rmask, rmask_in, "c7"), (hmask, hmask_in, "c8"), (sel, sel_in, "c9")):
        P.dma("sp", dst_[:], dram_v(src_), nm)
    A("dve", lambda e: e.tensor_tensor(out=lbt[:, 0, :].ap, in0=lbl[:, 0, :].ap, in1=lbl[:, 1, :].ap, op=ALU.subtract),
      [lbl[:]], [lbt[:, 0, :]])
    A("act", lambda e: e.activation(out=lbt[:, 1, :].ap, in_=lbt[:, 0, :].ap, func=AF.Sigmoid), [lbt[:, 0, :]], [lbt[:, 1, :]])
    A("dve", lambda e: e.tensor_scalar(out=lbt[:, 2, :].ap, in0=lbt[:, 1, :].ap, scalar1=-1.0, scalar2=1.0, op0=ALU.mult, op1=ALU.add),
      [lbt[:, 1, :]], [lbt[:, 2, :]])
    A("dve", lambda e: e.memset(S[:].ap, 0.0), [], [S[:]])
    P.dma("sp", bkv[:], dram_v(bkv_in), "c2")
    P.dma("sp", dram_v(kwin_s, kwin_s.h[:, 0:120, :], "_c"), dram_v(cache_k, cache_k.h[:, 8:128, :]), "kvck")
    P.dma("sp", dram_v(vwin_s, vwin_s.h[:, 0:120, :], "_c"), dram_v(cache_v, cache_v.h[:, 8:128, :]), "kvcv")
    tiles = [[0, 1]] + [[2 + 4 * t + i for i in range(4)] for t in range(4)]
    if stage == -2:
        rms_to_T(tiles[0], 0, xnT)
    elif stage == -3:
        ffn(tiles[0], 0)
    elif stage == -6:
        ffn(tiles[0], 0, 1)
    elif stage == -7:
        ffn(tiles[0], 0, 2)
    elif stage in (-4, -5):
        wg = w_gu[0].h.rearrange("(c p) n -> p c n", p=128)
        for f in range(3):
            for hh in range(2):
                if stage == -4:
                    P.dma("pool", WGU[:, f, :, hh * 128:(hh + 1) * 128],
                          dram_v(w_gu[0], wg[:, :, hh * DFF + f * 128: hh * DFF + (f + 1) * 128]), "wgu%d_%d" % (f, hh))
                else:
                    for c in range(8):
                        P.dma("pool", WGU[:, f, c, hh * 128:(hh + 1) * 128],
                              dram_v(w_gu[0], w_gu[0].h[c * 128:(c + 1) * 128, hh * DFF + f * 128: hh * DFF + (f + 1) * 128]), "wgu%d_%d" % (f, hh))
    elif stage >= 1:
        if not USE_CC:
            P.dma("sp", pflag[:], dram_v(pflag_in), "cn")
            scratch = tiles[1]
            bg_i = [0]

            def bg_dma(dst_v, src_v):
                bg_queue.append((dst_v, src_v))

            def bg_convert(t):
                if t == 0:
                    wg2 = w_gu[1].h.rearrange("(c p) n -> p c n", p=128)
                    for f in range(NF):
                        d3 = sc_gu[1].h[f].rearrange("p (c n) -> p c n", c=8)
                        for hh in range(2):
                            bg_dma(dram_v(sc_gu[1], d3[:, :, hh * 128:(hh + 1) * 128], "_%d" % f),
                                   dram_v(w_gu[1], wg2[:, :, hh * DFF + f * 128: hh * DFF + (f + 1) * 128]))
                        cached.add(("gu", 1, f))
                elif t == 1:
                    wd2 = w_dn[1].h.rearrange("(f p) n -> p f n", p=128)
                    for f in range(NF):
                        bg_dma(dram_v(sc_dn[1], sc_dn[1].h[f], "_%d" % f), dram_v(w_dn[1], wd2[:, f, :]))
                        cached.add(("dn", 1, f))
                    for c in range(8):
                        bg_dma(dram_v(sc_out, sc_out.h[c], "_%d" % c), dram_v(w_out, w_out.h[c * 128:(c + 1) * 128, :]))
                        cached.add(("out", c))
                elif t == 2:
                    for blk in (2, 0, 1, 3, 4, 9, 10):
                        d3 = sc_in.h[blk].rearrange("p (c n) -> p c n", c=8)
                        for hh in range(2):
                            bg_dma(dram_v(sc_in, d3[:, :, hh * 128:(hh + 1) * 128], "_%d" % blk),
                                   dram_v(w_in, wi_ap[:, :, blk * 256 + hh * 128: blk * 256 + (hh + 1) * 128]))
                        cached.add(("in", blk))

            scr2 = [tiles[1], tiles[2]]

            def load_prev(t):
                for i, g in enumerate(scr2[t % 2]):
                    r0 = t * 512 + i * 128
                    P.dma("sp", X[:, g, :], dram_v(xprev, xprev.h[r0:r0 + 128, :]), "x%d" % g)

            load_prev(0)
            for t in range(4):
                if t < 3:
                    load_prev(t + 1)
                cur = scr2[t % 2]
                nxt = scr2[(t + 1) % 2] if t < 3 else tiles[0]
                ffn(cur, 0)
                hgrn_scan_tile(cur, hook=(lambda nxt=nxt: prenorm_early(nxt, 0)))
            if os.environ.get("NOBG") is None:
                for t in range(3):
                    bg_convert(t)
            s2_ = S[:].ap.rearrange("p h v -> p (h v)")
            A("dve", lambda e, o=s2_, c=pflag[:].ap: e.tensor_scalar(out=o, in0=o, scalar1=c, scalar2=None, op0=ALU.mult),
              [S[:], pflag[:]], [S[:]])
            for g in range(2, NG):
                P.dma("sp", X[:, g, :], dram_v(xin, xin.h[g * 128:(g + 1) * 128, :]), "x%d" % g)
        for ti, tl in enumerate(tiles):
            nxt = tiles[ti + 1] if ti + 1 < len(tiles) else None
            ffn(tl, 0, hook=(lambda nxt=nxt: prenorm_early(nxt, 0)) if (nxt is not None and not os.environ.get("NOHOOK")) else None)
    bg_drain(10 ** 6)
    if stage >= 4:
        if USE_CC:
            for tl in tiles[1:]:
                hgrn_scan_tile(tl)
            exchange()
        A("act", lambda e: e.activation(out=Sbf[:, 0, :, :].ap, in_=S[:].ap, func=AF.Copy), [S[:]], [Sbf[:, 0, :, :]])
        if stage == 4:
            for tl in tiles[1:]:
                hgrn_scan_tile(tl)
            P.dma("sp", dram_v(state_p), S[:], "stout")
    if stage >= 5:
        for ti, tl in enumerate(tiles[:1] if os.environ.get("AUXONLY") else tiles):
            mixer_tile(tl)
            if not os.environ.get("AUXONLY"):
                nxt = tiles[ti + 1] if ti + 1 < len(tiles) else None
                ffn(tl, 1, hook=(lambda nxt=nxt: prenorm_early(nxt, 1)) if (nxt is not None and not os.environ.get("NOHOOK")) else None)
        P.dma("sp", dram_v(state_p), S[:], "stout")
    elif stage >= 2:
        for tl in tiles:
            ffn(tl, 1)
    for g in range(1, NG):
        P.dma("sp", dram_v(y, y.h[(g - 1) * 128: g * 128, :]), X[:, g, :], "yout")

    final = set()
    for k, v in P.dma_cnt.items():
        final.add((k, v))
    for e in ENGS:
        if P.cnt[e]:
            final.add((e, P.cnt[e]))
    P.final = final

    P.check()
    print("ops:", {e: len(P.q[e]) for e in ENGS}, "sems:", len(P.dma_cnt) + 5, flush=True)
    semkeys = list(ENGS) + sorted(P.dma_cnt.keys()) + ["cc"]
    sems = {k: es.enter_context(nc.semaphore("s_" + k.replace(":", "_"))) for k in semkeys}
    with nc.Block() as block:
        @block.tensor
        def _(e):
            P.replay("pe", e, sems)

        @block.scalar
        def _(e):
            P.replay("act", e, sems)

        @block.vector
        def _(e):
            P.replay("dve", e, sems)

        @block.gpsimd
        def _(e):
            P.replay("pool", e, sems)

        @block.sync
        def _(e):
            P.replay("sp", e, sems)
    es.close()
    return nc


def _core_inputs(inp, c):
    f32 = np.float32
    b, half = c // 2, c % 2
    xp = inp["x_prompt"]
    xs = inp["x_sample"]
    xin = np.zeros((NG * 128, D), f32)
    if half == 1:
        xin[0:128] = xp[b, 2048 - 128:2048]
    xin[128:256] = xs[16 * c:16 * c + 16].reshape(128, D)
    xin[256:] = xp[b, half * 2048:(half + 1) * 2048]
    m = {"xin": xin}
    m["w_gu1"] = inp["ffn1_w_gu"][0]
    m["w_gu2"] = inp["ffn2_w_gu"][0]
    m["w_dn1"] = inp["ffn1_w_down"][0]
    m["w_dn2"] = inp["ffn2_w_down"][0]
    m["w_in"] = inp["w_in"][0]
    m["w_out"] = inp["w_out"][0]
    gcols = np.stack([inp["norm_ffn1_pre"][0], inp["norm_mix_pre"][0], inp["norm_ffn2_pre"][0]])
    m["gcols"] = np.ascontiguousarray(gcols.reshape(3, 8, 128).transpose(2, 0, 1))
    gp = np.stack([inp["norm_ffn1_post"][0], inp["norm_mix_post"][0], inp["norm_ffn2_post"][0]])
    m["gpost"] = np.ascontiguousarray(np.broadcast_to(gp[:, None, :], (3, 128, D)))
    m["ident"] = np.eye(128, dtype=f32)
    bi = inp["b_in"][0]
    m["binF"] = bi.reshape(22, 128).T
    m["bhi"] = np.broadcast_to(bi[None, 1792:2304], (128, 512))
    m["bhg"] = np.broadcast_to(bi[None, 2304:2816], (128, 512))
    m["lbl"] = inp["hg_lb_logits"].reshape(2, 4, 128).transpose(2, 0, 1)
    rm = np.ones((128, 512), f32)
    rm[:, 0::64] = 0.0
    m["rmask"] = rm
    ii = np.arange(128)
    m["hmask"] = ((ii[:, None] // 64 == ii[None, :] // 64) & (ii[:, None] <= ii[None, :])).astype(f32)
    mp = np.full((128, 2, 128), NEG, f32)
    mp[:, 0][ii[:, None] >= ii[None, :]] = 0.0
    mp[:, 1][ii[:, None] <= ii[None, :]] = 0.0
    m["maskP"] = mp
    m["maskH"] = mp[:, 0] if half == 1 else np.full((128, 128), NEG, f32)
    zm = np.full((128, 248), NEG, f32)
    for t in range(8):
        zm[t:, 120 + t] = 0.0
    m["zmask"] = zm
    sq, tq = ii // 8, ii % 8
    m["newmask"] = np.where((sq[:, None] == sq[None, :]) & (tq[:, None] <= tq[None, :]), 0.0, NEG).astype(f32)
    m["hmaskS"] = ((sq[:, None] == sq[None, :]) & (tq[:, None] <= tq[None, :])).astype(f32)
    rms_ = np.ones((128, 128), f32)
    rms_[:, 0::8] = 0.0
    m["rmaskS"] = rms_
    m["seqsel"] = (sq[:, None] == np.arange(16)[None, :]).astype(f32)
    m["state_s_in"] = inp["state_hgrn"][0, 16 * c:16 * c + 16]
    m["sinks"] = np.broadcast_to(inp["attn_sinks"][0][None, :], (128, 8))
    m["gattn"] = np.broadcast_to(inp["attn_out_norm"][0][None, :], (128, 512))
    m["ghg4"] = np.broadcast_to(np.tile(inp["hg_out_norm"][0], 4)[None, :], (128, 512))
    bk = bi[512:640].reshape(2, 64)
    m["bk2"] = np.concatenate([bk, bk], axis=1).T
    cp = np.zeros((128, 2, 128), f32)
    cp[:, 0, 0:64] = 1.0
    cp[:, 1, 64:128] = 1.0
    m["cpar"] = cp
    m["xprev"] = xp[b, 0:2048] if half == 1 else np.zeros((2048, D), f32)
    m["pflag"] = np.full((128, 1), float(half), f32)
    selv = np.zeros((128, 8), f32)
    if half == 1:
        selv[:, c - 1] = 1.0
    m["sel"] = selv
    m["bkv"] = np.broadcast_to(inp["b_in"][0][None, 512:768], (128, 256))
    m["cache_k"] = inp["cache_k_win"][0, 16 * c:16 * c + 16].reshape(16, 128, 128)
    m["cache_v"] = inp["cache_v_win"][0, 16 * c:16 * c + 16].reshape(16, 128, 128)
    return {k: np.ascontiguousarray(v, dtype=f32) for k, v in m.items()}


_NC_CACHE = {}


def _run(inputs, stage=99):
    inp = {k: np.asarray(v) for k, v in inputs.items()}
    if stage not in _NC_CACHE:
        _NC_CACHE[stage] = build_program(stage)
    nc = _NC_CACHE[stage]
    in_maps = [_core_inputs(inp, c) for c in range(8)]
    res = run_bass_kernel_spmd(nc, in_maps, core_ids=list(range(8)))
    return res.results


def kernel(**inputs):
    r = _run(inputs, int(os.environ.get("KSTAGE", "99")))
    f32 = np.float32
    yp = np.zeros((4, 4096, D), f32)
    ys = np.zeros((128, 8, D), f32)
    for c in range(8):
        b, half = c // 2, c % 2
        yc = r[c]["y"]
        ys[16 * c:16 * c + 16] = yc[0:128].reshape(16, 8, D)
        yp[b, half * 2048:(half + 1) * 2048] = yc[128:]
    kp = np.zeros((1, 4, 128, 2, 64), f32)
    vp = np.zeros((1, 4, 128, 2, 64), f32)
    sp = np.zeros((1, 4, 4, 128, 128), f32)
    kd = np.zeros((1, 128, 128, 2, 64), f32)
    vd = np.zeros((1, 128, 128, 2, 64), f32)
    sd = np.zeros((1, 128, 4, 128, 128), f32)
    for c in range(8):
        if c % 2 == 1:
            sp[0, c // 2] = r[c]["state_p"].transpose(1, 0, 2)
            kp[0, c // 2] = r[c]["kwin_p"].reshape(128, 2, 64)
            vp[0, c // 2] = r[c]["vwin_p"].reshape(128, 2, 64)
        sd[0, 16 * c:16 * c + 16] = r[c]["state_s"]
        kd[0, 16 * c:16 * c + 16] = r[c]["kwin_s"].reshape(16, 128, 2, 64)
        vd[0, 16 * c:16 * c + 16] = r[c]["vwin_s"].reshape(16, 128, 2, 64)
    return (yp, ys, kp, vp, sp, kd, vd, sd)
```
